# Optimizing a Trainium2 kernel written in Bass

```python
import math
import jax, jax.numpy as jnp
from jax import lax
import numpy as np

D_MODEL = 1024
BATCH = 4
SEQ = 8192
DEPTH = 2
DEC_BATCH = 32
DEC_SEQ = 32
PAST_LEN = 1024

CHUNK = 64
N_A = DEPTH // 2
N_B = DEPTH - N_A
D_MIX = D_MODEL
CONV_W = 3
N_HEADS = 16
HEAD_DIM = D_MODEL // N_HEADS
D_ATT = N_HEADS * HEAD_DIM
D_FF = 2816
Q_BLOCK = 128
EPS = 1e-6

kernel_name = "yoco_shortconv_fox_convffn_step"


def _rmsnorm(x, g):
    xf = x.astype(jnp.float32)
    y = xf * lax.rsqrt(jnp.mean(xf * xf, axis=-1, keepdims=True) + EPS)
    return (y * g.astype(jnp.float32)).astype(x.dtype)


def _causal_dwconv(x, ctx, w):
    T = x.shape[1]
    xp = jnp.concatenate([ctx.astype(x.dtype), x], axis=1)
    y = xp[:, 0:T] * w[0]
    for i in range(1, CONV_W):
        y = y + xp[:, i:i + T] * w[i]
    return y, xp[:, T:]


def _mixer_a(h, ctx, w_in, conv_w, w_out):
    gb, gc, u = jnp.split(h @ w_in, 3, axis=-1)
    z, new_ctx = _causal_dwconv(gc * u, ctx, conv_w)
    return (gb * z) @ w_out, new_ctx


def _conv_ffn(h, ctx, w_up, conv_w, w_down):
    up = h @ w_up
    upc, new_ctx = _causal_dwconv(up, ctx, conv_w)
    g, v = jnp.split(upc, 2, axis=-1)
    return (jax.nn.silu(g) * v) @ w_down, new_ctx


def _fox_attend(q, k, v, cq, ck, qpos, kpos):
    s = jnp.einsum('bqhd,bkhd->bhqk', q, k).astype(jnp.float32) * (HEAD_DIM ** -0.5)
    s = s + jnp.swapaxes(cq, 1, 2)[..., :, None] - jnp.swapaxes(ck, 1, 2)[..., None, :]
    mask = kpos[None, :] <= qpos[:, None]
    s = jnp.where(mask, s, -jnp.inf)
    p = jax.nn.softmax(s, axis=-1)
    return jnp.einsum('bhqk,bkhd->bqhd', p.astype(v.dtype), v)


def _fox_mix(q, k_all, v_all, logf_all):
    B, T = q.shape[0], q.shape[1]
    Tk = k_all.shape[1]
    c = jnp.cumsum(logf_all.astype(jnp.float32), axis=1)
    cq = c[:, Tk - T:]
    kpos = jnp.arange(Tk)
    qpos = jnp.arange(Tk - T, Tk)
    if T % Q_BLOCK == 0:
        nb = T // Q_BLOCK
        qb = jnp.moveaxis(q.reshape(B, nb, Q_BLOCK, N_HEADS, HEAD_DIM), 1, 0)
        cqb = jnp.moveaxis(cq.reshape(B, nb, Q_BLOCK, N_HEADS), 1, 0)
        pb = qpos.reshape(nb, Q_BLOCK)
        ob = lax.map(lambda a: _fox_attend(a[0], k_all, v_all, a[1], c, a[2], kpos), (qb, cqb, pb))
        return jnp.moveaxis(ob, 0, 1).reshape(q.shape)
    return _fox_attend(q, k_all, v_all, cq, c, qpos, kpos)


def _trunk(x, sa, sf, past_k, past_v, past_logf,
           a_norm, w_a_in, a_conv_w, w_a_out, kv_norm, w_kv, b_f, b_norm, w_q, w_o,
           ffn_norm, w_ffn_up, ffn_conv_w, w_ffn_down, final_norm):
    B, T, _ = x.shape
    new_sa, new_sf = [], []
    k_new = v_new = logf_new = None
    k_all = v_all = logf_all = None
    for l in range(DEPTH):
        if l < N_A:
            y, st = _mixer_a(_rmsnorm(x, a_norm[l]), sa[l], w_a_in[l], a_conv_w[l], w_a_out[l])
            new_sa.append(st)
        else:
            j = l - N_A
            if j == 0:
                kvf = _rmsnorm(x, kv_norm) @ w_kv
                k_new = kvf[..., :D_ATT].reshape(B, T, N_HEADS, HEAD_DIM)
                v_new = kvf[..., D_ATT:2 * D_ATT].reshape(B, T, N_HEADS, HEAD_DIM)
                logf_new = jax.nn.log_sigmoid(kvf[..., 2 * D_ATT:].astype(jnp.float32)
                                              + b_f.astype(jnp.float32))
                if past_k is None:
                    k_all, v_all, logf_all = k_new, v_new, logf_new
                else:
                    k_all = jnp.concatenate([past_k.astype(k_new.dtype), k_new], axis=1)
                    v_all = jnp.concatenate([past_v.astype(v_new.dtype), v_new], axis=1)
                    logf_all = jnp.concatenate([past_logf.astype(jnp.float32), logf_new], axis=1)
            q = (_rmsnorm(x, b_norm[j]) @ w_q[j]).reshape(B, T, N_HEADS, HEAD_DIM)
            o = _fox_mix(q, k_all, v_all, logf_all)
            y = o.reshape(B, T, D_ATT) @ w_o[j]
        x = x + y
        f, st = _conv_ffn(_rmsnorm(x, ffn_norm[l]), sf[l], w_ffn_up[l], ffn_conv_w[l], w_ffn_down[l])
        new_sf.append(st)
        x = x + f
    return (_rmsnorm(x, final_norm), jnp.stack(new_sa), jnp.stack(new_sf),
            k_new, v_new, logf_new.astype(x.dtype))


def setup_inputs(seed: int = 0) -> dict:
    key = jax.random.key(seed)
    ks = jax.random.split(key, 24)
    f32 = jnp.float32
    nrm = lambda k, shape, s: jax.random.normal(k, shape, f32) * s
    gain = lambda k, shape: 1.0 + 0.01 * jax.random.normal(k, shape, f32)
    w_kv = nrm(ks[9], (D_MODEL, 2 * D_ATT + N_HEADS), D_MODEL ** -0.5)
    w_kv = w_kv.at[:, 2 * D_ATT:].multiply(0.1)
    return {
        "x_prompt": nrm(ks[0], (BATCH, SEQ, D_MODEL), 1.0),
        "x_sample": nrm(ks[1], (DEC_BATCH, DEC_SEQ, D_MODEL), 1.0),
        "state_conv_a": nrm(ks[2], (N_A, DEC_BATCH, CONV_W - 1, D_MIX), 1.0),
        "state_ffn_conv": nrm(ks[3], (DEPTH, DEC_BATCH, CONV_W - 1, 2 * D_FF), 1.0),
        "cache_k": nrm(ks[4], (DEC_BATCH, PAST_LEN, N_HEADS, HEAD_DIM), 1.0),
        "cache_v": nrm(ks[5], (DEC_BATCH, PAST_LEN, N_HEADS, HEAD_DIM), 1.0),
        "cache_logf": jax.nn.log_sigmoid(4.0 + 0.1 * jax.random.normal(ks[6], (DEC_BATCH, PAST_LEN, N_HEADS), f32)),
        "a_norm": gain(ks[7], (N_A, D_MODEL)),
        "w_a_in": nrm(ks[8], (N_A, D_MODEL, 3 * D_MIX), D_MODEL ** -0.5),
        "a_conv_w": nrm(ks[10], (N_A, CONV_W, D_MIX), CONV_W ** -0.5),
        "w_a_out": nrm(ks[11], (N_A, D_MIX, D_MODEL), D_MIX ** -0.5),
        "kv_norm": gain(ks[12], (D_MODEL,)),
        "w_kv": w_kv,
        "b_f": 4.0 + 0.1 * jax.random.normal(ks[13], (N_HEADS,), f32),
        "b_norm": gain(ks[14], (N_B, D_MODEL)),
        "w_q": nrm(ks[15], (N_B, D_MODEL, D_ATT), D_MODEL ** -0.5),
        "w_o": nrm(ks[16], (N_B, D_ATT, D_MODEL), D_ATT ** -0.5),
        "ffn_norm": gain(ks[17], (DEPTH, D_MODEL)),
        "w_ffn_up": nrm(ks[18], (DEPTH, D_MODEL, 2 * D_FF), D_MODEL ** -0.5),
        "ffn_conv_w": nrm(ks[19], (DEPTH, CONV_W, 2 * D_FF), CONV_W ** -0.5),
        "w_ffn_down": nrm(ks[20], (DEPTH, D_FF, D_MODEL), D_FF ** -0.5),
        "final_norm": gain(ks[21], (D_MODEL,)),
    }


def reference(x_prompt, x_sample, state_conv_a, state_ffn_conv, cache_k, cache_v, cache_logf,
              a_norm, w_a_in, a_conv_w, w_a_out, kv_norm, w_kv, b_f, b_norm, w_q, w_o,
              ffn_norm, w_ffn_up, ffn_conv_w, w_ffn_down, final_norm):
    weights = (a_norm, w_a_in, a_conv_w, w_a_out, kv_norm, w_kv, b_f, b_norm, w_q, w_o,
               ffn_norm, w_ffn_up, ffn_conv_w, w_ffn_down, final_norm)
    Bp = x_prompt.shape[0]
    sa0 = jnp.zeros((N_A, Bp, CONV_W - 1, D_MIX), x_prompt.dtype)
    sf0 = jnp.zeros((DEPTH, Bp, CONV_W - 1, 2 * D_FF), x_prompt.dtype)
    y_prompt, p_conv_a, p_ffn_conv, p_k, p_v, p_logf = _trunk(
        x_prompt, sa0, sf0, None, None, None, *weights)
    y_sample, s_conv_a, s_ffn_conv, s_k, s_v, s_logf = _trunk(
        x_sample, state_conv_a, state_ffn_conv, cache_k, cache_v, cache_logf, *weights)
    return (y_prompt, y_sample, p_conv_a, p_ffn_conv, p_k, p_v, p_logf,
            s_conv_a, s_ffn_conv, s_k, s_v, s_logf)
```

```python
import os, sys, contextlib
import numpy as np
import concourse.bass as bass
import concourse.mybir as mybir
from concourse.bass_utils import run_bass_kernel_spmd

F32 = mybir.dt.float32
BF16 = mybir.dt.bfloat16
ALU = mybir.AluOpType
AF = mybir.ActivationFunctionType

D = 1024
NUP = 5632
DFF = 2816
H = 16
KC = 8
NF = 44
NPAIR = 22
PAST = 1024
SB_ = 4
ST_ = 32
NS = SB_ * ST_
EPS = 1e-6
SEQ = int(os.environ.get("YK_SEQ", "8192"))
CH = 512
STRICT = bool(int(os.environ.get("YK_STRICT", "0")))


class Prog:
    COMPUTE = ("pe", "act", "dve", "pool")

    def __init__(self, nc, es, n_dma_sems=14):
        self.nc = nc
        self.ops = []
        self.res = {}
        self.eng = {"pe": nc.tensor, "act": nc.scalar, "dve": nc.vector, "pool": nc.gpsimd, "sp": nc.sync}
        self.sem = {e: es.enter_context(nc.semaphore("s_" + e)) for e in self.COMPUTE}
        self.dma_sems = {q: [es.enter_context(nc.semaphore(f"d_{q}{i}")) for i in range(n_dma_sems)]
                         for q in ("sp", "pool")}
        self.eng_idx = {e: 0 for e in self.eng}

    def op(self, eng, fn, reads=(), writes=(), dma=False):
        o = dict(eng=eng, fn=fn, dma=dma, deps=[], milestone=False, idx=self.eng_idx[eng], dbg=(list(reads), list(writes)))
        self.eng_idx[eng] += 1
        deps = {}
        for r in reads:
            st = self.res.setdefault(r, dict(w=None, rs=[]))
            if st["w"] is not None:
                deps[id(st["w"])] = (st["w"], "raw")
        for w in writes:
            st = self.res.setdefault(w, dict(w=None, rs=[]))
            if st["w"] is not None and id(st["w"]) not in deps:
                deps[id(st["w"])] = (st["w"], "waw")
            for r in st["rs"]:
                if id(r) not in deps:
                    deps[id(r)] = (r, "war")
        for p, kind in deps.values():
            if p is o:
                continue
            if (not p["dma"]) and p["eng"] == eng and not dma and not STRICT:
                if eng == "pe":
                    continue
                if kind != "raw" or (o["idx"] - p["idx"]) > 3:
                    continue
            o["deps"].append(p)
            p["milestone"] = True
        for r in reads:
            rs = self.res[r]["rs"]
            if not dma and not STRICT:
                rs[:] = [x for x in rs if x["dma"] or x["eng"] != eng]
            rs.append(o)
        for w in writes:
            st = self.res[w]
            st["w"] = o
            st["rs"] = []
        self.ops.append(o)
        return o

    def emit(self, final_wait_eng="sp"):
        cnt = {e: 0 for e in self.COMPUTE}
        dcnt = {q: 0 for q in self.dma_sems}
        semuse = {}
        for o in self.ops:
            if o["dma"]:
                q = o["eng"]
                pool = self.dma_sems[q]
                s = pool[dcnt[q] % len(pool)]
                dcnt[q] += 1
                semuse[id(s)] = semuse.get(id(s), 0) + 16
                o["sig"] = (s, semuse[id(s)])
            elif o["milestone"]:
                cnt[o["eng"]] += 1
                o["sig"] = (self.sem[o["eng"]], cnt[o["eng"]])
        waited = {e: {} for e in self.eng}
        nwaits = 0
        for o in self.ops:
            e = o["eng"]
            h = self.eng[e]
            need = {}
            if o["dma"]:
                s, v = o["sig"]
                if v > 16:
                    need[id(s)] = (s, v - 16)
            for p in o["deps"]:
                s, v = p["sig"]
                if id(s) not in need or need[id(s)][1] < v:
                    need[id(s)] = (s, v)
            for k, (s, v) in need.items():
                if waited[e].get(k, 0) >= v:
                    continue
                h.wait_ge(s, v)
                nwaits += 1
                waited[e][k] = v
            try:
                ins = o["fn"]()
            except Exception:
                print("[prog] failing op:", o["eng"], o.get("dbg"), file=sys.stderr)
                raise
            if o["dma"]:
                ins.then_inc(o["sig"][0], 16)
            elif o["milestone"]:
                ins.then_inc(o["sig"][0], 1)
        h = self.eng[final_wait_eng]
        for q, pool in self.dma_sems.items():
            for s in pool:
                v = semuse.get(id(s), 0)
                if v > 0 and waited[final_wait_eng].get(id(s), 0) < v:
                    h.wait_ge(s, v)
        print(f"[prog] ops={len(self.ops)} waits={nwaits} milestones={cnt} dmas={dcnt}", file=sys.stderr)


def build_program():
    nc = bass.Bass("TRN2", target_bir_lowering=False)
    NCH = SEQ // CH
    NSLOT = NCH // 2
    NKT = SEQ // 128
    din = lambda n, s: nc.dram_tensor(n, list(s), F32, kind="ExternalInput").ap()
    dout = lambda n, s: nc.dram_tensor(n, list(s), F32, kind="ExternalOutput").ap()
    dscr = lambda n, s, dt: nc.dram_tensor(n, list(s), dt).ap()
    xp = din("xp", [SEQ, D]); xs = din("xs", [NS, D])
    sta = din("sta", [SB_, 2, D]); stf = din("stf", [2, SB_, 2, NUP])
    ck = din("ck", [SB_, PAST, D]); cv = din("cv", [SB_, PAST, D]); clf = din("clf", [SB_, PAST, H])
    a_norm = din("a_norm", [D]); w_a_in = din("w_a_in", [D, 3 * D]); a_conv_w = din("a_conv_w", [3, D])
    w_a_out = din("w_a_out", [D, D]); kv_norm = din("kv_norm", [D]); w_kv = din("w_kv", [D, 2 * D + H])
    b_f = din("b_f", [H]); b_norm = din("b_norm", [D]); w_q = din("w_q", [D, D]); w_o = din("w_o", [D, D])
    ffn_norm = din("ffn_norm", [2, D]); w_up = din("w_up", [2, D, NUP]); ffn_conv_w = din("ffn_conv_w", [2, 3, NUP])
    w_dn = din("w_dn", [2, DFF, D]); final_norm = din("final_norm", [D])
    gain_src = [a_norm, ffn_norm[0], kv_norm, b_norm, ffn_norm[1], final_norm]
    yp = dout("yp", [NSLOT, CH, D]); ys = dout("ys", [NS, D])
    pca = dout("pca", [2, D]); pfc0 = dout("pfc0", [2, NUP]); pfc1 = dout("pfc1", [2, NUP])
    pk = dout("pk", [SEQ, D]); pv = dout("pv", [SEQ, D]); plf = dout("plf", [SEQ, H])
    sca = dout("sca", [SB_, 2, D]); sfc = dout("sfc", [2, SB_, 2, NUP])
    sk = dout("sk", [NS, D]); sv = dout("sv", [NS, D]); slf = dout("slf", [NS, H])
    wb_ain = dscr("wb_ain", [KC, 128, KC, 3, 128], BF16)
    wb_up = dscr("wb_up", [2, NPAIR, 128, KC, 2, 128], BF16)
    wb_aout = dscr("wb_aout", [D, D], BF16)
    wb_kv = dscr("wb_kv", [D, 2 * D + H], BF16)
    wb_q = dscr("wb_q", [D, D], BF16)
    wb_o = dscr("wb_o", [D, D], BF16)
    wb_dn = dscr("wb_dn", [2, DFF, D], BF16)
    x1s = dscr("x1s", [SEQ + 2, D], F32)
    qts = dscr("qts", [D, SEQ + 2], BF16)
    kts = dscr("kts", [D, SEQ], BF16)
    vsc = dscr("vsc", [8, 128, NKT, 130], BF16)
    kts_s = dscr("kts_s", [D, SB_, PAST + ST_], BF16)
    halfc = dscr("halfc", [2, 128], F32)
    x1own = dscr("x1own", [NSLOT * CH, D], F32)
    x1hal = dscr("x1hal", [NSLOT, 2, D], F32)
    qown = dscr("qown", [D, NSLOT, CH + 2], BF16)

    with contextlib.ExitStack() as es:
        P = Prog(nc, es)
        sb_bytes = [0]

        def sbt(n, s, dt=F32):
            sb_bytes[0] += int(np.prod(s[1:])) * (2 if dt == BF16 else 4)
            return nc.alloc_sbuf_tensor(n, list(s), dt)
        ident = sbt("ident", [128, 128], BF16)
        identf = sbt("identf", [128, 128])
        tri = sbt("tri", [128, 128])
        sel127 = sbt("sel127", [128, 128])
        sel31 = sbt("sel31", [128, 128])
        Dm = sbt("Dm", [128, 512])
        ones_f = sbt("ones_f", [128, 64])
        HALF = sbt("HALF", [128, 1])
        DELTA = sbt("DELTA", [128, 8])
        DELTAH = sbt("DELTAH", [128, 5])
        JT = sbt("JT", [128, 8])
        epsT = sbt("epsT", [128, 1])
        MK = sbt("MK", [128, 8, 512], BF16)
        MKH = sbt("MKH", [128, 5, 2], BF16)
        MKS = sbt("MKS", [128, ST_], BF16)
        gainb = [sbt(f"gain{i}", [128, D]) for i in range(2)]
        bf_bc = sbt("bf_bc", [128, H])
        cwa = sbt("cwa", [128, KC, 3])
        cwf = sbt("cwf", [128, 2, NF, 3])
        X = sbt("X", [128, 4, D])
        XH = sbt("XH", [128, D])
        XN = sbt("XN", [128, 4, D], BF16)
        XNH = sbt("XNH", [128, D], BF16)
        XT = sbt("XT", [128, KC * CH], BF16)
        XTH = sbt("XTH", [128, KC, 2], BF16)
        HA = sbt("HA", [128, KC, CH], BF16)
        HAH = sbt("HAH", [128, KC, 2], BF16)
        HF = sbt("HF", [128, 11, CH], BF16)
        Ub = [sbt(f"U{i}", [128, CH + 2 * SB_]) for i in range(2)]
        Ab = [sbt(f"A{i}", [128, CH]) for i in range(3)]
        GCb = [sbt(f"GC{i}", [128, CH]) for i in range(2)]
        Gb = [sbt(f"G{i}", [128, CH]) for i in range(2)]
        CTA = sbt("CTA", [128, KC, SB_, 2])
        CTF = sbt("CTF", [128, 2, NF, SB_, 2])
        CTAs = sbt("CTAs", [128, KC, SB_, 2])
        CTFs = sbt("CTFs", [128, 2, NF, SB_, 2])
        ss = sbt("ss", [128, 8])
        rstd = sbt("rstd", [128, 8])
        junk = sbt("junk", [128, D])
        CALL = sbt("CALL", [128, NKT + 1, H])
        CARRY = sbt("CARRY", [128, NKT + 1, H])
        CALLs = sbt("CALLs", [128, SB_, 9, H])
        CARRYs = sbt("CARRYs", [128, SB_, 10, H])
        LG = [sbt(f"LG{i}", [128, H]) for i in range(2)]
        LGt = [sbt(f"LGt{i}", [128, H]) for i in range(2)]
        NWR = 3
        WR = sbt("WR", [128, NWR, 6144], BF16)
        fence_t = sbt("fence_t", [128, 2])
        QS = sbt("QS", [128, KC * NS], BF16)
        VBS = sbt("VBS", [128, 1040], BF16)
        SCR_BYTES = 46080
        SCR = sbt("SCR", [128, SCR_BYTES // 4])
        scr_b = SCR[:].bitcast(BF16)

        def scr_alloc(cur, nelem, dt):
            sz = 2 if dt == BF16 else 4
            off = (cur[0] + 63) // 64 * 64
            cur[0] = off + nelem * sz
            assert cur[0] <= SCR_BYTES, (cur[0], SCR_BYTES)
            return scr_b[:, off // 2: off // 2 + nelem] if dt == BF16 else SCR[:, off // 4: off // 4 + nelem]
        ca = [0]
        KVST = scr_alloc(ca, 2 * 2048, F32)
        VB = scr_alloc(ca, 4 * 1040, BF16)
        KTST = scr_alloc(ca, KC * CH, BF16)
        QTST = scr_alloc(ca, KC * CH, BF16)
        CKB = scr_alloc(ca, 1024, BF16)
        KEYS_A = ["VB", "KTST", "QTST", "CKB"] + [("KVST", p_, c_) for p_ in range(2) for c_ in range(4)]
        cb_ = [0]
        QW = scr_alloc(cb_, KC * 514, BF16)
        KR = [scr_alloc(cb_, 2048, BF16) for _ in range(2)]
        VR = [scr_alloc(cb_, 16 * 130, BF16) for _ in range(2)]
        NPT = 4
        PT = [scr_alloc(cb_, 512, BF16) for _ in range(NPT)]
        OS = scr_alloc(cb_, 512, F32)
        RB = scr_alloc(cb_, 512, F32)
        GS = scr_alloc(cb_, 512, F32)
        BIAS = scr_alloc(cb_, H * 64, F32)
        KEYS_B = ["QW", "KR0", "KR1", "VR0", "VR1"] + [f"PT{i}" for i in range(NPT)] + ["OS", "RBrow", "GS", "BIAS"]
        print(f"[sbuf] {sb_bytes[0] / 1024:.1f} KiB/partition, scrA={ca[0]} scrB={cb_[0]}", file=sys.stderr)
        psf = es.enter_context(nc.psum_tensor("psf", [128, 6, 512], F32))
        psb = es.enter_context(nc.psum_tensor("psb", [128, 2, 1024], BF16))
        free_banks = list(range(6))
        rr = {}

        def bank_get():
            return free_banks.pop(0)

        def bank_put(b):
            free_banks.append(b)

        def nxt(k, n):
            v = rr.get(k, 0)
            rr[k] = (v + 1) % n
            return v

        def E(eng):
            return P.eng[eng]

        def cp(eng, out, in_, reads, writes):
            if eng == "act":
                P.op("act", lambda: nc.scalar.copy(out=out, in_=in_), reads, writes)
            else:
                P.op(eng, lambda: E(eng).tensor_copy(out=out, in_=in_), reads, writes)

        def act(out, in_, func, reads, writes, bias=None, scale=None, accum=None):
            kw = {}
            if bias is not None:
                kw["bias"] = bias
            if scale is not None:
                kw["scale"] = scale
            if accum is not None:
                kw["accum_out"] = accum
            P.op("act", lambda: nc.scalar.activation(out=out, in_=in_, func=func, **kw), reads, writes)

        def tt(eng, out, in0, in1, op, reads, writes):
            P.op(eng, lambda: E(eng).tensor_tensor(out=out, in0=in0, in1=in1, op=op), reads, writes)

        def ts(eng, out, in0, s1, s2, op0, op1, reads, writes):
            if op1 is None:
                P.op(eng, lambda: E(eng).tensor_scalar(out=out, in0=in0, scalar1=s1, scalar2=None, op0=op0), reads, writes)
            else:
                P.op(eng, lambda: E(eng).tensor_scalar(out=out, in0=in0, scalar1=s1, scalar2=s2, op0=op0, op1=op1), reads, writes)

        def stt(out, in0, scalar, in1, op0, op1, reads, writes):
            P.op("dve", lambda: nc.vector.scalar_tensor_tensor(out=out, in0=in0, scalar=scalar, in1=in1, op0=op0, op1=op1),
                 reads, writes)

        def mm(out, lhsT, rhs, start, stop, reads, writes, skip=False):
            if skip:
                P.op("pe", lambda: nc.tensor.matmul(out, lhsT=lhsT, rhs=rhs, start=start, stop=stop, skip_group_check=True), reads, writes)
            else:
                P.op("pe", lambda: nc.tensor.matmul(out, lhsT=lhsT, rhs=rhs, start=start, stop=stop), reads, writes)

        def tr(out, in_, idn, reads, writes):
            P.op("pe", lambda: nc.tensor.transpose(out=out, in_=in_, identity=idn), reads, writes)

        slow_flag = [False]

        @contextlib.contextmanager
        def slow_dma(reason=""):
            old = slow_flag[0]
            slow_flag[0] = True
            try:
                yield
            finally:
                slow_flag[0] = old

        def dma(out, in_, reads, writes, q="sp"):
            slow = slow_flag[0]
            if slow:
                P.op(q, lambda: E(q).dma_start(out=out, in_=in_, allow_slow_non_contiguous=True), reads, writes, dma=True)
            else:
                P.op(q, lambda: E(q).dma_start(out=out, in_=in_), reads, writes, dma=True)

        def memset(eng, ap, val, writes):
            P.op(eng, lambda: E(eng).memset(ap, val), [], writes)

        def fence(from_keys, to_keys):
            P.op("pool", lambda: nc.gpsimd.memset(fence_t[:], 0.0), reads=list(from_keys),
                 writes=list(from_keys) + list(to_keys) + ["fence_t"])

        def wload(src_ap, shape, rkey="wcast"):
            s = nxt("wr", NWR)
            n = int(np.prod(shape[1:]))
            dst = WR[:, s, 0:n]
            if len(shape) == 3:
                dstv = dst.rearrange("p (a b) -> p a b", a=shape[1])
            elif len(shape) == 4:
                dstv = dst.rearrange("p (a b c) -> p a b c", a=shape[1], b=shape[2])
            else:
                dstv = dst
            key = ("WR", s)
            dma(dstv, src_ap, reads=(list(rkey) if isinstance(rkey, list) else [rkey]), writes=[key])
            return dstv, key

        def gload(gi):
            s = nxt("gain", 2)
            dma(gainb[s][:], gain_src[gi].partition_broadcast(128), reads=[], writes=[("gain", s)])
            return gainb[s], ("gain", s)

        memset("pool", identf[:], 1.0, ["identf"])
        P.op("pool", lambda: nc.gpsimd.affine_select(out=identf[:], in_=identf[:], pattern=[[-1, 128]],
                                                     compare_op=ALU.is_equal, fill=0.0, base=0, channel_multiplier=1),
             reads=["identf"], writes=["identf"])
        cp("dve", ident[:], identf[:], ["identf"], ["ident"])
        memset("pool", tri[:], 1.0, ["tri"])
        P.op("pool", lambda: nc.gpsimd.affine_select(out=tri[:], in_=tri[:], pattern=[[1, 128]],
                                                     compare_op=ALU.is_ge, fill=0.0, base=0, channel_multiplier=-1),
             reads=["tri"], writes=["tri"])
        memset("pool", sel127[:], 1.0, ["sel127"])
        P.op("pool", lambda: nc.gpsimd.affine_select(out=sel127[:], in_=sel127[:], pattern=[[0, 128]], compare_op=ALU.is_equal,
                                                     fill=0.0, base=-127, channel_multiplier=1), reads=["sel127"], writes=["sel127"])
        memset("pool", sel31[:], 1.0, ["sel31"])
        P.op("pool", lambda: nc.gpsimd.affine_select(out=sel31[:], in_=sel31[:], pattern=[[0, 128]], compare_op=ALU.is_equal,
                                                     fill=0.0, base=-31, channel_multiplier=1), reads=["sel31"], writes=["sel31"])
        P.op("pool", lambda: nc.gpsimd.iota(Dm[:], pattern=[[-1, 512]], base=0, channel_multiplier=1,
                                            allow_small_or_imprecise_dtypes=True), writes=["Dm"])
        P.op("pool", lambda: nc.gpsimd.iota(JT[:], pattern=[[-128, 8]], base=0, channel_multiplier=0,
                                            allow_small_or_imprecise_dtypes=True), writes=["JT"])
        memset("dve", ones_f[:], 1.0, ["ones_f"])
        memset("dve", junk[:], 0.0, ["junk"])
        memset("dve", epsT[:], EPS, ["epsT"])
        memset("dve", CTA[:], 0.0, ["CTA"])
        memset("pool", CTF[:], 0.0, ["CTF"])
        memset("dve", CARRY[:, 0, :], 0.0, ["CARRY"])
        memset("dve", CARRYs[:], 0.0, ["CARRYs"])
        memset("dve", CALLs[:], 0.0, ["CALLs"])
        memset("dve", XH[:], 0.0, ["XH"])
        memset("dve", XNH[:], 0.0, ["XNH"])
        dma(halfc[0:1, :], junk[0:1, 0:128], reads=["junk"], writes=["halfc0"])
        dma(halfc[1:2, 0:64], ones_f[0:1, 0:64], reads=["ones_f"], writes=["halfc1"])
        dma(halfc[1:2, 64:128], ones_f[0:1, 0:64], reads=["ones_f"], writes=["halfc2"])
        pid = nc.sync.partition_id()
        par = pid % 2
        with slow_dma("tiny"):
            dma(HALF[:], halfc[bass.ds(par, 1), :].rearrange("a p -> p a"), reads=["halfc0", "halfc1", "halfc2"], writes=["HALF"])
        ts("dve", HALF[:], HALF[:], 512.0, None, ALU.mult, None, ["HALF"], ["H512"])
        ts("dve", DELTA[:], JT[:], HALF[:, 0:1], None, ALU.add, None, ["H512", "JT"], ["DELTA"])
        ts("dve", DELTAH[:], JT[:, 0:5], HALF[:, 0:1], 126.0, ALU.add, ALU.add, ["H512", "JT"], ["DELTAH"])
        for j in range(8):
            ts("pool" if j % 2 else "dve", MK[:, j, :], Dm[:], DELTA[:, j:j + 1], None, ALU.is_le, None, ["Dm", "DELTA"], ["MK"])
        for i in range(5):
            ts("dve", MKH[:, i, :], Dm[:, 0:2], DELTAH[:, i:i + 1], None, ALU.is_le, None, ["Dm", "DELTAH"], ["MKH"])
        ts("dve", MKS[:], Dm[:, 0:ST_], 0.0, None, ALU.is_le, None, ["Dm"], ["MKS"])
        dma(x1s[0:2, :], junk[0:2, :], reads=["junk"], writes=["x1pad"])
        zb = junk[:].bitcast(BF16)
        with slow_dma("tiny pad"):
            dma(qts[:, 0:2].rearrange("(f p) t -> p f t", p=128), zb[:, 0:16].rearrange("p (f t) -> p f t", t=2),
                reads=["junk"], writes=["qtpad"])
        dma(bf_bc[:], b_f.partition_broadcast(128), reads=[], writes=["bf_bc"])
        with slow_dma("tiny conv weight layout"):
            for w_ in range(3):
                dma(cwa[:, :, w_], a_conv_w[w_].rearrange("(j p) -> p j", p=128), reads=[], writes=["cw"])
                for l in range(2):
                    dma(cwf[:, l, :, w_], ffn_conv_w[l, w_].rearrange("(j p) -> p j", p=128), reads=[], writes=["cw"])
        for j in range(KC):
            for g in range(3):
                dma(wb_ain[j][:, :, g, :], w_a_in.rearrange("(kc p) (g f) -> p kc g f", p=128, g=3)[:, :, g, j * 128:(j + 1) * 128],
                    reads=[], writes=[("wc", "ain", j, g)], q="pool")
        dma(wb_aout, w_a_out, reads=[], writes=[("wc", "aout")], q="pool")
        for pr in range(NPAIR):
            for w_ in range(2):
                dma(wb_up[0, pr][:, :, w_, :], w_up[0].rearrange("(kc p) (w f) -> p kc w f", p=128, w=2)[:, :, w_, pr * 128:(pr + 1) * 128],
                    reads=[], writes=[("wc", "up", 0, pr, w_)], q="pool")
        dma(wb_dn[0], w_dn[0], reads=[], writes=[("wc", "dn", 0)], q="pool")
        dma(wb_kv, w_kv, reads=[], writes=[("wc", "kv")], q="pool")
        dma(wb_q, w_q, reads=[], writes=[("wc", "q")], q="pool")
        dma(wb_o, w_o, reads=[], writes=[("wc", "o")], q="pool")
        for pr in range(NPAIR):
            for w_ in range(2):
                dma(wb_up[1, pr][:, :, w_, :], w_up[1].rearrange("(kc p) (w f) -> p kc w f", p=128, w=2)[:, :, w_, pr * 128:(pr + 1) * 128],
                    reads=[], writes=[("wc", "up", 1, pr, w_)], q="pool")
        dma(wb_dn[1], w_dn[1], reads=[], writes=[("wc", "dn", 1)], q="pool")

        def mm_group(out_ap, pairs, bank_key):
            n = len(pairs)
            for i, (l_ap, r_ap, rds) in enumerate(pairs):
                mm(out_ap, l_ap, r_ap, i == 0, i == n - 1, rds, [bank_key])

        def rms_stats(Xt, xkey, nt, npart):
            for t_ in range(nt):
                act(junk[0:npart, :], Xt(t_), AF.Square, [xkey], ["junk", ("ss", t_)], accum=ss[0:npart, t_:t_ + 1])
                act(rstd[0:npart, t_:t_ + 1], ss[0:npart, t_:t_ + 1], AF.Sqrt, [("ss", t_), "epsT"], [("rstd", t_)],
                    bias=epsT[0:npart, 0:1], scale=1.0 / D)
                P.op("dve", lambda t_=t_: nc.vector.reciprocal(out=rstd[0:npart, t_:t_ + 1], in_=rstd[0:npart, t_:t_ + 1]),
                     reads=[("rstd", t_)], writes=[("rstd", t_)])

        def rms_apply(Xt, xkey, nt, npart, gi, ofn, okey, extra_reads=()):
            g, gkey = gload(gi)
            for t_ in range(nt):
                stt(ofn(t_), Xt(t_), rstd[0:npart, t_:t_ + 1], g[0:npart, :], ALU.mult, ALU.mult,
                    [xkey, ("rstd", t_), gkey] + list(extra_reads), [okey])

        def transpose_fm(src_fn, skey, nt, npart, dst_flat, dkey):
            blocks = [(kc, t_) for kc in range(KC) for t_ in range(nt)]
            for g0 in range(0, len(blocks), 4):
                grp = blocks[g0:g0 + 4]
                tb = nxt("tb", 2)
                for i, (kc, t_) in enumerate(grp):
                    tr(psb[:, tb, i * npart:(i + 1) * npart], src_fn(t_)[:, kc * 128:(kc + 1) * 128], ident[0:npart, 0:npart],
                       [skey, "ident"], [("psb", tb)])
                w = len(grp) * npart
                cp(("act", "dve")[nxt("ev", 2)], dst_flat[:, g0 * npart:g0 * npart + w], psb[:, tb, 0:w], [("psb", tb)], [dkey])

        def conv3(ps_ap3, pskey, pre_fn, cw_ap, ct_ap, ctkey, nb, T):
            ui = nxt("u", 2)
            U = Ub[ui][:, 0:nb * (T + 2)].rearrange("p (b t) -> p b t", b=nb)
            ukey = ("U", ui)
            ai = nxt("a", 3)
            A = Ab[ai][:, 0:nb * T].rearrange("p (b t) -> p b t", b=nb)
            akey = ("A", ai)
            if pre_fn is None:
                cp("act", U[:, :, 2:2 + T], ps_ap3, [pskey], [ukey])
                act(A, ps_ap3, AF.Copy, [pskey, "cw"], [akey], scale=cw_ap[:, 2:3])
            else:
                pre_fn(U[:, :, 2:2 + T], ukey)
                act(A, U[:, :, 2:2 + T], AF.Copy, [ukey, "cw"], [akey], scale=cw_ap[:, 2:3])
            cp("pool", U[:, :, 0:2], ct_ap, [ctkey], [ukey])
            cp("pool", ct_ap, U[:, :, T:T + 2], [ukey], [ctkey])
            for k in (1, 0):
                stt(A, U[:, :, k:k + T], cw_ap[:, k:k + 1], A, ALU.mult, ALU.add, [ukey, akey, "cw"], [akey])
            return A, akey

        def ffn_halo(l, CT, ctkey):
            hb = bank_get()
            hkey = ("ps", hb)
            for pr in range(NPAIR):
                wt, wkey = wload(wb_up[l, pr], [128, KC, 2, 128], [("wc", "up", l, pr, 0), ("wc", "up", l, pr, 1)])
                for wh in range(2):
                    j = wh * NPAIR + pr
                    mm_group(psf[:, hb, 2 * j:2 * j + 2], [(wt[:, kc, wh, :], XTH[:, kc, :], [wkey, "XTH"]) for kc in range(KC)], hkey)
            cp("act", CT[:, l, :, 0, :], psf[:, hb, 0:2 * NF].rearrange("p (j t) -> p j t", t=2), [hkey], [ctkey])
            bank_put(hb)

        def w_rows(wb, k0, kn, hf):
            return wb.rearrange("(k p) n -> p k n", p=128)[:, k0:k0 + kn, hf * 512:(hf + 1) * 512]

        def down_proj(lhs_fn, lkey, k_base, nk, wb, nt, npart, Xt, xkey, rkey):
            for hf in range(2):
                wt, wkey = wload(w_rows(wb, k_base, nk, hf), [128, nk, 512], rkey)
                for t_ in range(nt):
                    b_ = bank_get()
                    bkey = ("ps", b_)
                    mm_group(psf[0:npart, b_, :], [(lhs_fn(k, t_), wt[:, k, :], [lkey, wkey]) for k in range(nk)], bkey)
                    tt("dve", Xt(t_)[:, hf * 512:(hf + 1) * 512], psf[0:npart, b_, :], Xt(t_)[:, hf * 512:(hf + 1) * 512], ALU.add,
                       [bkey, xkey], [xkey])
                    bank_put(b_)

        def ffn_full(l, xtkey, N, nb, T, CT, ctkey, nt, npart, Xt):
            for grp in range(2):
                for pl in range(11):
                    pr = grp * 11 + pl
                    wt, wkey = wload(wb_up[l, pr], [128, KC, 2, 128], [("wc", "up", l, pr, 0), ("wc", "up", l, pr, 1)])
                    As = []
                    for wh in range(2):
                        j = wh * NPAIR + pr
                        b_ = bank_get()
                        bkey = ("ps", b_)
                        mm_group(psf[:, b_, 0:N], [(wt[:, kc, wh, :], XT[:, kc * N:(kc + 1) * N], [wkey, xtkey]) for kc in range(KC)], bkey)
                        A, akey = conv3(psf[:, b_, 0:N].rearrange("p (b t) -> p b t", b=nb), bkey, None, cwf[:, l, j, :],
                                        CT[:, l, j, 0:nb, :], ctkey, nb, T)
                        bank_put(b_)
                        As.append((A, akey))
                    gi = nxt("g", 2)
                    G = Gb[gi][:, 0:N].rearrange("p (b t) -> p b t", b=nb)
                    act(G, As[0][0], AF.Silu, [As[0][1]], [("G", gi)])
                    tt("pool", HF[:, pl, 0:N].rearrange("p (b t) -> p b t", b=nb), G, As[1][0], ALU.mult, [("G", gi), As[1][1]], ["HF"])
                down_proj(lambda k, t_: HF[:, k, t_ * 128:t_ * 128 + npart], "HF", grp * 11, 11, wb_dn[l], nt, npart, Xt, "X", ("wc", "dn", l))

        def logsig(ps_lf, pkey, npart, li):
            L, Lt = LG[li], LGt[li]
            lk, ltk = ("LG", li), ("LGt", li)
            tt("dve", L[0:npart], ps_lf, bf_bc[0:npart], ALU.add, [pkey, "bf_bc"], [lk])
            ts("dve", Lt[0:npart], L[0:npart], -1.0, None, ALU.mult, None, [lk], [ltk])
            tt("dve", Lt[0:npart], Lt[0:npart], L[0:npart], ALU.min, [lk, ltk], [ltk])
            act(Lt[0:npart], Lt[0:npart], AF.Exp, [ltk], [ltk])
            act(Lt[0:npart], Lt[0:npart], AF.Ln, [ltk], [ltk], bias=1.0, scale=1.0)
            ts("dve", L[0:npart], L[0:npart], 0.0, None, ALU.min, None, [lk], [lk])
            tt("dve", L[0:npart], L[0:npart], Lt[0:npart], ALU.subtract, [lk, ltk], [lk])
            return L, lk

        def cumsum_tile(L_ap, lk, npart, call_ap, ckey, carry_in_ap, carry_out_ap, cakey, sel, selkey):
            b_ = bank_get()
            bkey = ("ps", b_)
            mm(psf[0:npart, b_, 0:H], tri[0:npart, 0:npart], L_ap, True, True, ["tri", lk], [bkey])
            tt("dve", call_ap, psf[0:npart, b_, 0:H], carry_in_ap, ALU.add, [bkey, cakey], [ckey])
            mm(psf[:, b_, 256:256 + H], sel[0:npart, :], call_ap, True, True, [selkey, ckey], [bkey])
            cp("act", carry_out_ap, psf[:, b_, 256:256 + H], [bkey], [cakey])
            bank_put(b_)

        def phase_a(x_src, N, nb, T, cta, ctakey, ctf, ctfkey, sample, chunk_idx):
            nt = max(1, N // 128)
            npart = min(N, 128)
            Xt = lambda t_: X[0:npart, t_, :]
            XNt = lambda t_: XN[0:npart, t_, :]
            dma(X[0:npart, 0:nt, :], x_src, reads=[], writes=["X"])
            rms_stats(Xt, "X", nt, npart)
            rms_apply(Xt, "X", nt, npart, 0, XNt, "XN")
            transpose_fm(XNt, "XN", nt, npart, XT, "XT")
            for j in range(KC):
                wt, wkey = wload(wb_ain[j], [128, KC, 3, 128], [("wc", "ain", j, g_) for g_ in range(3)])
                bs = [bank_get() for _ in range(3)]
                for g in range(3):
                    mm_group(psf[:, bs[g], 0:N], [(wt[:, kc, g, :], XT[:, kc * N:(kc + 1) * N], [wkey, "XT"]) for kc in range(KC)],
                             ("ps", bs[g]))
                gci = nxt("gc", 2)
                GC = GCb[gci][:, 0:N].rearrange("p (b t) -> p b t", b=nb)
                cp("act", GC, psf[:, bs[1], 0:N].rearrange("p (b t) -> p b t", b=nb), [("ps", bs[1])], [("GC", gci)])

                def pre(Uv, ukey, b2=bs[2], GC=GC, gci=gci):
                    tt("dve", Uv, psf[:, b2, 0:N].rearrange("p (b t) -> p b t", b=nb), GC, ALU.mult, [("ps", b2), ("GC", gci)], [ukey])
                A, akey = conv3(None, None, pre, cwa[:, j, :], cta[:, j, 0:nb, :], ctakey, nb, T)
                tt("dve", HA[:, j, 0:N].rearrange("p (b t) -> p b t", b=nb), psf[:, bs[0], 0:N].rearrange("p (b t) -> p b t", b=nb), A,
                   ALU.mult, [("ps", bs[0]), akey], ["HA"])
                for b_ in bs:
                    bank_put(b_)
            down_proj(lambda k, t_: HA[:, k, t_ * 128:t_ * 128 + npart], "HA", 0, KC, wb_aout, nt, npart, Xt, "X", ("wc", "aout"))
            rms_stats(Xt, "X", nt, npart)
            rms_apply(Xt, "X", nt, npart, 1, XNt, "XN")
            transpose_fm(XNt, "XN", nt, npart, XT, "XT")
            ffn_full(0, "XT", N, nb, T, ctf, ctfkey, nt, npart, Xt)
            tok0 = chunk_idx * CH
            if not sample:
                for t_ in range(nt):
                    dma(x1s[2 + tok0 + t_ * 128:2 + tok0 + (t_ + 1) * 128, :], X[:, t_, :], reads=["X"], writes=["x1s"])
            rms_stats(Xt, "X", nt, npart)
            rms_apply(Xt, "X", nt, npart, 2, XNt, "XN")
            transpose_fm(XNt, "XN", nt, npart, XT, "XT")
            wkv = wb_kv.rearrange("(k p) n -> p k n", p=128)
            for cb in range(4):
                wt, wkey = wload(wkv[:, :, cb * 512:(cb + 1) * 512], [128, KC, 512], ("wc", "kv"))
                for t_ in range(nt):
                    b_ = bank_get()
                    mm_group(psf[0:npart, b_, :], [(XT[:, kc * N + t_ * 128:kc * N + t_ * 128 + npart], wt[:, kc, :], ["XT", wkey])
                                                   for kc in range(KC)], ("ps", b_))
                    kvo = (t_ % 2) * 2048 + cb * 512
                    kvkey = ("KVST", t_ % 2, cb)
                    cp("act", KVST[0:npart, kvo:kvo + 512], psf[0:npart, b_, :], [("ps", b_)], [kvkey])
                    if cb >= 2:
                        h0 = (cb - 2) * 8
                        vbv = VB[0:npart, t_ * 1040:(t_ + 1) * 1040].rearrange("p (h e) -> p h e", e=65)
                        cp("dve", vbv[:, h0:h0 + 8, 0:64], KVST[0:npart, kvo:kvo + 512].rearrange("p (h e) -> p h e", e=64), [kvkey], ["VB"])
                    bank_put(b_)
                    dst = ((sk, sv) if sample else (pk, pv))[cb // 2]
                    r0 = 0 if sample else tok0 + t_ * 128
                    dma(dst[r0:r0 + npart, (cb % 2) * 512:(cb % 2 + 1) * 512], KVST[0:npart, kvo:kvo + 512],
                        reads=[kvkey], writes=["kvout"])
                if cb < 2:
                    for f4 in range(4):
                        f = cb * 4 + f4
                        b_ = bank_get()
                        mm_group(psf[:, b_, 0:N], [(wt[:, kc, f4 * 128:(f4 + 1) * 128], XT[:, kc * N:(kc + 1) * N], [wkey, "XT"])
                                                   for kc in range(KC)], ("ps", b_))
                        cp(("act", "dve")[nxt("ev", 2)], KTST[:, f * N:(f + 1) * N], psf[:, b_, 0:N], [("ps", b_)], ["KTST"])
                        bank_put(b_)
            wtl, wlkey = wload(wkv[:, :, 2048:2064], [128, KC, H], ("wc", "kv"))
            for t_ in range(nt):
                b_ = bank_get()
                mm_group(psf[0:npart, b_, 0:H], [(XT[:, kc * N + t_ * 128:kc * N + t_ * 128 + npart], wtl[:, kc, :], ["XT", wlkey])
                                                 for kc in range(KC)], ("ps", b_))
                li = nxt("lg", 2)
                L, lk = logsig(psf[0:npart, b_, 0:H], ("ps", b_), npart, li)
                bank_put(b_)
                if not sample:
                    j = chunk_idx * 4 + t_
                    dma(plf[tok0 + t_ * 128:tok0 + (t_ + 1) * 128, :], L[:, :], reads=[lk], writes=["lfout"])
                    cumsum_tile(L[:, :], lk, 128, CALL[:, j, :], "CALL", CARRY[:, j, :], CARRY[:, j + 1, :], "CARRY", sel127, "sel127")
                else:
                    dma(slf[:, :], L[0:npart, :], reads=[lk], writes=["lfout"])
                    sample_state["L"] = (L, lk)
            if not sample:
                with slow_dma("V head split"):
                    for hp in range(8):
                        dma(vsc[hp, :, chunk_idx * 4:chunk_idx * 4 + 4, :],
                            VB[:, 0:4 * 1040].rearrange("p (t x) -> p t x", t=4)[:, :, 2 * hp * 65:2 * hp * 65 + 130],
                            reads=["VB"], writes=["vsc"])
                dma(kts.rearrange("(f p) t -> p f t", p=128)[:, :, tok0:tok0 + CH],
                    KTST[:, 0:KC * N].rearrange("p (f t) -> p f t", f=KC), reads=["KTST"], writes=["kts"])
            rms_apply(Xt, "X", nt, npart, 3, XNt, "XN")
            transpose_fm(XNt, "XN", nt, npart, XT, "XT")
            wqv = wb_q.rearrange("(k p) n -> p k n", p=128)
            for cb in range(2):
                wt, wkey = wload(wqv[:, :, cb * 512:(cb + 1) * 512], [128, KC, 512], ("wc", "q"))
                for f4 in range(4):
                    f = cb * 4 + f4
                    b_ = bank_get()
                    mm_group(psf[:, b_, 0:N], [(wt[:, kc, f4 * 128:(f4 + 1) * 128], XT[:, kc * N:(kc + 1) * N], [wkey, "XT"])
                                               for kc in range(KC)], ("ps", b_))
                    cp(("act", "dve")[nxt("ev", 2)], QTST[:, f * N:(f + 1) * N], psf[:, b_, 0:N], [("ps", b_)], ["QTST"])
                    bank_put(b_)
            if not sample:
                dma(qts.rearrange("(f p) t -> p f t", p=128)[:, :, 2 + tok0:2 + tok0 + CH],
                    QTST[:, 0:KC * N].rearrange("p (f t) -> p f t", f=KC), reads=["QTST"], writes=["qts"])

        sample_state = {}

        def att_open(nq):
            ob = bank_get()
            return dict(ob=ob, okey=("ps", ob), first=True, nq=nq)

        def att_close(stt_, out_ap, okey_out):
            nq, ob, okb = stt_["nq"], stt_["ob"], stt_["okey"]
            ts("dve", RB[64:65, 0:nq], psf[64:65, ob, 0:nq], 1e-30, None, ALU.add, None, [okb], ["RBrow"])
            P.op("dve", lambda: nc.vector.reciprocal(out=RB[64:65, 0:nq], in_=RB[64:65, 0:nq]), reads=["RBrow"], writes=["RBrow"])
            bb = bank_get()
            mm(psf[0:64, bb, 0:nq], ones_f[64:65, 0:64], RB[64:65, 0:nq], True, True, ["RBrow", "ones_f"], [("ps", bb)])
            cp("dve", OS[0:64, 0:nq], psf[0:64, ob, 0:nq], [okb], ["OS"])
            tt("dve", out_ap, OS[0:64, 0:nq], psf[0:64, bb, 0:nq], ALU.mult, ["OS", ("ps", bb)], [okey_out])
            bank_put(ob)
            bank_put(bb)

        def run_items(items, lag=2):
            for i, it in enumerate(items):
                it["qk"]()
                it["sm"]()
                if i >= lag:
                    items[i - lag]["pv"]()
            for it in items[max(0, len(items) - lag):]:
                it["pv"]()

        def main_item(stt_, q_ap, qkey, kt_ap, v_ap, kvkeys, bias_ap, mask_ap, kn=128):
            nq = stt_["nq"]
            d = {}

            def qk():
                sb_ = bank_get()
                d["sb"] = sb_
                mm(psf[0:kn, sb_, 0:nq], kt_ap, q_ap, True, True, kvkeys + [qkey], [("ps", sb_)])

            def sm():
                pi = nxt("pt", NPT)
                d["pt"] = pi
                ptv = PT[pi]
                act(ptv[0:kn, 0:nq], psf[0:kn, d["sb"], 0:nq], AF.Exp, [("ps", d["sb"]), "BIAS"], [f"PT{pi}"], bias=bias_ap, scale=0.125)
                bank_put(d["sb"])
                if mask_ap is not None:
                    tt("pool", ptv[0:kn, 0:nq], ptv[0:kn, 0:nq], mask_ap, ALU.mult, [f"PT{pi}", "MK"], [f"PT{pi}"])

            def pv():
                pi = d["pt"]
                mm(psf[0:65, stt_["ob"], 0:nq], v_ap, PT[pi][0:kn, 0:nq], stt_["first"], False, kvkeys + [f"PT{pi}"], [stt_["okey"]], skip=True)
                stt_["first"] = False
            return dict(qk=qk, sm=sm, pv=pv)

        def group_item(stt_, q_ap, qkey, tiles, kvkeys, bias_grp, bkeys):
            nq = stt_["nq"]
            n = len(tiles)
            w = n * nq
            kn0 = tiles[0][2]
            assert all(t[2] == kn0 for t in tiles)
            d = {}

            def qk():
                sb_ = bank_get()
                d["sb"] = sb_
                for i, (kt_ap, v_ap, kn, mask_ap) in enumerate(tiles):
                    mm(psf[0:kn, sb_, i * nq:(i + 1) * nq], kt_ap, q_ap, True, True, kvkeys + [qkey], [("ps", sb_)])

            def sm():
                pi = nxt("pt", NPT)
                d["pt"] = pi
                ptv = PT[pi]
                stt(GS[0:kn0, 0:w].rearrange("p (g q) -> p g q", q=nq), psf[0:kn0, d["sb"], 0:w].rearrange("p (g q) -> p g q", q=nq), 0.125,
                    bias_grp[0:kn0].unsqueeze(2).to_broadcast([kn0, n, nq]), ALU.mult, ALU.add, [("ps", d["sb"])] + bkeys, ["GS"])
                bank_put(d["sb"])
                act(ptv[0:kn0, 0:w], GS[0:kn0, 0:w], AF.Exp, ["GS"], [f"PT{pi}"])
                for i, (kt_ap, v_ap, kn, mask_ap) in enumerate(tiles):
                    if mask_ap is not None:
                        tt("pool", ptv[0:kn, i * nq:(i + 1) * nq], ptv[0:kn, i * nq:(i + 1) * nq], mask_ap, ALU.mult,
                           [f"PT{pi}", "MK"], [f"PT{pi}"])

            def pv():
                pi = d["pt"]
                for i, (kt_ap, v_ap, kn, mask_ap) in enumerate(tiles):
                    mm(psf[0:65, stt_["ob"], 0:nq], v_ap, PT[pi][0:kn, i * nq:(i + 1) * nq], stt_["first"], False,
                       kvkeys + [f"PT{pi}"], [stt_["okey"]], skip=True)
                    stt_["first"] = False
            return dict(qk=qk, sm=sm, pv=pv)

        def layer1_rest(N, nb, T, ctf, ctfkey, y_dst, halo):
            nt = max(1, N // 128)
            npart = min(N, 128)
            Xt = lambda t_: X[0:npart, t_, :]
            XNt = lambda t_: XN[0:npart, t_, :]
            if halo:
                down_proj(lambda k, t_: HAH[:, k, 0:2], "HAH", 0, KC, wb_o, 1, 2, lambda t_: XH[0:2, :], "XH", ("wc", "o"))
                rms_stats(lambda t_: XH[0:2, :], "XH", 1, 2)
                rms_apply(lambda t_: XH[0:2, :], "XH", 1, 2, 4, lambda t_: XNH[0:2, :], "XNH")
                transpose_fm(lambda t_: XNH[0:2, :], "XNH", 1, 2, XTH[:].rearrange("p k t -> p (k t)"), "XTH")
                ffn_halo(1, ctf, ctfkey)
            down_proj(lambda k, t_: HA[:, k, t_ * 128:t_ * 128 + npart], "HA", 0, KC, wb_o, nt, npart, Xt, "X", ("wc", "o"))
            rms_stats(Xt, "X", nt, npart)
            rms_apply(Xt, "X", nt, npart, 4, XNt, "XN")
            transpose_fm(XNt, "XN", nt, npart, XT, "XT")
            ffn_full(1, "XT", N, nb, T, ctf, ctfkey, nt, npart, Xt)
            rms_stats(Xt, "X", nt, npart)
            rms_apply(Xt, "X", nt, npart, 5, Xt, "X")
            dma(y_dst, X[0:npart, 0:nt, :], reads=["X"], writes=["yout"])

        with slow_dma("conv state layout (tiny)"):
            for b in range(SB_):
                for t2 in range(2):
                    dma(CTAs[:, :, b, t2], sta[b, t2].rearrange("(j p) -> p j", p=128), reads=[], writes=["CTAs"])
                    for l in range(2):
                        dma(CTFs[:, l, :, b, t2], stf[l, b, t2].rearrange("(j p) -> p j", p=128), reads=[], writes=["CTFs"])
        memset("pool", VB[:, 0:4 * 1040], 1.0, ["VB"])
        phase_a(xs.rearrange("(t p) d -> p t d", p=128), NS, SB_, ST_, CTAs, "CTAs", CTFs, "CTFs", True, 0)
        with slow_dma("conv state layout (tiny)"):
            for b in range(SB_):
                for t2 in range(2):
                    dma(sca[b, t2].rearrange("(j p) -> p j", p=128), CTAs[:, :, b, t2], reads=["CTAs"], writes=["sca"])
                    dma(sfc[0, b, t2].rearrange("(j p) -> p j", p=128), CTFs[:, 0, :, b, t2], reads=["CTFs"], writes=["sfc"])
        kts_sv = kts_s.rearrange("(f p) b t -> p f b t", p=128)
        for b in range(SB_):
            dma(kts_sv[:, :, b, PAST:PAST + ST_], KTST[:, 0:KC * NS].rearrange("p (f b t) -> p f b t", f=KC, b=SB_)[:, :, b, :],
                reads=["KTST"], writes=["kts_s"])
        for b in range(SB_):
            for t8 in range(8):
                dma(CKB[:, :], ck[b, t8 * 128:(t8 + 1) * 128, :], reads=[], writes=["CKB"], q="pool")
                transpose_fm(lambda t_: CKB, "CKB", 1, 128, XT, "XT")
                dma(kts_sv[:, :, b, t8 * 128:(t8 + 1) * 128], XT[:, 0:KC * 128].rearrange("p (f t) -> p f t", f=KC),
                    reads=["XT"], writes=["kts_s"])
        Ls, lsk = sample_state["L"]
        for b in range(SB_):
            for t8 in range(8):
                li = nxt("lg", 2)
                dma(LGt[li][:, :], clf[b, t8 * 128:(t8 + 1) * 128, :], reads=[], writes=[("LGt", li)])
                cumsum_tile(LGt[li][:, :], ("LGt", li), 128, CALLs[:, b, t8, :], "CALLs", CARRYs[:, b, t8, :], CARRYs[:, b, t8 + 1, :],
                            "CARRYs", sel127, "sel127")
            li = nxt("lg", 2)
            dma(LGt[li][0:ST_, :], Ls[b * ST_:(b + 1) * ST_, :], reads=[lsk], writes=[("LGt", li)])
            cumsum_tile(LGt[li][0:ST_, :], ("LGt", li), ST_, CALLs[0:ST_, b, 8, :], "CALLs", CARRYs[0:ST_, b, 8, :], CARRYs[:, b, 9, :],
                        "CARRYs", sel31, "sel31")
        cp("dve", QS[:, :], QTST[:, 0:KC * NS], ["QTST"], ["QS"])
        cp("dve", VBS[:, :], VB[:, 0:1040], ["VB"], ["VBS"])
        fence(KEYS_A, KEYS_B)
        for i in range(2):
            memset("dve", VR[i][:, :], 1.0, [f"VR{i}"])
        for b in range(SB_):
            cp("dve", QW[:, 0:KC * ST_].rearrange("p (f t) -> p f t", f=KC),
               QS[:, :].rearrange("p (f b t) -> p f b t", f=KC, b=SB_)[:, :, b, :], ["QS"], ["QW"])
            tt("dve", BIAS[:, 0:H * 9].rearrange("p (h t) -> p h t", h=H), CARRYs[:, b, 9, :].unsqueeze(2).to_broadcast([128, H, 9]),
               CALLs[:, b, :, :].rearrange("p t h -> p h t"), ALU.subtract, ["CALLs", "CARRYs"], ["BIAS"])
            qv = QW[:, 0:KC * ST_].rearrange("p (f t) -> p f t", f=KC)
            for hp in range(8):
                ki = nxt("kr", 2)
                vi = nxt("vr", 2)
                dma(KR[ki][:, 0:PAST + ST_], kts_sv[:, hp, b, :], reads=["kts_s"], writes=[f"KR{ki}"])
                vrv = VR[vi][:, 0:9 * 130].rearrange("p (t h e) -> p t h e", t=9, e=65)
                with slow_dma("V head split"):
                    for h2 in range(2):
                        dma(vrv[:, 0:8, h2, 0:64], cv[b].rearrange("(t p) (h e) -> p t h e", p=128, e=64)[:, :, 2 * hp + h2, :],
                            reads=[], writes=[f"VR{vi}"], q="pool")
                    dma(vrv[0:ST_, 8, :, 0:64], VBS[b * ST_:(b + 1) * ST_, :].rearrange("p (h e) -> p h e", e=65)[:, 2 * hp:2 * hp + 2, 0:64],
                        reads=["VBS"], writes=[f"VR{vi}"])
                for hl in range(2):
                    h = 2 * hp + hl
                    st_ = att_open(ST_)
                    tiles = []
                    for t9 in range(9):
                        kn = 128 if t9 < 8 else ST_
                        tiles.append((KR[ki][64 * hl:64 * hl + 64, t9 * 128:t9 * 128 + kn],
                                      VR[vi][0:kn, t9 * 130 + hl * 65:t9 * 130 + hl * 65 + 65], kn,
                                      (MKS[0:ST_, 0:ST_] if t9 == 8 else None)))
                    it1 = group_item(st_, qv[64 * hl:64 * hl + 64, hp, :], "QW", tiles[0:8], [f"KR{ki}", f"VR{vi}"],
                                     BIAS[:, h * 9:h * 9 + 8], ["BIAS"])
                    it2 = group_item(st_, qv[64 * hl:64 * hl + 64, hp, :], "QW", tiles[8:9], [f"KR{ki}", f"VR{vi}"],
                                     BIAS[:, h * 9 + 8:h * 9 + 9], ["BIAS"])
                    run_items([it1, it2])
                    att_close(st_, HA[64 * hl:64 * hl + 64, hp, b * ST_:(b + 1) * ST_], "HA")
        layer1_rest(NS, SB_, ST_, CTFs, "CTFs", ys.rearrange("(t p) d -> p t d", p=128), halo=False)
        with slow_dma("conv state layout (tiny)"):
            for b in range(SB_):
                for t2 in range(2):
                    dma(sfc[1, b, t2].rearrange("(j p) -> p j", p=128), CTFs[:, 1, :, b, t2], reads=["CTFs"], writes=["sfc"])
        fence(KEYS_B, KEYS_A)
        memset("pool", VB[:, 0:4 * 1040], 1.0, ["VB"])

        for c in range(NCH):
            phase_a(xp[c * CH:(c + 1) * CH, :].rearrange("(t p) d -> p t d", p=128), CH, 1, CH, CTA, "CTA", CTF, "CTF", False, c)
        with slow_dma("conv state layout (tiny)"):
            for t2 in range(2):
                dma(pca[t2].rearrange("(j p) -> p j", p=128), CTA[:, :, 0, t2], reads=["CTA"], writes=["pca"])
                dma(pfc0[t2].rearrange("(j p) -> p j", p=128), CTF[:, 0, :, 0, t2], reads=["CTF"], writes=["pfc0"])
        fence(KEYS_A, KEYS_B)

        ktsv = kts.rearrange("(f p) t -> p f t", p=128)
        dma(x1own.rearrange("(s o r) d -> s o r d", o=1, r=CH),
            x1s[2:SEQ + 2, :].rearrange("(s p r) d -> s p r d", p=2, r=CH)[:, bass.ds(par, 1), :, :], reads=["x1s"], writes=["x1own"])
        dma(x1hal.rearrange("s (o t) d -> s o t d", o=1),
            x1s[0:SEQ, :].rearrange("(s p r) d -> s p r d", p=2, r=CH)[:, bass.ds(par, 1), 0:2, :], reads=["x1s", "x1pad"], writes=["x1hal"])
        dma(qown[:, :, 2:CH + 2].rearrange("f s (o t) -> f s o t", o=1),
            qts[:, 2:SEQ + 2].rearrange("f (s p r) -> f s p r", p=2, r=CH)[:, :, bass.ds(par, 1), :], reads=["qts"], writes=["qown"])
        with slow_dma("tiny halo columns"):
            dma(qown[:, :, 0:2].rearrange("f s (o t) -> f s o t", o=1),
                qts[:, 0:SEQ].rearrange("f (s p r) -> f s p r", p=2, r=CH)[:, :, bass.ds(par, 1), 0:2], reads=["qts", "qtpad"], writes=["qownh"])
        qownv = qown.rearrange("(f p) s t -> p f s t", p=128)
        for s in range(NSLOT):
            nkt = 8 * (s + 1)
            nkth = 8 * s + 4
            dma(QW[:, 0:KC * 514].rearrange("p (f t) -> p f t", f=KC), qownv[:, :, s, :], reads=["qown", "qownh"], writes=["QW"])
            dma(X[:, 0:4, :], x1own[s * CH:(s + 1) * CH, :].rearrange("(t p) d -> p t d", p=128), reads=["x1own"], writes=["X"])
            dma(XH[0:2, :], x1hal[s], reads=["x1hal"], writes=["XH"])
            qv = QW[:, 0:KC * 514].rearrange("p (f t) -> p f t", f=KC)
            jref = 8 * s + 8
            tt("dve", BIAS[:, 0:H * nkt].rearrange("p (h t) -> p h t", h=H), CARRY[:, jref, :].unsqueeze(2).to_broadcast([128, H, nkt]),
               CALL[:, 0:nkt, :].rearrange("p t h -> p h t"), ALU.subtract, ["CALL", "CARRY"], ["BIAS"])
            bv = BIAS[:, 0:H * nkt].rearrange("p (h t) -> p h t", h=H)
            for hp in range(8):
                stm = [att_open(CH) for _ in range(2)]
                sth = [att_open(2) for _ in range(2)]
                for seg in range(0, nkt, 16):
                    segn = min(16, nkt - seg)
                    ki = nxt("kr", 2)
                    vi = nxt("vr", 2)
                    dma(KR[ki][:, 0:segn * 128], ktsv[:, hp, seg * 128:(seg + segn) * 128], reads=["kts"], writes=[f"KR{ki}"])
                    dma(VR[vi][:, 0:segn * 130], vsc[hp, :, seg:seg + segn, :].rearrange("p t e -> p (t e)"), reads=["vsc"], writes=[f"VR{vi}"])
                    kvk = [f"KR{ki}", f"VR{vi}"]
                    items = []
                    for lt in range(segn):
                        t_ = seg + lt
                        j = t_ - 8 * s
                        for hl in range(2):
                            h = 2 * hp + hl
                            items.append(main_item(stm[hl], qv[64 * hl:64 * hl + 64, hp, 2:514], "QW",
                                                   KR[ki][64 * hl:64 * hl + 64, lt * 128:(lt + 1) * 128],
                                                   VR[vi][:, lt * 130 + hl * 65:lt * 130 + hl * 65 + 65], kvk,
                                                   bv[:, h, t_:t_ + 1], (MK[:, j, :] if j >= 0 else None)))
                    nh = min(segn, nkth - seg)
                    if nh > 0:
                        for hl in range(2):
                            h = 2 * hp + hl
                            tiles = []
                            for lt in range(nh):
                                j = seg + lt - 8 * s
                                tiles.append((KR[ki][64 * hl:64 * hl + 64, lt * 128:(lt + 1) * 128],
                                              VR[vi][:, lt * 130 + hl * 65:lt * 130 + hl * 65 + 65], 128,
                                              (MKH[:, j + 1, :] if j >= -1 else None)))
                            items.append(group_item(sth[hl], qv[64 * hl:64 * hl + 64, hp, 0:2], "QW", tiles, kvk,
                                                    bv[:, h, seg:seg + nh], ["BIAS"]))
                    run_items(items)
                for hl in range(2):
                    att_close(stm[hl], HA[64 * hl:64 * hl + 64, hp, :], "HA")
                    att_close(sth[hl], HAH[64 * hl:64 * hl + 64, hp, :], "HAH")
            layer1_rest(CH, 1, CH, CTF, "CTF", yp[s].rearrange("(t p) d -> p t d", p=128), halo=True)
        with slow_dma("conv state layout (tiny)"):
            for t2 in range(2):
                dma(pfc1[t2].rearrange("(j p) -> p j", p=128), CTF[:, 1, :, 0, t2], reads=["CTF"], writes=["pfc1"])
        P.emit()
    return nc


_NC_CACHE = {}


def kernel(x_prompt, x_sample, state_conv_a, state_ffn_conv, cache_k, cache_v, cache_logf,
           a_norm, w_a_in, a_conv_w, w_a_out, kv_norm, w_kv, b_f, b_norm, w_q, w_o,
           ffn_norm, w_ffn_up, ffn_conv_w, w_ffn_down, final_norm):
    f = lambda a: np.ascontiguousarray(np.asarray(a, dtype=np.float32))
    x_prompt, x_sample = f(x_prompt), f(x_sample)
    B = x_prompt.shape[0]
    assert x_prompt.shape[1] == SEQ
    n = 2 * B
    if "nc" not in _NC_CACHE:
        _NC_CACHE["nc"] = build_program()
    nc = _NC_CACHE["nc"]
    shared = dict(a_norm=f(a_norm).reshape(D), w_a_in=f(w_a_in)[0], a_conv_w=f(a_conv_w)[0], w_a_out=f(w_a_out)[0],
                  kv_norm=f(kv_norm), w_kv=f(w_kv), b_f=f(b_f), b_norm=f(b_norm).reshape(D), w_q=f(w_q)[0], w_o=f(w_o)[0],
                  ffn_norm=f(ffn_norm), w_up=f(w_ffn_up), ffn_conv_w=f(ffn_conv_w), w_dn=f(w_ffn_down), final_norm=f(final_norm))
    sca_, sfc_ = f(state_conv_a), f(state_ffn_conv)
    ck_, cv_, clf_ = f(cache_k), f(cache_v), f(cache_logf)
    in_maps = []
    for c in range(n):
        sl = slice(SB_ * c, SB_ * (c + 1))
        m = dict(shared)
        m.update(xp=x_prompt[c // 2], xs=x_sample[sl].reshape(NS, D), sta=sca_[0, sl], stf=np.ascontiguousarray(sfc_[:, sl]),
                 ck=ck_[sl].reshape(SB_, PAST, D), cv=cv_[sl].reshape(SB_, PAST, D), clf=clf_[sl])
        in_maps.append(m)
    res = run_bass_kernel_spmd(nc, in_maps, core_ids=list(range(n)))
    R = res.results
    NSLOT = SEQ // CH // 2
    DB = SB_ * n
    y_prompt = np.zeros((B, SEQ, D), np.float32)
    y_sample = np.zeros((DB, ST_, D), np.float32)
    p_conv_a = np.zeros((1, B, 2, D), np.float32)
    p_ffn_conv = np.zeros((2, B, 2, NUP), np.float32)
    p_k = np.zeros((B, SEQ, H, 64), np.float32)
    p_v = np.zeros((B, SEQ, H, 64), np.float32)
    p_logf = np.zeros((B, SEQ, H), np.float32)
    s_conv_a = np.zeros((1, DB, 2, D), np.float32)
    s_ffn_conv = np.zeros((2, DB, 2, NUP), np.float32)
    s_k = np.zeros((DB, ST_, H, 64), np.float32)
    s_v = np.zeros((DB, ST_, H, 64), np.float32)
    s_logf = np.zeros((DB, ST_, H), np.float32)
    for c in range(n):
        b, half = c // 2, c % 2
        r = R[c]
        for s in range(NSLOT):
            qb = 2 * s + half
            y_prompt[b, qb * CH:(qb + 1) * CH] = r["yp"][s]
        sl = slice(SB_ * c, SB_ * (c + 1))
        y_sample[sl] = r["ys"].reshape(SB_, ST_, D)
        if half == 0:
            p_conv_a[0, b] = r["pca"]
            p_ffn_conv[0, b] = r["pfc0"]
            p_k[b] = r["pk"].reshape(SEQ, H, 64)
            p_v[b] = r["pv"].reshape(SEQ, H, 64)
            p_logf[b] = r["plf"]
        else:
            p_ffn_conv[1, b] = r["pfc1"]
        s_conv_a[0, sl] = r["sca"]
        s_ffn_conv[:, sl] = r["sfc"]
        s_k[sl] = r["sk"].reshape(SB_, ST_, H, 64)
        s_v[sl] = r["sv"].reshape(SB_, ST_, H, 64)
        s_logf[sl] = r["slf"].reshape(SB_, ST_, H)
    return (y_prompt, y_sample, p_conv_a, p_ffn_conv, p_k, p_v, p_logf, s_conv_a, s_ffn_conv, s_k, s_v, s_logf)
```

```python
import os, sys, contextlib
import numpy as np
import concourse.bass as bass
import concourse.mybir as mybir
from concourse.bass_utils import run_bass_kernel_spmd

F32 = mybir.dt.float32
BF16 = mybir.dt.bfloat16
ALU = mybir.AluOpType
AF = mybir.ActivationFunctionType

D = 1024
NUP = 5632
DFF = 2816
H = 16
KC = 8
NF = 44
NPAIR = 22
PAST = 1024
SB_ = 4
ST_ = 32
NS = SB_ * ST_
EPS = 1e-6
SEQ = int(os.environ.get("YK_SEQ", "8192"))
CH = 512
STRICT = bool(int(os.environ.get("YK_STRICT", "0")))


class Prog:
    COMPUTE = ("pe", "act", "dve", "pool")

    def __init__(self, nc, es, n_dma_sems=14):
        self.nc = nc
        self.ops = []
        self.res = {}
        self.eng = {"pe": nc.tensor, "act": nc.scalar, "dve": nc.vector, "pool": nc.gpsimd, "sp": nc.sync}
        self.sem = {e: es.enter_context(nc.semaphore("s_" + e)) for e in self.COMPUTE}
        self.dma_sems = {q: [es.enter_context(nc.semaphore(f"d_{q}{i}")) for i in range(n_dma_sems)]
                         for q in ("sp", "pool")}
        self.eng_idx = {e: 0 for e in self.eng}

    def op(self, eng, fn, reads=(), writes=(), dma=False):
        o = dict(eng=eng, fn=fn, dma=dma, deps=[], milestone=False, idx=self.eng_idx[eng], dbg=(list(reads), list(writes)))
        self.eng_idx[eng] += 1
        deps = {}
        for r in reads:
            st = self.res.setdefault(r, dict(w=None, rs=[]))
            if st["w"] is not None:
                deps[id(st["w"])] = (st["w"], "raw")
        for w in writes:
            st = self.res.setdefault(w, dict(w=None, rs=[]))
            if st["w"] is not None and id(st["w"]) not in deps:
                deps[id(st["w"])] = (st["w"], "waw")
            for r in st["rs"]:
                if id(r) not in deps:
                    deps[id(r)] = (r, "war")
        for p, kind in deps.values():
            if p is o:
                continue
            if (not p["dma"]) and p["eng"] == eng and not dma and not STRICT:
                if eng == "pe":
                    continue
                if kind != "raw" or (o["idx"] - p["idx"]) > 3:
                    continue
            o["deps"].append(p)
            p["milestone"] = True
        for r in reads:
            rs = self.res[r]["rs"]
            if not dma and not STRICT:
                rs[:] = [x for x in rs if x["dma"] or x["eng"] != eng]
            rs.append(o)
        for w in writes:
            st = self.res[w]
            st["w"] = o
            st["rs"] = []
        self.ops.append(o)
        return o

    def emit(self, final_wait_eng="sp"):
        cnt = {e: 0 for e in self.COMPUTE}
        dcnt = {q: 0 for q in self.dma_sems}
        semuse = {}
        for o in self.ops:
            if o["dma"]:
                q = o["eng"]
                pool = self.dma_sems[q]
                s = pool[dcnt[q] % len(pool)]
                dcnt[q] += 1
                semuse[id(s)] = semuse.get(id(s), 0) + 16
                o["sig"] = (s, semuse[id(s)])
            elif o["milestone"]:
                cnt[o["eng"]] += 1
                o["sig"] = (self.sem[o["eng"]], cnt[o["eng"]])
        waited = {e: {} for e in self.eng}
        nwaits = 0
        for o in self.ops:
            e = o["eng"]
            h = self.eng[e]
            need = {}
            if o["dma"]:
                s, v = o["sig"]
                if v > 16:
                    need[id(s)] = (s, v - 16)
            for p in o["deps"]:
                s, v = p["sig"]
                if id(s) not in need or need[id(s)][1] < v:
                    need[id(s)] = (s, v)
            for k, (s, v) in need.items():
                if waited[e].get(k, 0) >= v:
                    continue
                h.wait_ge(s, v)
                nwaits += 1
                waited[e][k] = v
            try:
                ins = o["fn"]()
            except Exception:
                print("[prog] failing op:", o["eng"], o.get("dbg"), file=sys.stderr)
                raise
            if o["dma"]:
                ins.then_inc(o["sig"][0], 16)
            elif o["milestone"]:
                ins.then_inc(o["sig"][0], 1)
        h = self.eng[final_wait_eng]
        for q, pool in self.dma_sems.items():
            for s in pool:
                v = semuse.get(id(s), 0)
                if v > 0 and waited[final_wait_eng].get(id(s), 0) < v:
                    h.wait_ge(s, v)
        print(f"[prog] ops={len(self.ops)} waits={nwaits} milestones={cnt} dmas={dcnt}", file=sys.stderr)


def build_program():
    nc = bass.Bass("TRN2", target_bir_lowering=False)
    NCH = SEQ // CH
    NSLOT = NCH // 2
    NKT = SEQ // 128
    din = lambda n, s: nc.dram_tensor(n, list(s), F32, kind="ExternalInput").ap()
    dout = lambda n, s: nc.dram_tensor(n, list(s), F32, kind="ExternalOutput").ap()
    dscr = lambda n, s, dt: nc.dram_tensor(n, list(s), dt).ap()
    xp = din("xp", [SEQ, D]); xs = din("xs", [NS, D])
    sta = din("sta", [SB_, 2, D]); stf = din("stf", [2, SB_, 2, NUP])
    ck = din("ck", [SB_, PAST, D]); cv = din("cv", [SB_, PAST, D]); clf = din("clf", [SB_, PAST, H])
    a_norm = din("a_norm", [D]); w_a_in = din("w_a_in", [D, 3 * D]); a_conv_w = din("a_conv_w", [3, D])
    w_a_out = din("w_a_out", [D, D]); kv_norm = din("kv_norm", [D]); w_kv = din("w_kv", [D, 2 * D + H])
    b_f = din("b_f", [H]); b_norm = din("b_norm", [D]); w_q = din("w_q", [D, D]); w_o = din("w_o", [D, D])
    ffn_norm = din("ffn_norm", [2, D]); w_up = din("w_up", [2, D, NUP]); ffn_conv_w = din("ffn_conv_w", [2, 3, NUP])
    w_dn = din("w_dn", [2, DFF, D]); final_norm = din("final_norm", [D])
    gain_src = [a_norm, ffn_norm[0], kv_norm, b_norm, ffn_norm[1], final_norm]
    yp = dout("yp", [NSLOT, CH, D]); ys = dout("ys", [NS, D])
    pca = dout("pca", [2, D]); pfc0 = dout("pfc0", [2, NUP]); pfc1 = dout("pfc1", [2, NUP])
    pk = dout("pk", [SEQ, D]); pv = dout("pv", [SEQ, D]); plf = dout("plf", [SEQ, H])
    sca = dout("sca", [SB_, 2, D]); sfc = dout("sfc", [2, SB_, 2, NUP])
    sk = dout("sk", [NS, D]); sv = dout("sv", [NS, D]); slf = dout("slf", [NS, H])
    wb_ain = dscr("wb_ain", [KC, 128, KC, 3, 128], BF16)
    wb_up = dscr("wb_up", [2, NPAIR, 128, KC, 2, 128], BF16)
    wb_aout = dscr("wb_aout", [D, D], BF16)
    wb_kv = dscr("wb_kv", [D, 2 * D + H], BF16)
    wb_q = dscr("wb_q", [D, D], BF16)
    wb_o = dscr("wb_o", [D, D], BF16)
    wb_dn = dscr("wb_dn", [2, DFF, D], BF16)
    x1s = dscr("x1s", [SEQ + 2, D], F32)
    qts = dscr("qts", [D, SEQ + 2], BF16)
    kts = dscr("kts", [D, SEQ], BF16)
    vsc = dscr("vsc", [8, 128, NKT, 130], BF16)
    kts_s = dscr("kts_s", [D, SB_, PAST + ST_], BF16)
    halfc = dscr("halfc", [2, 128], F32)
    x1own = dscr("x1own", [NSLOT * CH, D], F32)
    x1hal = dscr("x1hal", [NSLOT, 2, D], F32)
    qown = dscr("qown", [D, NSLOT, CH + 2], BF16)

    with contextlib.ExitStack() as es:
        P = Prog(nc, es)
        sb_bytes = [0]

        def sbt(n, s, dt=F32):
            sb_bytes[0] += int(np.prod(s[1:])) * (2 if dt == BF16 else 4)
            return nc.alloc_sbuf_tensor(n, list(s), dt)
        ident = sbt("ident", [128, 128], BF16)
        identf = sbt("identf", [128, 128])
        tri = sbt("tri", [128, 128])
        sel127 = sbt("sel127", [128, 128])
        sel31 = sbt("sel31", [128, 128])
        Dm = sbt("Dm", [128, 512])
        ones_f = sbt("ones_f", [128, 64])
        HALF = sbt("HALF", [128, 1])
        DELTA = sbt("DELTA", [128, 8])
        DELTAH = sbt("DELTAH", [128, 5])
        JT = sbt("JT", [128, 8])
        epsT = sbt("epsT", [128, 1])
        MK = sbt("MK", [128, 8, 512], BF16)
        MKH = sbt("MKH", [128, 5, 2], BF16)
        MKS = sbt("MKS", [128, ST_], BF16)
        gainb = [sbt(f"gain{i}", [128, D]) for i in range(2)]
        bf_bc = sbt("bf_bc", [128, H])
        cwa = sbt("cwa", [128, KC, 3])
        cwf = sbt("cwf", [128, 2, NF, 3])
        X = sbt("X", [128, 4, D])
        XH = sbt("XH", [128, D])
        XN = sbt("XN", [128, 4, D], BF16)
        XNH = sbt("XNH", [128, D], BF16)
        XT = sbt("XT", [128, KC * CH], BF16)
        XTH = sbt("XTH", [128, KC, 2], BF16)
        HA = sbt("HA", [128, KC, CH], BF16)
        HAH = sbt("HAH", [128, KC, 2], BF16)
        HF = sbt("HF", [128, 11, CH], BF16)
        Ub = [sbt(f"U{i}", [128, CH + 2 * SB_]) for i in range(2)]
        Ab = [sbt(f"A{i}", [128, CH]) for i in range(3)]
        GCb = [sbt(f"GC{i}", [128, CH]) for i in range(2)]
        Gb = [sbt(f"G{i}", [128, CH]) for i in range(2)]
        CTA = sbt("CTA", [128, KC, SB_, 2])
        CTF = sbt("CTF", [128, 2, NF, SB_, 2])
        CTAs = sbt("CTAs", [128, KC, SB_, 2])
        CTFs = sbt("CTFs", [128, 2, NF, SB_, 2])
        ss = sbt("ss", [128, 8])
        rstd = sbt("rstd", [128, 8])
        junk = sbt("junk", [128, D])
        CALL = sbt("CALL", [128, NKT + 1, H])
        CARRY = sbt("CARRY", [128, NKT + 1, H])
        CALLs = sbt("CALLs", [128, SB_, 9, H])
        CARRYs = sbt("CARRYs", [128, SB_, 10, H])
        LG = [sbt(f"LG{i}", [128, H]) for i in range(2)]
        LGt = [sbt(f"LGt{i}", [128, H]) for i in range(2)]
        NWR = 3
        WR = sbt("WR", [128, NWR, 6144], BF16)
        fence_t = sbt("fence_t", [128, 2])
        QS = sbt("QS", [128, KC * NS], BF16)
        VBS = sbt("VBS", [128, 1040], BF16)
        SCR_BYTES = 46080
        SCR = sbt("SCR", [128, SCR_BYTES // 4])
        scr_b = SCR[:].bitcast(BF16)

        def scr_alloc(cur, nelem, dt):
            sz = 2 if dt == BF16 else 4
            off = (cur[0] + 63) // 64 * 64
            cur[0] = off + nelem * sz
            assert cur[0] <= SCR_BYTES, (cur[0], SCR_BYTES)
            return scr_b[:, off // 2: off // 2 + nelem] if dt == BF16 else SCR[:, off // 4: off // 4 + nelem]
        ca = [0]
        KVST = scr_alloc(ca, 2 * 2048, F32)
        VB = scr_alloc(ca, 4 * 1040, BF16)
        KTST = scr_alloc(ca, KC * CH, BF16)
        QTST = scr_alloc(ca, KC * CH, BF16)
        CKB = scr_alloc(ca, 1024, BF16)
        KEYS_A = ["VB", "KTST", "QTST", "CKB"] + [("KVST", p_, c_) for p_ in range(2) for c_ in range(4)]
        cb_ = [0]
        QW = scr_alloc(cb_, KC * 514, BF16)
        KR = [scr_alloc(cb_, 2048, BF16) for _ in range(2)]
        VR = [scr_alloc(cb_, 16 * 130, BF16) for _ in range(2)]
        NPT = 6
        PT = [scr_alloc(cb_, 512, BF16) for _ in range(NPT)]
        OS = scr_alloc(cb_, 512, F32)
        RB = scr_alloc(cb_, 512, F32)
        GS = scr_alloc(cb_, 512, F32)
        BIAS = scr_alloc(cb_, H * 64, F32)
        KEYS_B = ["QW", "KR0", "KR1", "VR0", "VR1"] + [f"PT{i}" for i in range(NPT)] + ["OS", "RBrow", "GS", "BIAS"]
        print(f"[sbuf] {sb_bytes[0] / 1024:.1f} KiB/partition, scrA={ca[0]} scrB={cb_[0]}", file=sys.stderr)
        psf = es.enter_context(nc.psum_tensor("psf", [128, 8, 512], F32))
        free_banks = list(range(8))
        rr = {}

        def bank_get():
            return free_banks.pop(0)

        def bank_put(b):
            free_banks.append(b)

        def nxt(k, n):
            v = rr.get(k, 0)
            rr[k] = (v + 1) % n
            return v

        def E(eng):
            return P.eng[eng]

        def cp(eng, out, in_, reads, writes):
            if eng == "act":
                P.op("act", lambda: nc.scalar.copy(out=out, in_=in_), reads, writes)
            else:
                P.op(eng, lambda: E(eng).tensor_copy(out=out, in_=in_), reads, writes)

        def act(out, in_, func, reads, writes, bias=None, scale=None, accum=None):
            kw = {}
            if bias is not None:
                kw["bias"] = bias
            if scale is not None:
                kw["scale"] = scale
            if accum is not None:
                kw["accum_out"] = accum
            P.op("act", lambda: nc.scalar.activation(out=out, in_=in_, func=func, **kw), reads, writes)

        def tt(eng, out, in0, in1, op, reads, writes):
            P.op(eng, lambda: E(eng).tensor_tensor(out=out, in0=in0, in1=in1, op=op), reads, writes)

        def ts(eng, out, in0, s1, s2, op0, op1, reads, writes):
            if op1 is None:
                P.op(eng, lambda: E(eng).tensor_scalar(out=out, in0=in0, scalar1=s1, scalar2=None, op0=op0), reads, writes)
            else:
                P.op(eng, lambda: E(eng).tensor_scalar(out=out, in0=in0, scalar1=s1, scalar2=s2, op0=op0, op1=op1), reads, writes)

        def stt(out, in0, scalar, in1, op0, op1, reads, writes):
            P.op("dve", lambda: nc.vector.scalar_tensor_tensor(out=out, in0=in0, scalar=scalar, in1=in1, op0=op0, op1=op1),
                 reads, writes)

        def mm(out, lhsT, rhs, start, stop, reads, writes, skip=False):
            if skip:
                P.op("pe", lambda: nc.tensor.matmul(out, lhsT=lhsT, rhs=rhs, start=start, stop=stop, skip_group_check=True), reads, writes)
            else:
                P.op("pe", lambda: nc.tensor.matmul(out, lhsT=lhsT, rhs=rhs, start=start, stop=stop), reads, writes)

        def tr(out, in_, idn, reads, writes):
            P.op("pe", lambda: nc.tensor.transpose(out=out, in_=in_, identity=idn), reads, writes)

        slow_flag = [False]

        @contextlib.contextmanager
        def slow_dma(reason=""):
            old = slow_flag[0]
            slow_flag[0] = True
            try:
                yield
            finally:
                slow_flag[0] = old

        def dma(out, in_, reads, writes, q="sp"):
            slow = slow_flag[0]
            if slow:
                P.op(q, lambda: E(q).dma_start(out=out, in_=in_, allow_slow_non_contiguous=True), reads, writes, dma=True)
            else:
                P.op(q, lambda: E(q).dma_start(out=out, in_=in_), reads, writes, dma=True)

        def memset(eng, ap, val, writes):
            P.op(eng, lambda: E(eng).memset(ap, val), [], writes)

        def fence(from_keys, to_keys):
            P.op("pool", lambda: nc.gpsimd.memset(fence_t[:], 0.0), reads=list(from_keys),
                 writes=list(from_keys) + list(to_keys) + ["fence_t"])

        def wload(src_ap, shape, rkey="wcast"):
            s = nxt("wr", NWR)
            n = int(np.prod(shape[1:]))
            dst = WR[:, s, 0:n]
            if len(shape) == 3:
                dstv = dst.rearrange("p (a b) -> p a b", a=shape[1])
            elif len(shape) == 4:
                dstv = dst.rearrange("p (a b c) -> p a b c", a=shape[1], b=shape[2])
            else:
                dstv = dst
            key = ("WR", s)
            dma(dstv, src_ap, reads=(list(rkey) if isinstance(rkey, list) else [rkey]), writes=[key])
            return dstv, key

        def gload(gi):
            s = nxt("gain", 2)
            dma(gainb[s][:], gain_src[gi].partition_broadcast(128), reads=[], writes=[("gain", s)])
            return gainb[s], ("gain", s)

        memset("pool", identf[:], 1.0, ["identf"])
        P.op("pool", lambda: nc.gpsimd.affine_select(out=identf[:], in_=identf[:], pattern=[[-1, 128]],
                                                     compare_op=ALU.is_equal, fill=0.0, base=0, channel_multiplier=1),
             reads=["identf"], writes=["identf"])
        cp("dve", ident[:], identf[:], ["identf"], ["ident"])
        memset("pool", tri[:], 1.0, ["tri"])
        P.op("pool", lambda: nc.gpsimd.affine_select(out=tri[:], in_=tri[:], pattern=[[1, 128]],
                                                     compare_op=ALU.is_ge, fill=0.0, base=0, channel_multiplier=-1),
             reads=["tri"], writes=["tri"])
        memset("pool", sel127[:], 1.0, ["sel127"])
        P.op("pool", lambda: nc.gpsimd.affine_select(out=sel127[:], in_=sel127[:], pattern=[[0, 128]], compare_op=ALU.is_equal,
                                                     fill=0.0, base=-127, channel_multiplier=1), reads=["sel127"], writes=["sel127"])
        memset("pool", sel31[:], 1.0, ["sel31"])
        P.op("pool", lambda: nc.gpsimd.affine_select(out=sel31[:], in_=sel31[:], pattern=[[0, 128]], compare_op=ALU.is_equal,
                                                     fill=0.0, base=-31, channel_multiplier=1), reads=["sel31"], writes=["sel31"])
        P.op("pool", lambda: nc.gpsimd.iota(Dm[:], pattern=[[-1, 512]], base=0, channel_multiplier=1,
                                            allow_small_or_imprecise_dtypes=True), writes=["Dm"])
        P.op("pool", lambda: nc.gpsimd.iota(JT[:], pattern=[[-128, 8]], base=0, channel_multiplier=0,
                                            allow_small_or_imprecise_dtypes=True), writes=["JT"])
        memset("dve", ones_f[:], 1.0, ["ones_f"])
        memset("dve", junk[:], 0.0, ["junk"])
        memset("dve", epsT[:], EPS, ["epsT"])
        memset("dve", CTA[:], 0.0, ["CTA"])
        memset("pool", CTF[:], 0.0, ["CTF"])
        memset("dve", CARRY[:, 0, :], 0.0, ["CARRY"])
        memset("dve", CARRYs[:], 0.0, ["CARRYs"])
        memset("dve", CALLs[:], 0.0, ["CALLs"])
        memset("dve", XH[:], 0.0, ["XH"])
        memset("dve", XNH[:], 0.0, ["XNH"])
        dma(halfc[0:1, :], junk[0:1, 0:128], reads=["junk"], writes=["halfc0"])
        dma(halfc[1:2, 0:64], ones_f[0:1, 0:64], reads=["ones_f"], writes=["halfc1"])
        dma(halfc[1:2, 64:128], ones_f[0:1, 0:64], reads=["ones_f"], writes=["halfc2"])
        pid = nc.sync.partition_id()
        par = pid % 2
        with slow_dma("tiny"):
            dma(HALF[:], halfc[bass.ds(par, 1), :].rearrange("a p -> p a"), reads=["halfc0", "halfc1", "halfc2"], writes=["HALF"])
        ts("dve", HALF[:], HALF[:], 512.0, None, ALU.mult, None, ["HALF"], ["H512"])
        ts("dve", DELTA[:], JT[:], HALF[:, 0:1], None, ALU.add, None, ["H512", "JT"], ["DELTA"])
        ts("dve", DELTAH[:], JT[:, 0:5], HALF[:, 0:1], 126.0, ALU.add, ALU.add, ["H512", "JT"], ["DELTAH"])
        for j in range(8):
            ts("pool" if j % 2 else "dve", MK[:, j, :], Dm[:], DELTA[:, j:j + 1], None, ALU.is_le, None, ["Dm", "DELTA"], ["MK"])
        for i in range(5):
            ts("dve", MKH[:, i, :], Dm[:, 0:2], DELTAH[:, i:i + 1], None, ALU.is_le, None, ["Dm", "DELTAH"], ["MKH"])
        ts("dve", MKS[:], Dm[:, 0:ST_], 0.0, None, ALU.is_le, None, ["Dm"], ["MKS"])
        dma(x1s[0:2, :], junk[0:2, :], reads=["junk"], writes=["x1pad"])
        zb = junk[:].bitcast(BF16)
        with slow_dma("tiny pad"):
            dma(qts[:, 0:2].rearrange("(f p) t -> p f t", p=128), zb[:, 0:16].rearrange("p (f t) -> p f t", t=2),
                reads=["junk"], writes=["qtpad"])
        dma(bf_bc[:], b_f.partition_broadcast(128), reads=[], writes=["bf_bc"])
        with slow_dma("tiny conv weight layout"):
            for w_ in range(3):
                dma(cwa[:, :, w_], a_conv_w[w_].rearrange("(j p) -> p j", p=128), reads=[], writes=["cw"])
                for l in range(2):
                    dma(cwf[:, l, :, w_], ffn_conv_w[l, w_].rearrange("(j p) -> p j", p=128), reads=[], writes=["cw"])
        for j in range(KC):
            for g in range(3):
                dma(wb_ain[j][:, :, g, :], w_a_in.rearrange("(kc p) (g f) -> p kc g f", p=128, g=3)[:, :, g, j * 128:(j + 1) * 128],
                    reads=[], writes=[("wc", "ain", j, g)], q="pool")
        dma(wb_aout, w_a_out, reads=[], writes=[("wc", "aout")], q="pool")
        for pr in range(NPAIR):
            for w_ in range(2):
                dma(wb_up[0, pr][:, :, w_, :], w_up[0].rearrange("(kc p) (w f) -> p kc w f", p=128, w=2)[:, :, w_, pr * 128:(pr + 1) * 128],
                    reads=[], writes=[("wc", "up", 0, pr, w_)], q="pool")
        dma(wb_dn[0], w_dn[0], reads=[], writes=[("wc", "dn", 0)], q="pool")
        dma(wb_kv, w_kv, reads=[], writes=[("wc", "kv")], q="pool")
        dma(wb_q, w_q, reads=[], writes=[("wc", "q")], q="pool")
        dma(wb_o, w_o, reads=[], writes=[("wc", "o")], q="pool")
        for pr in range(NPAIR):
            for w_ in range(2):
                dma(wb_up[1, pr][:, :, w_, :], w_up[1].rearrange("(kc p) (w f) -> p kc w f", p=128, w=2)[:, :, w_, pr * 128:(pr + 1) * 128],
                    reads=[], writes=[("wc", "up", 1, pr, w_)], q="pool")
        dma(wb_dn[1], w_dn[1], reads=[], writes=[("wc", "dn", 1)], q="pool")

        def mm_group(out_ap, pairs, bank_key):
            n = len(pairs)
            for i, (l_ap, r_ap, rds) in enumerate(pairs):
                mm(out_ap, l_ap, r_ap, i == 0, i == n - 1, rds, [bank_key])

        def xk(xkey, t_):
            return (xkey, t_) if xkey == "X" else xkey

        def rms_stats_tile(Xt, xkey, t_, npart):
            act(junk[0:npart, :], Xt(t_), AF.Square, [xk(xkey, t_)], ["junk", ("ss", t_)], accum=ss[0:npart, t_:t_ + 1])
            act(rstd[0:npart, t_:t_ + 1], ss[0:npart, t_:t_ + 1], AF.Sqrt, [("ss", t_), "epsT"], [("rstd", t_)],
                bias=epsT[0:npart, 0:1], scale=1.0 / D)
            P.op("dve", lambda: nc.vector.reciprocal(out=rstd[0:npart, t_:t_ + 1], in_=rstd[0:npart, t_:t_ + 1]),
                 reads=[("rstd", t_)], writes=[("rstd", t_)])

        def rms_apply_tile(Xt, xkey, t_, npart, g, gkey, ofn, okey):
            wk = xk(okey, t_)
            stt(ofn(t_), Xt(t_), rstd[0:npart, t_:t_ + 1], g[0:npart, :], ALU.mult, ALU.mult,
                [xk(xkey, t_), ("rstd", t_), gkey], [wk])

        def rms_stats(Xt, xkey, nt, npart):
            for t_ in range(nt):
                rms_stats_tile(Xt, xkey, t_, npart)

        def rms_apply(Xt, xkey, nt, npart, gi, ofn, okey, extra_reads=()):
            g, gkey = gload(gi)
            for t_ in range(nt):
                rms_apply_tile(Xt, xkey, t_, npart, g, gkey, ofn, okey)

        def norm_cb(Xt, xkey, npart, gi, ofn, okey):
            g, gkey = gload(gi)

            def cb(t_):
                rms_stats_tile(Xt, xkey, t_, npart)
                rms_apply_tile(Xt, xkey, t_, npart, g, gkey, ofn, okey)
            return cb

        def xall(nt):
            return [("X", t_) for t_ in range(nt)]

        def transpose_fm(src_fn, skey, nt, npart, dst_flat, dkey):
            blocks = [(kc, t_) for kc in range(KC) for t_ in range(nt)]
            for g0 in range(0, len(blocks), 4):
                grp = blocks[g0:g0 + 4]
                tb = bank_get()
                pbv = psf[:, tb, :].bitcast(BF16)
                for i, (kc, t_) in enumerate(grp):
                    tr(pbv[:, i * npart:(i + 1) * npart], src_fn(t_)[:, kc * 128:(kc + 1) * 128], ident[0:npart, 0:npart],
                       [skey, "ident"], [("ps", tb)])
                w = len(grp) * npart
                cp(("act", "dve")[nxt("ev", 2)], dst_flat[:, g0 * npart:g0 * npart + w], pbv[:, 0:w], [("ps", tb)], [dkey])
                bank_put(tb)

        def conv3(ps_ap3, pskey, pre_fn, cw_ap, ct_ap, ctkey, nb, T):
            ui = nxt("u", 2)
            U = Ub[ui][:, 0:nb * (T + 2)].rearrange("p (b t) -> p b t", b=nb)
            ukey = ("U", ui)
            ai = nxt("a", 3)
            A = Ab[ai][:, 0:nb * T].rearrange("p (b t) -> p b t", b=nb)
            akey = ("A", ai)
            if pre_fn is None:
                cp("act", U[:, :, 2:2 + T], ps_ap3, [pskey], [ukey])
                act(A, ps_ap3, AF.Copy, [pskey, "cw"], [akey], scale=cw_ap[:, 2:3])
            else:
                pre_fn(U[:, :, 2:2 + T], ukey)
                act(A, U[:, :, 2:2 + T], AF.Copy, [ukey, "cw"], [akey], scale=cw_ap[:, 2:3])
            cp("pool", U[:, :, 0:2], ct_ap, [ctkey], [ukey])
            cp("pool", ct_ap, U[:, :, T:T + 2], [ukey], [ctkey])
            for k in (1, 0):
                stt(A, U[:, :, k:k + T], cw_ap[:, k:k + 1], A, ALU.mult, ALU.add, [ukey, akey, "cw"], [akey])
            return A, akey

        def ffn_halo(l, CT, ctkey):
            hb = bank_get()
            hkey = ("ps", hb)
            for pr in range(NPAIR):
                wt, wkey = wload(wb_up[l, pr], [128, KC, 2, 128], [("wc", "up", l, pr, 0), ("wc", "up", l, pr, 1)])
                for wh in range(2):
                    j = wh * NPAIR + pr
                    mm_group(psf[:, hb, 2 * j:2 * j + 2], [(wt[:, kc, wh, :], XTH[:, kc, :], [wkey, "XTH"]) for kc in range(KC)], hkey)
            cp("act", CT[:, l, :, 0, :], psf[:, hb, 0:2 * NF].rearrange("p (j t) -> p j t", t=2), [hkey], [ctkey])
            bank_put(hb)

        def w_rows(wb, k0, kn, hf):
            return wb.rearrange("(k p) n -> p k n", p=128)[:, k0:k0 + kn, hf * 512:(hf + 1) * 512]

        def down_proj(lhs_fn, lkey, k_base, nk, wb, nt, npart, Xt, xkey, rkey, after_tile=None):
            ws = [wload(w_rows(wb, k_base, nk, hf), [128, nk, 512], rkey) for hf in range(2)]
            for t_ in range(nt):
                for hf in range(2):
                    wt, wkey = ws[hf]
                    b_ = bank_get()
                    bkey = ("ps", b_)
                    mm_group(psf[0:npart, b_, :], [(lhs_fn(k, t_), wt[:, k, :], [lkey, wkey]) for k in range(nk)], bkey)
                    tt("dve", Xt(t_)[:, hf * 512:(hf + 1) * 512], psf[0:npart, b_, :], Xt(t_)[:, hf * 512:(hf + 1) * 512], ALU.add,
                       [bkey, xk(xkey, t_)], [xk(xkey, t_)])
                    bank_put(b_)
                if after_tile is not None:
                    after_tile(t_)

        def ffn_full(l, xtkey, N, nb, T, CT, ctkey, nt, npart, Xt, after_tile=None):
            for grp in range(2):
                for pl in range(11):
                    pr = grp * 11 + pl
                    wt, wkey = wload(wb_up[l, pr], [128, KC, 2, 128], [("wc", "up", l, pr, 0), ("wc", "up", l, pr, 1)])
                    As = []
                    for wh in range(2):
                        j = wh * NPAIR + pr
                        b_ = bank_get()
                        bkey = ("ps", b_)
                        mm_group(psf[:, b_, 0:N], [(wt[:, kc, wh, :], XT[:, kc * N:(kc + 1) * N], [wkey, xtkey]) for kc in range(KC)], bkey)
                        A, akey = conv3(psf[:, b_, 0:N].rearrange("p (b t) -> p b t", b=nb), bkey, None, cwf[:, l, j, :],
                                        CT[:, l, j, 0:nb, :], ctkey, nb, T)
                        bank_put(b_)
                        As.append((A, akey))
                    gi = nxt("g", 2)
                    G = Gb[gi][:, 0:N].rearrange("p (b t) -> p b t", b=nb)
                    act(G, As[0][0], AF.Silu, [As[0][1]], [("G", gi)])
                    tt("pool", HF[:, pl, 0:N].rearrange("p (b t) -> p b t", b=nb), G, As[1][0], ALU.mult, [("G", gi), As[1][1]], ["HF"])
                down_proj(lambda k, t_: HF[:, k, t_ * 128:t_ * 128 + npart], "HF", grp * 11, 11, wb_dn[l], nt, npart, Xt, "X", ("wc", "dn", l),
                          after_tile=(after_tile if grp == 1 else None))

        def logsig(ps_lf, pkey, npart, li):
            L, Lt = LG[li], LGt[li]
            lk, ltk = ("LG", li), ("LGt", li)
            tt("dve", L[0:npart], ps_lf, bf_bc[0:npart], ALU.add, [pkey, "bf_bc"], [lk])
            ts("dve", Lt[0:npart], L[0:npart], -1.0, None, ALU.mult, None, [lk], [ltk])
            tt("dve", Lt[0:npart], Lt[0:npart], L[0:npart], ALU.min, [lk, ltk], [ltk])
            act(Lt[0:npart], Lt[0:npart], AF.Exp, [ltk], [ltk])
            act(Lt[0:npart], Lt[0:npart], AF.Ln, [ltk], [ltk], bias=1.0, scale=1.0)
            ts("dve", L[0:npart], L[0:npart], 0.0, None, ALU.min, None, [lk], [lk])
            tt("dve", L[0:npart], L[0:npart], Lt[0:npart], ALU.subtract, [lk, ltk], [lk])
            return L, lk

        def cumsum_tile(L_ap, lk, npart, call_ap, ckey, carry_in_ap, carry_out_ap, cakey, sel, selkey):
            b_ = bank_get()
            bkey = ("ps", b_)
            mm(psf[0:npart, b_, 0:H], tri[0:npart, 0:npart], L_ap, True, True, ["tri", lk], [bkey])
            tt("dve", call_ap, psf[0:npart, b_, 0:H], carry_in_ap, ALU.add, [bkey, cakey], [ckey])
            mm(psf[:, b_, 256:256 + H], sel[0:npart, :], call_ap, True, True, [selkey, ckey], [bkey])
            cp("act", carry_out_ap, psf[:, b_, 256:256 + H], [bkey], [cakey])
            bank_put(b_)

        def phase_a(x_src, N, nb, T, cta, ctakey, ctf, ctfkey, sample, chunk_idx):
            nt = max(1, N // 128)
            npart = min(N, 128)
            Xt = lambda t_: X[0:npart, t_, :]
            XNt = lambda t_: XN[0:npart, t_, :]
            dma(X[0:npart, 0:nt, :], x_src, reads=[], writes=xall(nt))
            rms_stats(Xt, "X", nt, npart)
            rms_apply(Xt, "X", nt, npart, 0, XNt, "XN")
            transpose_fm(XNt, "XN", nt, npart, XT, "XT")
            for j in range(KC):
                wt, wkey = wload(wb_ain[j], [128, KC, 3, 128], [("wc", "ain", j, g_) for g_ in range(3)])
                bs = [bank_get() for _ in range(3)]
                for g in range(3):
                    mm_group(psf[:, bs[g], 0:N], [(wt[:, kc, g, :], XT[:, kc * N:(kc + 1) * N], [wkey, "XT"]) for kc in range(KC)],
                             ("ps", bs[g]))
                gci = nxt("gc", 2)
                GC = GCb[gci][:, 0:N].rearrange("p (b t) -> p b t", b=nb)
                cp("act", GC, psf[:, bs[1], 0:N].rearrange("p (b t) -> p b t", b=nb), [("ps", bs[1])], [("GC", gci)])

                def pre(Uv, ukey, b2=bs[2], GC=GC, gci=gci):
                    tt("dve", Uv, psf[:, b2, 0:N].rearrange("p (b t) -> p b t", b=nb), GC, ALU.mult, [("ps", b2), ("GC", gci)], [ukey])
                A, akey = conv3(None, None, pre, cwa[:, j, :], cta[:, j, 0:nb, :], ctakey, nb, T)
                tt("dve", HA[:, j, 0:N].rearrange("p (b t) -> p b t", b=nb), psf[:, bs[0], 0:N].rearrange("p (b t) -> p b t", b=nb), A,
                   ALU.mult, [("ps", bs[0]), akey], ["HA"])
                for b_ in bs:
                    bank_put(b_)
            down_proj(lambda k, t_: HA[:, k, t_ * 128:t_ * 128 + npart], "HA", 0, KC, wb_aout, nt, npart, Xt, "X", ("wc", "aout"),
                      after_tile=norm_cb(Xt, "X", npart, 1, XNt, "XN"))
            transpose_fm(XNt, "XN", nt, npart, XT, "XT")
            ffn_full(0, "XT", N, nb, T, ctf, ctfkey, nt, npart, Xt, after_tile=norm_cb(Xt, "X", npart, 2, XNt, "XN"))
            tok0 = chunk_idx * CH
            if not sample:
                for t_ in range(nt):
                    dma(x1s[2 + tok0 + t_ * 128:2 + tok0 + (t_ + 1) * 128, :], X[:, t_, :], reads=[("X", t_)], writes=["x1s"])
            transpose_fm(XNt, "XN", nt, npart, XT, "XT")
            wkv = wb_kv.rearrange("(k p) n -> p k n", p=128)
            for cb in range(4):
                wt, wkey = wload(wkv[:, :, cb * 512:(cb + 1) * 512], [128, KC, 512], ("wc", "kv"))
                for t_ in range(nt):
                    b_ = bank_get()
                    mm_group(psf[0:npart, b_, :], [(XT[:, kc * N + t_ * 128:kc * N + t_ * 128 + npart], wt[:, kc, :], ["XT", wkey])
                                                   for kc in range(KC)], ("ps", b_))
                    kvo = (t_ % 2) * 2048 + cb * 512
                    kvkey = ("KVST", t_ % 2, cb)
                    cp("act", KVST[0:npart, kvo:kvo + 512], psf[0:npart, b_, :], [("ps", b_)], [kvkey])
                    if cb >= 2:
                        h0 = (cb - 2) * 8
                        vbv = VB[0:npart, t_ * 1040:(t_ + 1) * 1040].rearrange("p (h e) -> p h e", e=65)
                        cp("dve", vbv[:, h0:h0 + 8, 0:64], KVST[0:npart, kvo:kvo + 512].rearrange("p (h e) -> p h e", e=64), [kvkey], ["VB"])
                    bank_put(b_)
                    dst = ((sk, sv) if sample else (pk, pv))[cb // 2]
                    r0 = 0 if sample else tok0 + t_ * 128
                    dma(dst[r0:r0 + npart, (cb % 2) * 512:(cb % 2 + 1) * 512], KVST[0:npart, kvo:kvo + 512],
                        reads=[kvkey], writes=["kvout"])
                if cb < 2:
                    for f4 in range(4):
                        f = cb * 4 + f4
                        b_ = bank_get()
                        mm_group(psf[:, b_, 0:N], [(wt[:, kc, f4 * 128:(f4 + 1) * 128], XT[:, kc * N:(kc + 1) * N], [wkey, "XT"])
                                                   for kc in range(KC)], ("ps", b_))
                        cp(("act", "dve")[nxt("ev", 2)], KTST[:, f * N:(f + 1) * N], psf[:, b_, 0:N], [("ps", b_)], ["KTST"])
                        bank_put(b_)
            wtl, wlkey = wload(wkv[:, :, 2048:2064], [128, KC, H], ("wc", "kv"))
            for t_ in range(nt):
                b_ = bank_get()
                mm_group(psf[0:npart, b_, 0:H], [(XT[:, kc * N + t_ * 128:kc * N + t_ * 128 + npart], wtl[:, kc, :], ["XT", wlkey])
                                                 for kc in range(KC)], ("ps", b_))
                li = nxt("lg", 2)
                L, lk = logsig(psf[0:npart, b_, 0:H], ("ps", b_), npart, li)
                bank_put(b_)
                if not sample:
                    j = chunk_idx * 4 + t_
                    dma(plf[tok0 + t_ * 128:tok0 + (t_ + 1) * 128, :], L[:, :], reads=[lk], writes=["lfout"])
                    cumsum_tile(L[:, :], lk, 128, CALL[:, j, :], "CALL", CARRY[:, j, :], CARRY[:, j + 1, :], "CARRY", sel127, "sel127")
                else:
                    dma(slf[:, :], L[0:npart, :], reads=[lk], writes=["lfout"])
                    sample_state["L"] = (L, lk)
            if not sample:
                with slow_dma("V head split"):
                    for hp in range(8):
                        dma(vsc[hp, :, chunk_idx * 4:chunk_idx * 4 + 4, :],
                            VB[:, 0:4 * 1040].rearrange("p (t x) -> p t x", t=4)[:, :, 2 * hp * 65:2 * hp * 65 + 130],
                            reads=["VB"], writes=["vsc"])
                dma(kts.rearrange("(f p) t -> p f t", p=128)[:, :, tok0:tok0 + CH],
                    KTST[:, 0:KC * N].rearrange("p (f t) -> p f t", f=KC), reads=["KTST"], writes=["kts"])
            rms_apply(Xt, "X", nt, npart, 3, XNt, "XN")
            transpose_fm(XNt, "XN", nt, npart, XT, "XT")
            wqv = wb_q.rearrange("(k p) n -> p k n", p=128)
            for cb in range(2):
                wt, wkey = wload(wqv[:, :, cb * 512:(cb + 1) * 512], [128, KC, 512], ("wc", "q"))
                for f4 in range(4):
                    f = cb * 4 + f4
                    b_ = bank_get()
                    mm_group(psf[:, b_, 0:N], [(wt[:, kc, f4 * 128:(f4 + 1) * 128], XT[:, kc * N:(kc + 1) * N], [wkey, "XT"])
                                               for kc in range(KC)], ("ps", b_))
                    cp(("act", "dve")[nxt("ev", 2)], QTST[:, f * N:(f + 1) * N], psf[:, b_, 0:N], [("ps", b_)], ["QTST"])
                    bank_put(b_)
            if not sample:
                dma(qts.rearrange("(f p) t -> p f t", p=128)[:, :, 2 + tok0:2 + tok0 + CH],
                    QTST[:, 0:KC * N].rearrange("p (f t) -> p f t", f=KC), reads=["QTST"], writes=["qts"])

        sample_state = {}

        def att_open(nq):
            ob = bank_get()
            return dict(ob=ob, okey=("ps", ob), first=True, nq=nq)

        def att_close(stt_, out_ap, okey_out):
            nq, ob, okb = stt_["nq"], stt_["ob"], stt_["okey"]
            ts("dve", RB[64:65, 0:nq], psf[64:65, ob, 0:nq], 1e-30, None, ALU.add, None, [okb], ["RBrow"])
            P.op("dve", lambda: nc.vector.reciprocal(out=RB[64:65, 0:nq], in_=RB[64:65, 0:nq]), reads=["RBrow"], writes=["RBrow"])
            bb = bank_get()
            mm(psf[0:64, bb, 0:nq], ones_f[64:65, 0:64], RB[64:65, 0:nq], True, True, ["RBrow", "ones_f"], [("ps", bb)])
            cp("dve", OS[0:64, 0:nq], psf[0:64, ob, 0:nq], [okb], ["OS"])
            tt("dve", out_ap, OS[0:64, 0:nq], psf[0:64, bb, 0:nq], ALU.mult, ["OS", ("ps", bb)], [okey_out])
            bank_put(ob)
            bank_put(bb)

        def run_items(items, lag=3):
            for i, it in enumerate(items):
                it["qk"]()
                it["sm"]()
                if i >= lag:
                    items[i - lag]["pv"]()
            for it in items[max(0, len(items) - lag):]:
                it["pv"]()

        def main_item(stt_, q_ap, qkey, kt_ap, v_ap, kvkeys, bias_ap, mask_ap, kn=128):
            nq = stt_["nq"]
            d = {}

            def qk():
                sb_ = bank_get()
                d["sb"] = sb_
                mm(psf[0:kn, sb_, 0:nq], kt_ap, q_ap, True, True, kvkeys + [qkey], [("ps", sb_)])

            def sm():
                pi = nxt("pt", NPT)
                d["pt"] = pi
                ptv = PT[pi]
                act(ptv[0:kn, 0:nq], psf[0:kn, d["sb"], 0:nq], AF.Exp, [("ps", d["sb"]), "BIAS"], [f"PT{pi}"], bias=bias_ap, scale=0.125)
                bank_put(d["sb"])
                if mask_ap is not None:
                    tt(("pool", "dve")[nxt("mk", 2)], ptv[0:kn, 0:nq], ptv[0:kn, 0:nq], mask_ap, ALU.mult, [f"PT{pi}", "MK"], [f"PT{pi}"])

            def pv():
                pi = d["pt"]
                mm(psf[0:65, stt_["ob"], 0:nq], v_ap, PT[pi][0:kn, 0:nq], stt_["first"], False, kvkeys + [f"PT{pi}"], [stt_["okey"]], skip=True)
                stt_["first"] = False
            return dict(qk=qk, sm=sm, pv=pv)

        def group_item(stt_, q_ap, qkey, tiles, kvkeys, bias_grp, bkeys):
            nq = stt_["nq"]
            n = len(tiles)
            w = n * nq
            kn0 = tiles[0][2]
            assert all(t[2] == kn0 for t in tiles)
            d = {}

            def qk():
                sb_ = bank_get()
                d["sb"] = sb_
                for i, (kt_ap, v_ap, kn, mask_ap) in enumerate(tiles):
                    mm(psf[0:kn, sb_, i * nq:(i + 1) * nq], kt_ap, q_ap, True, True, kvkeys + [qkey], [("ps", sb_)])

            def sm():
                pi = nxt("pt", NPT)
                d["pt"] = pi
                ptv = PT[pi]
                stt(GS[0:kn0, 0:w].rearrange("p (g q) -> p g q", q=nq), psf[0:kn0, d["sb"], 0:w].rearrange("p (g q) -> p g q", q=nq), 0.125,
                    bias_grp[0:kn0].unsqueeze(2).to_broadcast([kn0, n, nq]), ALU.mult, ALU.add, [("ps", d["sb"])] + bkeys, ["GS"])
                bank_put(d["sb"])
                act(ptv[0:kn0, 0:w], GS[0:kn0, 0:w], AF.Exp, ["GS"], [f"PT{pi}"])
                for i, (kt_ap, v_ap, kn, mask_ap) in enumerate(tiles):
                    if mask_ap is not None:
                        tt("pool", ptv[0:kn, i * nq:(i + 1) * nq], ptv[0:kn, i * nq:(i + 1) * nq], mask_ap, ALU.mult,
                           [f"PT{pi}", "MK"], [f"PT{pi}"])

            def pv():
                pi = d["pt"]
                for i, (kt_ap, v_ap, kn, mask_ap) in enumerate(tiles):
                    mm(psf[0:65, stt_["ob"], 0:nq], v_ap, PT[pi][0:kn, i * nq:(i + 1) * nq], stt_["first"], False,
                       kvkeys + [f"PT{pi}"], [stt_["okey"]], skip=True)
                    stt_["first"] = False
            return dict(qk=qk, sm=sm, pv=pv)

        def layer1_rest(N, nb, T, ctf, ctfkey, y_dst, halo):
            nt = max(1, N // 128)
            npart = min(N, 128)
            Xt = lambda t_: X[0:npart, t_, :]
            XNt = lambda t_: XN[0:npart, t_, :]
            if halo:
                down_proj(lambda k, t_: HAH[:, k, 0:2], "HAH", 0, KC, wb_o, 1, 2, lambda t_: XH[0:2, :], "XH", ("wc", "o"))
                rms_stats(lambda t_: XH[0:2, :], "XH", 1, 2)
                rms_apply(lambda t_: XH[0:2, :], "XH", 1, 2, 4, lambda t_: XNH[0:2, :], "XNH")
                transpose_fm(lambda t_: XNH[0:2, :], "XNH", 1, 2, XTH[:].rearrange("p k t -> p (k t)"), "XTH")
                ffn_halo(1, ctf, ctfkey)
            down_proj(lambda k, t_: HA[:, k, t_ * 128:t_ * 128 + npart], "HA", 0, KC, wb_o, nt, npart, Xt, "X", ("wc", "o"),
                      after_tile=norm_cb(Xt, "X", npart, 4, XNt, "XN"))
            transpose_fm(XNt, "XN", nt, npart, XT, "XT")
            ffn_full(1, "XT", N, nb, T, ctf, ctfkey, nt, npart, Xt, after_tile=norm_cb(Xt, "X", npart, 5, Xt, "X"))
            dma(y_dst, X[0:npart, 0:nt, :], reads=xall(nt), writes=["yout"])

        with slow_dma("conv state layout (tiny)"):
            for b in range(SB_):
                for t2 in range(2):
                    dma(CTAs[:, :, b, t2], sta[b, t2].rearrange("(j p) -> p j", p=128), reads=[], writes=["CTAs"])
                    for l in range(2):
                        dma(CTFs[:, l, :, b, t2], stf[l, b, t2].rearrange("(j p) -> p j", p=128), reads=[], writes=["CTFs"])
        memset("pool", VB[:, 0:4 * 1040], 1.0, ["VB"])
        phase_a(xs.rearrange("(t p) d -> p t d", p=128), NS, SB_, ST_, CTAs, "CTAs", CTFs, "CTFs", True, 0)
        with slow_dma("conv state layout (tiny)"):
            for b in range(SB_):
                for t2 in range(2):
                    dma(sca[b, t2].rearrange("(j p) -> p j", p=128), CTAs[:, :, b, t2], reads=["CTAs"], writes=["sca"])
                    dma(sfc[0, b, t2].rearrange("(j p) -> p j", p=128), CTFs[:, 0, :, b, t2], reads=["CTFs"], writes=["sfc"])
        kts_sv = kts_s.rearrange("(f p) b t -> p f b t", p=128)
        for b in range(SB_):
            dma(kts_sv[:, :, b, PAST:PAST + ST_], KTST[:, 0:KC * NS].rearrange("p (f b t) -> p f b t", f=KC, b=SB_)[:, :, b, :],
                reads=["KTST"], writes=["kts_s"])
        for b in range(SB_):
            for t8 in range(8):
                dma(CKB[:, :], ck[b, t8 * 128:(t8 + 1) * 128, :], reads=[], writes=["CKB"], q="pool")
                transpose_fm(lambda t_: CKB, "CKB", 1, 128, XT, "XT")
                dma(kts_sv[:, :, b, t8 * 128:(t8 + 1) * 128], XT[:, 0:KC * 128].rearrange("p (f t) -> p f t", f=KC),
                    reads=["XT"], writes=["kts_s"])
        Ls, lsk = sample_state["L"]
        for b in range(SB_):
            for t8 in range(8):
                li = nxt("lg", 2)
                dma(LGt[li][:, :], clf[b, t8 * 128:(t8 + 1) * 128, :], reads=[], writes=[("LGt", li)])
                cumsum_tile(LGt[li][:, :], ("LGt", li), 128, CALLs[:, b, t8, :], "CALLs", CARRYs[:, b, t8, :], CARRYs[:, b, t8 + 1, :],
                            "CARRYs", sel127, "sel127")
            li = nxt("lg", 2)
            dma(LGt[li][0:ST_, :], Ls[b * ST_:(b + 1) * ST_, :], reads=[lsk], writes=[("LGt", li)])
            cumsum_tile(LGt[li][0:ST_, :], ("LGt", li), ST_, CALLs[0:ST_, b, 8, :], "CALLs", CARRYs[0:ST_, b, 8, :], CARRYs[:, b, 9, :],
                        "CARRYs", sel31, "sel31")
        cp("dve", QS[:, :], QTST[:, 0:KC * NS], ["QTST"], ["QS"])
        cp("dve", VBS[:, :], VB[:, 0:1040], ["VB"], ["VBS"])
        fence(KEYS_A, KEYS_B)
        for i in range(2):
            memset("dve", VR[i][:, :], 1.0, [f"VR{i}"])
        for b in range(SB_):
            cp("dve", QW[:, 0:KC * ST_].rearrange("p (f t) -> p f t", f=KC),
               QS[:, :].rearrange("p (f b t) -> p f b t", f=KC, b=SB_)[:, :, b, :], ["QS"], ["QW"])
            tt("dve", BIAS[:, 0:H * 9].rearrange("p (h t) -> p h t", h=H), CARRYs[:, b, 9, :].unsqueeze(2).to_broadcast([128, H, 9]),
               CALLs[:, b, :, :].rearrange("p t h -> p h t"), ALU.subtract, ["CALLs", "CARRYs"], ["BIAS"])
            qv = QW[:, 0:KC * ST_].rearrange("p (f t) -> p f t", f=KC)
            for hp in range(8):
                ki = nxt("kr", 2)
                vi = nxt("vr", 2)
                dma(KR[ki][:, 0:PAST + ST_], kts_sv[:, hp, b, :], reads=["kts_s"], writes=[f"KR{ki}"])
                vrv = VR[vi][:, 0:9 * 130].rearrange("p (t h e) -> p t h e", t=9, e=65)
                with slow_dma("V head split"):
                    for h2 in range(2):
                        dma(vrv[:, 0:8, h2, 0:64], cv[b].rearrange("(t p) (h e) -> p t h e", p=128, e=64)[:, :, 2 * hp + h2, :],
                            reads=[], writes=[f"VR{vi}"], q="pool")
                    dma(vrv[0:ST_, 8, :, 0:64], VBS[b * ST_:(b + 1) * ST_, :].rearrange("p (h e) -> p h e", e=65)[:, 2 * hp:2 * hp + 2, 0:64],
                        reads=["VBS"], writes=[f"VR{vi}"])
                for hl in range(2):
                    h = 2 * hp + hl
                    st_ = att_open(ST_)
                    tiles = []
                    for t9 in range(9):
                        kn = 128 if t9 < 8 else ST_
                        tiles.append((KR[ki][64 * hl:64 * hl + 64, t9 * 128:t9 * 128 + kn],
                                      VR[vi][0:kn, t9 * 130 + hl * 65:t9 * 130 + hl * 65 + 65], kn,
                                      (MKS[0:ST_, 0:ST_] if t9 == 8 else None)))
                    it1 = group_item(st_, qv[64 * hl:64 * hl + 64, hp, :], "QW", tiles[0:8], [f"KR{ki}", f"VR{vi}"],
                                     BIAS[:, h * 9:h * 9 + 8], ["BIAS"])
                    it2 = group_item(st_, qv[64 * hl:64 * hl + 64, hp, :], "QW", tiles[8:9], [f"KR{ki}", f"VR{vi}"],
                                     BIAS[:, h * 9 + 8:h * 9 + 9], ["BIAS"])
                    run_items([it1, it2])
                    att_close(st_, HA[64 * hl:64 * hl + 64, hp, b * ST_:(b + 1) * ST_], "HA")
        layer1_rest(NS, SB_, ST_, CTFs, "CTFs", ys.rearrange("(t p) d -> p t d", p=128), halo=False)
        with slow_dma("conv state layout (tiny)"):
            for b in range(SB_):
                for t2 in range(2):
                    dma(sfc[1, b, t2].rearrange("(j p) -> p j", p=128), CTFs[:, 1, :, b, t2], reads=["CTFs"], writes=["sfc"])
        fence(KEYS_B, KEYS_A)
        memset("pool", VB[:, 0:4 * 1040], 1.0, ["VB"])

        for c in range(NCH):
            phase_a(xp[c * CH:(c + 1) * CH, :].rearrange("(t p) d -> p t d", p=128), CH, 1, CH, CTA, "CTA", CTF, "CTF", False, c)
        with slow_dma("conv state layout (tiny)"):
            for t2 in range(2):
                dma(pca[t2].rearrange("(j p) -> p j", p=128), CTA[:, :, 0, t2], reads=["CTA"], writes=["pca"])
                dma(pfc0[t2].rearrange("(j p) -> p j", p=128), CTF[:, 0, :, 0, t2], reads=["CTF"], writes=["pfc0"])
        fence(KEYS_A, KEYS_B)

        ktsv = kts.rearrange("(f p) t -> p f t", p=128)
        dma(x1own.rearrange("(s o r) d -> s o r d", o=1, r=CH),
            x1s[2:SEQ + 2, :].rearrange("(s p r) d -> s p r d", p=2, r=CH)[:, bass.ds(par, 1), :, :], reads=["x1s"], writes=["x1own"])
        dma(x1hal.rearrange("s (o t) d -> s o t d", o=1),
            x1s[0:SEQ, :].rearrange("(s p r) d -> s p r d", p=2, r=CH)[:, bass.ds(par, 1), 0:2, :], reads=["x1s", "x1pad"], writes=["x1hal"])
        dma(qown[:, :, 2:CH + 2].rearrange("f s (o t) -> f s o t", o=1),
            qts[:, 2:SEQ + 2].rearrange("f (s p r) -> f s p r", p=2, r=CH)[:, :, bass.ds(par, 1), :], reads=["qts"], writes=["qown"])
        with slow_dma("tiny halo columns"):
            dma(qown[:, :, 0:2].rearrange("f s (o t) -> f s o t", o=1),
                qts[:, 0:SEQ].rearrange("f (s p r) -> f s p r", p=2, r=CH)[:, :, bass.ds(par, 1), 0:2], reads=["qts", "qtpad"], writes=["qownh"])
        qownv = qown.rearrange("(f p) s t -> p f s t", p=128)
        for s in range(NSLOT):
            nkt = 8 * (s + 1)
            nkth = 8 * s + 4
            dma(QW[:, 0:KC * 514].rearrange("p (f t) -> p f t", f=KC), qownv[:, :, s, :], reads=["qown", "qownh"], writes=["QW"])
            dma(X[:, 0:4, :], x1own[s * CH:(s + 1) * CH, :].rearrange("(t p) d -> p t d", p=128), reads=["x1own"], writes=xall(4))
            dma(XH[0:2, :], x1hal[s], reads=["x1hal"], writes=["XH"])
            qv = QW[:, 0:KC * 514].rearrange("p (f t) -> p f t", f=KC)
            jref = 8 * s + 8
            tt("dve", BIAS[:, 0:H * nkt].rearrange("p (h t) -> p h t", h=H), CARRY[:, jref, :].unsqueeze(2).to_broadcast([128, H, nkt]),
               CALL[:, 0:nkt, :].rearrange("p t h -> p h t"), ALU.subtract, ["CALL", "CARRY"], ["BIAS"])
            bv = BIAS[:, 0:H * nkt].rearrange("p (h t) -> p h t", h=H)
            for hp in range(8):
                stm = [att_open(CH) for _ in range(2)]
                sth = [att_open(2) for _ in range(2)]
                for seg in range(0, nkt, 16):
                    segn = min(16, nkt - seg)
                    ki = nxt("kr", 2)
                    vi = nxt("vr", 2)
                    dma(KR[ki][:, 0:segn * 128], ktsv[:, hp, seg * 128:(seg + segn) * 128], reads=["kts"], writes=[f"KR{ki}"])
                    dma(VR[vi][:, 0:segn * 130], vsc[hp, :, seg:seg + segn, :].rearrange("p t e -> p (t e)"), reads=["vsc"], writes=[f"VR{vi}"])
                    kvk = [f"KR{ki}", f"VR{vi}"]
                    items = []
                    for lt in range(segn):
                        t_ = seg + lt
                        j = t_ - 8 * s
                        for hl in range(2):
                            h = 2 * hp + hl
                            items.append(main_item(stm[hl], qv[64 * hl:64 * hl + 64, hp, 2:514], "QW",
                                                   KR[ki][64 * hl:64 * hl + 64, lt * 128:(lt + 1) * 128],
                                                   VR[vi][:, lt * 130 + hl * 65:lt * 130 + hl * 65 + 65], kvk,
                                                   bv[:, h, t_:t_ + 1], (MK[:, j, :] if j >= 0 else None)))
                    nh = min(segn, nkth - seg)
                    if nh > 0:
                        for hl in range(2):
                            h = 2 * hp + hl
                            tiles = []
                            for lt in range(nh):
                                j = seg + lt - 8 * s
                                tiles.append((KR[ki][64 * hl:64 * hl + 64, lt * 128:(lt + 1) * 128],
                                              VR[vi][:, lt * 130 + hl * 65:lt * 130 + hl * 65 + 65], 128,
                                              (MKH[:, j + 1, :] if j >= -1 else None)))
                            items.append(group_item(sth[hl], qv[64 * hl:64 * hl + 64, hp, 0:2], "QW", tiles, kvk,
                                                    bv[:, h, seg:seg + nh], ["BIAS"]))
                    run_items(items)
                for hl in range(2):
                    att_close(stm[hl], HA[64 * hl:64 * hl + 64, hp, :], "HA")
                    att_close(sth[hl], HAH[64 * hl:64 * hl + 64, hp, :], "HAH")
            layer1_rest(CH, 1, CH, CTF, "CTF", yp[s].rearrange("(t p) d -> p t d", p=128), halo=True)
        with slow_dma("conv state layout (tiny)"):
            for t2 in range(2):
                dma(pfc1[t2].rearrange("(j p) -> p j", p=128), CTF[:, 1, :, 0, t2], reads=["CTF"], writes=["pfc1"])
        P.emit()
    return nc


_NC_CACHE = {}


def kernel(x_prompt, x_sample, state_conv_a, state_ffn_conv, cache_k, cache_v, cache_logf,
           a_norm, w_a_in, a_conv_w, w_a_out, kv_norm, w_kv, b_f, b_norm, w_q, w_o,
           ffn_norm, w_ffn_up, ffn_conv_w, w_ffn_down, final_norm):
    f = lambda a: np.ascontiguousarray(np.asarray(a, dtype=np.float32))
    x_prompt, x_sample = f(x_prompt), f(x_sample)
    B = x_prompt.shape[0]
    assert x_prompt.shape[1] == SEQ
    n = 2 * B
    if "nc" not in _NC_CACHE:
        _NC_CACHE["nc"] = build_program()
    nc = _NC_CACHE["nc"]
    shared = dict(a_norm=f(a_norm).reshape(D), w_a_in=f(w_a_in)[0], a_conv_w=f(a_conv_w)[0], w_a_out=f(w_a_out)[0],
                  kv_norm=f(kv_norm), w_kv=f(w_kv), b_f=f(b_f), b_norm=f(b_norm).reshape(D), w_q=f(w_q)[0], w_o=f(w_o)[0],
                  ffn_norm=f(ffn_norm), w_up=f(w_ffn_up), ffn_conv_w=f(ffn_conv_w), w_dn=f(w_ffn_down), final_norm=f(final_norm))
    sca_, sfc_ = f(state_conv_a), f(state_ffn_conv)
    ck_, cv_, clf_ = f(cache_k), f(cache_v), f(cache_logf)
    in_maps = []
    for c in range(n):
        sl = slice(SB_ * c, SB_ * (c + 1))
        m = dict(shared)
        m.update(xp=x_prompt[c // 2], xs=x_sample[sl].reshape(NS, D), sta=sca_[0, sl], stf=np.ascontiguousarray(sfc_[:, sl]),
                 ck=ck_[sl].reshape(SB_, PAST, D), cv=cv_[sl].reshape(SB_, PAST, D), clf=clf_[sl])
        in_maps.append(m)
    res = run_bass_kernel_spmd(nc, in_maps, core_ids=list(range(n)))
    R = res.results
    NSLOT = SEQ // CH // 2
    DB = SB_ * n
    y_prompt = np.zeros((B, SEQ, D), np.float32)
    y_sample = np.zeros((DB, ST_, D), np.float32)
    p_conv_a = np.zeros((1, B, 2, D), np.float32)
    p_ffn_conv = np.zeros((2, B, 2, NUP), np.float32)
    p_k = np.zeros((B, SEQ, H, 64), np.float32)
    p_v = np.zeros((B, SEQ, H, 64), np.float32)
    p_logf = np.zeros((B, SEQ, H), np.float32)
    s_conv_a = np.zeros((1, DB, 2, D), np.float32)
    s_ffn_conv = np.zeros((2, DB, 2, NUP), np.float32)
    s_k = np.zeros((DB, ST_, H, 64), np.float32)
    s_v = np.zeros((DB, ST_, H, 64), np.float32)
    s_logf = np.zeros((DB, ST_, H), np.float32)
    for c in range(n):
        b, half = c // 2, c % 2
        r = R[c]
        for s in range(NSLOT):
            qb = 2 * s + half
            y_prompt[b, qb * CH:(qb + 1) * CH] = r["yp"][s]
        sl = slice(SB_ * c, SB_ * (c + 1))
        y_sample[sl] = r["ys"].reshape(SB_, ST_, D)
        if half == 0:
            p_conv_a[0, b] = r["pca"]
            p_ffn_conv[0, b] = r["pfc0"]
            p_k[b] = r["pk"].reshape(SEQ, H, 64)
            p_v[b] = r["pv"].reshape(SEQ, H, 64)
            p_logf[b] = r["plf"]
        else:
            p_ffn_conv[1, b] = r["pfc1"]
        s_conv_a[0, sl] = r["sca"]
        s_ffn_conv[:, sl] = r["sfc"]
        s_k[sl] = r["sk"].reshape(SB_, ST_, H, 64)
        s_v[sl] = r["sv"].reshape(SB_, ST_, H, 64)
        s_logf[sl] = r["slf"].reshape(SB_, ST_, H)
    return (y_prompt, y_sample, p_conv_a, p_ffn_conv, p_k, p_v, p_logf, s_conv_a, s_ffn_conv, s_k, s_v, s_logf)
```

```python
import os, sys, contextlib
import numpy as np
import concourse.bass as bass
import concourse.mybir as mybir
from concourse.bass_utils import run_bass_kernel_spmd

F32 = mybir.dt.float32
BF16 = mybir.dt.bfloat16
ALU = mybir.AluOpType
AF = mybir.ActivationFunctionType

D = 1024
NUP = 5632
DFF = 2816
H = 16
KC = 8
NF = 44
NPAIR = 22
PAST = 1024
SB_ = 4
ST_ = 32
NS = SB_ * ST_
EPS = 1e-6
SEQ = int(os.environ.get("YK_SEQ", "8192"))
CH = 512
STRICT = bool(int(os.environ.get("YK_STRICT", "0")))


class Prog:
    COMPUTE = ("pe", "act", "dve", "pool")

    def __init__(self, nc, es, n_dma_sems=24):
        self.nc = nc
        self.ops = []
        self.res = {}
        self.eng = {"pe": nc.tensor, "act": nc.scalar, "dve": nc.vector, "pool": nc.gpsimd, "sp": nc.sync}
        self.sem = {e: es.enter_context(nc.semaphore("s_" + e)) for e in self.COMPUTE}
        self.dma_sems = {q: [es.enter_context(nc.semaphore(f"d_{q}{i}")) for i in range(n_dma_sems)]
                         for q in ("sp", "pool")}
        self.eng_idx = {e: 0 for e in self.eng}

    def op(self, eng, fn, reads=(), writes=(), dma=False):
        o = dict(eng=eng, fn=fn, dma=dma, deps=[], milestone=False, idx=self.eng_idx[eng], dbg=(list(reads), list(writes)))
        self.eng_idx[eng] += 1
        deps = {}
        for r in reads:
            st = self.res.setdefault(r, dict(w=None, rs=[]))
            if st["w"] is not None:
                deps[id(st["w"])] = (st["w"], "raw")
        for w in writes:
            st = self.res.setdefault(w, dict(w=None, rs=[]))
            if st["w"] is not None and id(st["w"]) not in deps:
                deps[id(st["w"])] = (st["w"], "waw")
            for r in st["rs"]:
                if id(r) not in deps:
                    deps[id(r)] = (r, "war")
        for p, kind in deps.values():
            if p is o:
                continue
            if (not p["dma"]) and p["eng"] == eng and not dma and not STRICT:
                if eng == "pe":
                    continue
                if kind != "raw" or (o["idx"] - p["idx"]) > 3:
                    continue
            o["deps"].append(p)
            p["milestone"] = True
        for r in reads:
            rs = self.res[r]["rs"]
            if not dma and not STRICT:
                rs[:] = [x for x in rs if x["dma"] or x["eng"] != eng]
            rs.append(o)
        for w in writes:
            st = self.res[w]
            st["w"] = o
            st["rs"] = []
        self.ops.append(o)
        return o

    def emit(self, final_wait_eng="sp"):
        cnt = {e: 0 for e in self.COMPUTE}
        dcnt = {q: 0 for q in self.dma_sems}
        semuse = {}
        for o in self.ops:
            if o["dma"]:
                q = o["eng"]
                pool = self.dma_sems[q]
                s = pool[dcnt[q] % len(pool)]
                dcnt[q] += 1
                semuse[id(s)] = semuse.get(id(s), 0) + 16
                o["sig"] = (s, semuse[id(s)])
            elif o["milestone"]:
                cnt[o["eng"]] += 1
                o["sig"] = (self.sem[o["eng"]], cnt[o["eng"]])
        waited = {e: {} for e in self.eng}
        nwaits = 0
        for o in self.ops:
            e = o["eng"]
            h = self.eng[e]
            need = {}
            if o["dma"]:
                s, v = o["sig"]
                if v > 16:
                    need[id(s)] = (s, v - 16)
            for p in o["deps"]:
                s, v = p["sig"]
                if id(s) not in need or need[id(s)][1] < v:
                    need[id(s)] = (s, v)
            for k, (s, v) in need.items():
                if waited[e].get(k, 0) >= v:
                    continue
                h.wait_ge(s, v)
                nwaits += 1
                waited[e][k] = v
            try:
                ins = o["fn"]()
            except Exception:
                print("[prog] failing op:", o["eng"], o.get("dbg"), file=sys.stderr)
                raise
            if o["dma"]:
                ins.then_inc(o["sig"][0], 16)
            elif o["milestone"]:
                ins.then_inc(o["sig"][0], 1)
        h = self.eng[final_wait_eng]
        for q, pool in self.dma_sems.items():
            for s in pool:
                v = semuse.get(id(s), 0)
                if v > 0 and waited[final_wait_eng].get(id(s), 0) < v:
                    h.wait_ge(s, v)
        print(f"[prog] ops={len(self.ops)} waits={nwaits} milestones={cnt} dmas={dcnt}", file=sys.stderr)


def build_program():
    nc = bass.Bass("TRN2", target_bir_lowering=False)
    NCH = SEQ // CH
    NSLOT = NCH // 2
    NKT = SEQ // 128
    din = lambda n, s: nc.dram_tensor(n, list(s), F32, kind="ExternalInput").ap()
    dout = lambda n, s: nc.dram_tensor(n, list(s), F32, kind="ExternalOutput").ap()
    dscr = lambda n, s, dt: nc.dram_tensor(n, list(s), dt).ap()
    xp = din("xp", [SEQ, D]); xs = din("xs", [NS, D])
    sta = din("sta", [SB_, 2, D]); stf = din("stf", [2, SB_, 2, NUP])
    ck = din("ck", [SB_, PAST, D]); cv = din("cv", [SB_, PAST, D]); clf = din("clf", [SB_, PAST, H])
    a_norm = din("a_norm", [D]); w_a_in = din("w_a_in", [D, 3 * D]); a_conv_w = din("a_conv_w", [3, D])
    w_a_out = din("w_a_out", [D, D]); kv_norm = din("kv_norm", [D]); w_kv = din("w_kv", [D, 2 * D + H])
    b_f = din("b_f", [H]); b_norm = din("b_norm", [D]); w_q = din("w_q", [D, D]); w_o = din("w_o", [D, D])
    ffn_norm = din("ffn_norm", [2, D]); w_up = din("w_up", [2, D, NUP]); ffn_conv_w = din("ffn_conv_w", [2, 3, NUP])
    w_dn = din("w_dn", [2, DFF, D]); final_norm = din("final_norm", [D])
    gain_src = [a_norm, ffn_norm[0], kv_norm, b_norm, ffn_norm[1], final_norm]
    yp = dout("yp", [NSLOT, CH, D]); ys = dout("ys", [NS, D])
    pca = dout("pca", [2, D]); pfc0 = dout("pfc0", [2, NUP]); pfc1 = dout("pfc1", [2, NUP])
    pk = dout("pk", [SEQ, D]); pv = dout("pv", [SEQ, D]); plf = dout("plf", [SEQ, H])
    sca = dout("sca", [SB_, 2, D]); sfc = dout("sfc", [2, SB_, 2, NUP])
    sk = dout("sk", [NS, D]); sv = dout("sv", [NS, D]); slf = dout("slf", [NS, H])
    wb_ain = dscr("wb_ain", [KC, 128, KC, 3, 128], BF16)
    wb_up = dscr("wb_up", [2, NPAIR, 128, KC, 2, 128], BF16)
    wb_aout = dscr("wb_aout", [D, D], BF16)
    wb_kv = dscr("wb_kv", [D, 2 * D + H], BF16)
    wb_q = dscr("wb_q", [D, D], BF16)
    wb_o = dscr("wb_o", [D, D], BF16)
    wb_dn = dscr("wb_dn", [2, DFF, D], BF16)
    x1s = dscr("x1s", [SEQ + 2, D], F32)
    qts = dscr("qts", [D, SEQ + 2], BF16)
    kts = dscr("kts", [D, SEQ], BF16)
    vsc = dscr("vsc", [8, 128, NKT, 130], BF16)
    kts_s = dscr("kts_s", [D, SB_, PAST + ST_], BF16)
    halfc = dscr("halfc", [2, 128], F32)
    x1own = dscr("x1own", [NSLOT * CH, D], F32)
    x1hal = dscr("x1hal", [NSLOT, 2, D], F32)
    qown = dscr("qown", [D, NSLOT, CH + 2], BF16)

    with contextlib.ExitStack() as es:
        P = Prog(nc, es)
        sb_bytes = [0]

        def sbt(n, s, dt=F32):
            sb_bytes[0] += int(np.prod(s[1:])) * (2 if dt == BF16 else 4)
            return nc.alloc_sbuf_tensor(n, list(s), dt)
        ident = sbt("ident", [128, 128], BF16)
        identf = sbt("identf", [128, 128])
        tri = sbt("tri", [128, 128])
        sel127 = sbt("sel127", [128, 128])
        sel31 = sbt("sel31", [128, 128])
        Dm = sbt("Dm", [128, 512])
        ones_f = sbt("ones_f", [128, 64])
        HALF = sbt("HALF", [128, 1])
        DELTA = sbt("DELTA", [128, 8])
        DELTAH = sbt("DELTAH", [128, 5])
        JT = sbt("JT", [128, 8])
        epsT = sbt("epsT", [128, 1])
        MK = sbt("MK", [128, 8, 512], BF16)
        MKH = sbt("MKH", [128, 5, 2], BF16)
        MKS = sbt("MKS", [128, ST_], BF16)
        gainb = [sbt(f"gain{i}", [128, D]) for i in range(2)]
        bf_bc = sbt("bf_bc", [128, H])
        cwa = sbt("cwa", [128, KC, 3])
        cwf = sbt("cwf", [128, 2, NF, 3])
        X = sbt("X", [128, 4, D])
        XH = sbt("XH", [128, D])
        XN = sbt("XN", [128, 4, D], BF16)
        XNH = sbt("XNH", [128, D], BF16)
        XT = sbt("XT", [128, KC * CH], BF16)
        XTH = sbt("XTH", [128, KC, 2], BF16)
        HA = sbt("HA", [128, KC, CH], BF16)
        HAH = sbt("HAH", [128, KC, 2], BF16)
        HF = sbt("HF", [128, 11, CH], BF16)
        Ub = [sbt(f"U{i}", [128, CH + 2 * SB_]) for i in range(2)]
        Ab = [sbt(f"A{i}", [128, CH]) for i in range(3)]
        GCb = [sbt(f"GC{i}", [128, CH]) for i in range(2)]
        Gb = [sbt(f"G{i}", [128, CH]) for i in range(2)]
        CTA = sbt("CTA", [128, KC, SB_, 2])
        CTF = sbt("CTF", [128, 2, NF, SB_, 2])
        CTAs = sbt("CTAs", [128, KC, SB_, 2])
        CTFs = sbt("CTFs", [128, 2, NF, SB_, 2])
        ss = sbt("ss", [128, 8])
        rstd = sbt("rstd", [128, 8])
        junk = sbt("junk", [128, D])
        CALL = sbt("CALL", [128, NKT + 1, H])
        CARRY = sbt("CARRY", [128, NKT + 1, H])
        CALLs = sbt("CALLs", [128, SB_, 9, H])
        CARRYs = sbt("CARRYs", [128, SB_, 10, H])
        LG = [sbt(f"LG{i}", [128, H]) for i in range(2)]
        LGt = [sbt(f"LGt{i}", [128, H]) for i in range(2)]
        NWR = 3
        WR = sbt("WR", [128, NWR, 6144], BF16)
        fence_t = sbt("fence_t", [128, 2])
        QS = sbt("QS", [128, KC * NS], BF16)
        VBS = sbt("VBS", [128, 1040], BF16)
        SCR_BYTES = 46080
        SCR = sbt("SCR", [128, SCR_BYTES // 4])
        scr_b = SCR[:].bitcast(BF16)

        def scr_alloc(cur, nelem, dt):
            sz = 2 if dt == BF16 else 4
            off = (cur[0] + 63) // 64 * 64
            cur[0] = off + nelem * sz
            assert cur[0] <= SCR_BYTES, (cur[0], SCR_BYTES)
            return scr_b[:, off // 2: off // 2 + nelem] if dt == BF16 else SCR[:, off // 4: off // 4 + nelem]
        ca = [0]
        KVST = scr_alloc(ca, 2 * 2048, F32)
        VB = scr_alloc(ca, 4 * 1040, BF16)
        KTST = scr_alloc(ca, KC * CH, BF16)
        QTST = scr_alloc(ca, KC * CH, BF16)
        CKB = scr_alloc(ca, 1024, BF16)
        KEYS_A = ["VB", "KTST", "QTST", "CKB"] + [("KVST", p_, c_) for p_ in range(2) for c_ in range(4)]
        cb_ = [0]
        QW = scr_alloc(cb_, KC * 514, BF16)
        KR = [scr_alloc(cb_, 2048, BF16) for _ in range(2)]
        VR = [scr_alloc(cb_, 16 * 130, BF16) for _ in range(2)]
        NPT = 6
        PT = [scr_alloc(cb_, 512, BF16) for _ in range(NPT)]
        OS = scr_alloc(cb_, 512, F32)
        RB = scr_alloc(cb_, 512, F32)
        GS = scr_alloc(cb_, 512, F32)
        BIAS = scr_alloc(cb_, H * 64, F32)
        KEYS_B = ["QW", "KR0", "KR1", "VR0", "VR1"] + [f"PT{i}" for i in range(NPT)] + ["OS", "RBrow", "GS", "BIAS"]
        print(f"[sbuf] {sb_bytes[0] / 1024:.1f} KiB/partition, scrA={ca[0]} scrB={cb_[0]}", file=sys.stderr)
        psf = es.enter_context(nc.psum_tensor("psf", [128, 8, 512], F32))
        free_banks = list(range(8))
        rr = {}

        def bank_get():
            return free_banks.pop(0)

        def bank_put(b):
            free_banks.append(b)

        def nxt(k, n):
            v = rr.get(k, 0)
            rr[k] = (v + 1) % n
            return v

        def E(eng):
            return P.eng[eng]

        def cp(eng, out, in_, reads, writes):
            if eng == "act":
                P.op("act", lambda: nc.scalar.copy(out=out, in_=in_), reads, writes)
            else:
                P.op(eng, lambda: E(eng).tensor_copy(out=out, in_=in_), reads, writes)

        def act(out, in_, func, reads, writes, bias=None, scale=None, accum=None):
            kw = {}
            if bias is not None:
                kw["bias"] = bias
            if scale is not None:
                kw["scale"] = scale
            if accum is not None:
                kw["accum_out"] = accum
            P.op("act", lambda: nc.scalar.activation(out=out, in_=in_, func=func, **kw), reads, writes)

        def tt(eng, out, in0, in1, op, reads, writes):
            P.op(eng, lambda: E(eng).tensor_tensor(out=out, in0=in0, in1=in1, op=op), reads, writes)

        def ts(eng, out, in0, s1, s2, op0, op1, reads, writes):
            if op1 is None:
                P.op(eng, lambda: E(eng).tensor_scalar(out=out, in0=in0, scalar1=s1, scalar2=None, op0=op0), reads, writes)
            else:
                P.op(eng, lambda: E(eng).tensor_scalar(out=out, in0=in0, scalar1=s1, scalar2=s2, op0=op0, op1=op1), reads, writes)

        def stt(out, in0, scalar, in1, op0, op1, reads, writes):
            P.op("dve", lambda: nc.vector.scalar_tensor_tensor(out=out, in0=in0, scalar=scalar, in1=in1, op0=op0, op1=op1),
                 reads, writes)

        def mm(out, lhsT, rhs, start, stop, reads, writes, skip=False):
            if skip:
                P.op("pe", lambda: nc.tensor.matmul(out, lhsT=lhsT, rhs=rhs, start=start, stop=stop, skip_group_check=True), reads, writes)
            else:
                P.op("pe", lambda: nc.tensor.matmul(out, lhsT=lhsT, rhs=rhs, start=start, stop=stop), reads, writes)

        def tr(out, in_, idn, reads, writes):
            P.op("pe", lambda: nc.tensor.transpose(out=out, in_=in_, identity=idn), reads, writes)

        slow_flag = [False]

        @contextlib.contextmanager
        def slow_dma(reason=""):
            old = slow_flag[0]
            slow_flag[0] = True
            try:
                yield
            finally:
                slow_flag[0] = old

        def dma(out, in_, reads, writes, q="sp"):
            slow = slow_flag[0]
            if slow:
                P.op(q, lambda: E(q).dma_start(out=out, in_=in_, allow_slow_non_contiguous=True), reads, writes, dma=True)
            else:
                P.op(q, lambda: E(q).dma_start(out=out, in_=in_), reads, writes, dma=True)

        def memset(eng, ap, val, writes):
            P.op(eng, lambda: E(eng).memset(ap, val), [], writes)

        def fence(from_keys, to_keys):
            P.op("pool", lambda: nc.gpsimd.memset(fence_t[:], 0.0), reads=list(from_keys),
                 writes=list(from_keys) + list(to_keys) + ["fence_t"])

        def wload(src_ap, shape, rkey="wcast"):
            s = nxt("wr", NWR)
            n = int(np.prod(shape[1:]))
            dst = WR[:, s, 0:n]
            if len(shape) == 3:
                dstv = dst.rearrange("p (a b) -> p a b", a=shape[1])
            elif len(shape) == 4:
                dstv = dst.rearrange("p (a b c) -> p a b c", a=shape[1], b=shape[2])
            else:
                dstv = dst
            key = ("WR", s)
            dma(dstv, src_ap, reads=(list(rkey) if isinstance(rkey, list) else [rkey]), writes=[key])
            return dstv, key

        def gload(gi):
            s = nxt("gain", 2)
            dma(gainb[s][:], gain_src[gi].partition_broadcast(128), reads=[], writes=[("gain", s)])
            return gainb[s], ("gain", s)

        memset("pool", identf[:], 1.0, ["identf"])
        P.op("pool", lambda: nc.gpsimd.affine_select(out=identf[:], in_=identf[:], pattern=[[-1, 128]],
                                                     compare_op=ALU.is_equal, fill=0.0, base=0, channel_multiplier=1),
             reads=["identf"], writes=["identf"])
        cp("dve", ident[:], identf[:], ["identf"], ["ident"])
        memset("pool", tri[:], 1.0, ["tri"])
        P.op("pool", lambda: nc.gpsimd.affine_select(out=tri[:], in_=tri[:], pattern=[[1, 128]],
                                                     compare_op=ALU.is_ge, fill=0.0, base=0, channel_multiplier=-1),
             reads=["tri"], writes=["tri"])
        memset("pool", sel127[:], 1.0, ["sel127"])
        P.op("pool", lambda: nc.gpsimd.affine_select(out=sel127[:], in_=sel127[:], pattern=[[0, 128]], compare_op=ALU.is_equal,
                                                     fill=0.0, base=-127, channel_multiplier=1), reads=["sel127"], writes=["sel127"])
        memset("pool", sel31[:], 1.0, ["sel31"])
        P.op("pool", lambda: nc.gpsimd.affine_select(out=sel31[:], in_=sel31[:], pattern=[[0, 128]], compare_op=ALU.is_equal,
                                                     fill=0.0, base=-31, channel_multiplier=1), reads=["sel31"], writes=["sel31"])
        P.op("pool", lambda: nc.gpsimd.iota(Dm[:], pattern=[[-1, 512]], base=0, channel_multiplier=1,
                                            allow_small_or_imprecise_dtypes=True), writes=["Dm"])
        P.op("pool", lambda: nc.gpsimd.iota(JT[:], pattern=[[-128, 8]], base=0, channel_multiplier=0,
                                            allow_small_or_imprecise_dtypes=True), writes=["JT"])
        memset("dve", ones_f[:], 1.0, ["ones_f"])
        memset("dve", junk[:], 0.0, ["junk"])
        memset("dve", epsT[:], EPS, ["epsT"])
        memset("dve", CTA[:], 0.0, ["CTA"])
        memset("pool", CTF[:], 0.0, ["CTF"])
        memset("dve", CARRY[:, 0, :], 0.0, ["CARRY"])
        memset("dve", CARRYs[:], 0.0, ["CARRYs"])
        memset("dve", CALLs[:], 0.0, ["CALLs"])
        memset("dve", XH[:], 0.0, ["XH"])
        memset("dve", XNH[:], 0.0, ["XNH"])
        dma(halfc[0:1, :], junk[0:1, 0:128], reads=["junk"], writes=["halfc0"])
        dma(halfc[1:2, 0:64], ones_f[0:1, 0:64], reads=["ones_f"], writes=["halfc1"])
        dma(halfc[1:2, 64:128], ones_f[0:1, 0:64], reads=["ones_f"], writes=["halfc2"])
        pid = nc.sync.partition_id()
        par = pid % 2
        with slow_dma("tiny"):
            dma(HALF[:], halfc[bass.ds(par, 1), :].rearrange("a p -> p a"), reads=["halfc0", "halfc1", "halfc2"], writes=["HALF"])
        ts("dve", HALF[:], HALF[:], 512.0, None, ALU.mult, None, ["HALF"], ["H512"])
        ts("dve", DELTA[:], JT[:], HALF[:, 0:1], None, ALU.add, None, ["H512", "JT"], ["DELTA"])
        ts("dve", DELTAH[:], JT[:, 0:5], HALF[:, 0:1], 126.0, ALU.add, ALU.add, ["H512", "JT"], ["DELTAH"])
        for j in range(8):
            ts("pool" if j % 2 else "dve", MK[:, j, :], Dm[:], DELTA[:, j:j + 1], None, ALU.is_le, None, ["Dm", "DELTA"], ["MK"])
        for i in range(5):
            ts("dve", MKH[:, i, :], Dm[:, 0:2], DELTAH[:, i:i + 1], None, ALU.is_le, None, ["Dm", "DELTAH"], ["MKH"])
        ts("dve", MKS[:], Dm[:, 0:ST_], 0.0, None, ALU.is_le, None, ["Dm"], ["MKS"])
        dma(x1s[0:2, :], junk[0:2, :], reads=["junk"], writes=["x1pad"])
        zb = junk[:].bitcast(BF16)
        with slow_dma("tiny pad"):
            dma(qts[:, 0:2].rearrange("(f p) t -> p f t", p=128), zb[:, 0:16].rearrange("p (f t) -> p f t", t=2),
                reads=["junk"], writes=["qtpad"])
        dma(bf_bc[:], b_f.partition_broadcast(128), reads=[], writes=["bf_bc"])
        with slow_dma("tiny conv weight layout"):
            for w_ in range(3):
                dma(cwa[:, :, w_], a_conv_w[w_].rearrange("(j p) -> p j", p=128), reads=[], writes=["cw"])
                for l in range(2):
                    dma(cwf[:, l, :, w_], ffn_conv_w[l, w_].rearrange("(j p) -> p j", p=128), reads=[], writes=["cw"])
        for j in range(KC):
            for g in range(3):
                dma(wb_ain[j][:, :, g, :], w_a_in.rearrange("(kc p) (g f) -> p kc g f", p=128, g=3)[:, :, g, j * 128:(j + 1) * 128],
                    reads=[], writes=[("wc", "ain", j, g)], q="pool")
        dma(wb_aout, w_a_out, reads=[], writes=[("wc", "aout")], q="pool")
        for pr in range(NPAIR):
            for w_ in range(2):
                dma(wb_up[0, pr][:, :, w_, :], w_up[0].rearrange("(kc p) (w f) -> p kc w f", p=128, w=2)[:, :, w_, pr * 128:(pr + 1) * 128],
                    reads=[], writes=[("wc", "up", 0, pr, w_)], q="pool")
        dma(wb_dn[0], w_dn[0], reads=[], writes=[("wc", "dn", 0)], q="pool")
        dma(wb_kv, w_kv, reads=[], writes=[("wc", "kv")], q="pool")
        dma(wb_q, w_q, reads=[], writes=[("wc", "q")], q="pool")
        dma(wb_o, w_o, reads=[], writes=[("wc", "o")], q="pool")
        for pr in range(NPAIR):
            for w_ in range(2):
                dma(wb_up[1, pr][:, :, w_, :], w_up[1].rearrange("(kc p) (w f) -> p kc w f", p=128, w=2)[:, :, w_, pr * 128:(pr + 1) * 128],
                    reads=[], writes=[("wc", "up", 1, pr, w_)], q="pool")
        dma(wb_dn[1], w_dn[1], reads=[], writes=[("wc", "dn", 1)], q="pool")

        def mm_group(out_ap, pairs, bank_key):
            n = len(pairs)
            for i, (l_ap, r_ap, rds) in enumerate(pairs):
                mm(out_ap, l_ap, r_ap, i == 0, i == n - 1, rds, [bank_key])

        def xk(xkey, t_):
            return (xkey, t_) if xkey == "X" else xkey

        def rms_stats_tile(Xt, xkey, t_, npart):
            act(junk[0:npart, :], Xt(t_), AF.Square, [xk(xkey, t_)], ["junk", ("ss", t_)], accum=ss[0:npart, t_:t_ + 1])
            act(rstd[0:npart, t_:t_ + 1], ss[0:npart, t_:t_ + 1], AF.Sqrt, [("ss", t_), "epsT"], [("rstd", t_)],
                bias=epsT[0:npart, 0:1], scale=1.0 / D)
            P.op("dve", lambda: nc.vector.reciprocal(out=rstd[0:npart, t_:t_ + 1], in_=rstd[0:npart, t_:t_ + 1]),
                 reads=[("rstd", t_)], writes=[("rstd", t_)])

        def rms_apply_tile(Xt, xkey, t_, npart, g, gkey, ofn, okey):
            wk = xk(okey, t_)
            stt(ofn(t_), Xt(t_), rstd[0:npart, t_:t_ + 1], g[0:npart, :], ALU.mult, ALU.mult,
                [xk(xkey, t_), ("rstd", t_), gkey], [wk])

        def rms_stats(Xt, xkey, nt, npart):
            for t_ in range(nt):
                rms_stats_tile(Xt, xkey, t_, npart)

        def rms_apply(Xt, xkey, nt, npart, gi, ofn, okey, extra_reads=(), pre=None):
            g, gkey = pre if pre is not None else gload(gi)
            for t_ in range(nt):
                rms_apply_tile(Xt, xkey, t_, npart, g, gkey, ofn, okey)

        def norm_cb(Xt, xkey, npart, gi, ofn, okey):
            g, gkey = gload(gi)

            def cb(t_):
                rms_stats_tile(Xt, xkey, t_, npart)
                rms_apply_tile(Xt, xkey, t_, npart, g, gkey, ofn, okey)
            return cb

        def xall(nt):
            return [("X", t_) for t_ in range(nt)]

        def transpose_fm(src_fn, skey, nt, npart, dst_flat, dkey):
            blocks = [(kc, t_) for kc in range(KC) for t_ in range(nt)]
            for g0 in range(0, len(blocks), 4):
                grp = blocks[g0:g0 + 4]
                tb = bank_get()
                pbv = psf[:, tb, :].bitcast(BF16)
                for i, (kc, t_) in enumerate(grp):
                    tr(pbv[:, i * npart:(i + 1) * npart], src_fn(t_)[:, kc * 128:(kc + 1) * 128], ident[0:npart, 0:npart],
                       [skey, "ident"], [("ps", tb)])
                w = len(grp) * npart
                cp(("act", "dve")[nxt("ev", 2)], dst_flat[:, g0 * npart:g0 * npart + w], pbv[:, 0:w], [("ps", tb)], [dkey])
                bank_put(tb)

        def conv3(ps_ap3, pskey, pre_fn, cw_ap, ct_ap, ctkey, nb, T):
            ui = nxt("u", 2)
            U = Ub[ui][:, 0:nb * (T + 2)].rearrange("p (b t) -> p b t", b=nb)
            ukey = ("U", ui)
            ai = nxt("a", 3)
            A = Ab[ai][:, 0:nb * T].rearrange("p (b t) -> p b t", b=nb)
            akey = ("A", ai)
            if pre_fn is None:
                cp("act", U[:, :, 2:2 + T], ps_ap3, [pskey], [ukey])
                act(A, ps_ap3, AF.Copy, [pskey, "cw"], [akey], scale=cw_ap[:, 2:3])
            else:
                pre_fn(U[:, :, 2:2 + T], ukey)
                act(A, U[:, :, 2:2 + T], AF.Copy, [ukey, "cw"], [akey], scale=cw_ap[:, 2:3])
            cp("pool", U[:, :, 0:2], ct_ap, [ctkey], [ukey])
            cp("pool", ct_ap, U[:, :, T:T + 2], [ukey], [ctkey])
            for k in (1, 0):
                stt(A, U[:, :, k:k + T], cw_ap[:, k:k + 1], A, ALU.mult, ALU.add, [ukey, akey, "cw"], [akey])
            return A, akey

        def ffn_halo(l, CT, ctkey):
            hb = bank_get()
            hkey = ("ps", hb)
            for pr in range(NPAIR):
                wt, wkey = wload(wb_up[l, pr], [128, KC, 2, 128], [("wc", "up", l, pr, 0), ("wc", "up", l, pr, 1)])
                for wh in range(2):
                    j = wh * NPAIR + pr
                    mm_group(psf[:, hb, 2 * j:2 * j + 2], [(wt[:, kc, wh, :], XTH[:, kc, :], [wkey, "XTH"]) for kc in range(KC)], hkey)
            cp("act", CT[:, l, :, 0, :], psf[:, hb, 0:2 * NF].rearrange("p (j t) -> p j t", t=2), [hkey], [ctkey])
            bank_put(hb)

        def w_rows(wb, k0, kn, hf):
            return wb.rearrange("(k p) n -> p k n", p=128)[:, k0:k0 + kn, hf * 512:(hf + 1) * 512]

        def down_proj(lhs_fn, lkey, k_base, nk, wb, nt, npart, Xt, xkey, rkey, after_tile=None):
            ws = [wload(w_rows(wb, k_base, nk, hf), [128, nk, 512], rkey) for hf in range(2)]
            for t_ in range(nt):
                for hf in range(2):
                    wt, wkey = ws[hf]
                    b_ = bank_get()
                    bkey = ("ps", b_)
                    mm_group(psf[0:npart, b_, :], [(lhs_fn(k, t_), wt[:, k, :], [lkey, wkey]) for k in range(nk)], bkey)
                    tt("dve", Xt(t_)[:, hf * 512:(hf + 1) * 512], psf[0:npart, b_, :], Xt(t_)[:, hf * 512:(hf + 1) * 512], ALU.add,
                       [bkey, xk(xkey, t_)], [xk(xkey, t_)])
                    bank_put(b_)
                if after_tile is not None:
                    after_tile(t_)

        def ffn_full(l, xtkey, N, nb, T, CT, ctkey, nt, npart, Xt, after_tile=None):
            for grp in range(2):
                for pl in range(11):
                    pr = grp * 11 + pl
                    wt, wkey = wload(wb_up[l, pr], [128, KC, 2, 128], [("wc", "up", l, pr, 0), ("wc", "up", l, pr, 1)])
                    As = []
                    for wh in range(2):
                        j = wh * NPAIR + pr
                        b_ = bank_get()
                        bkey = ("ps", b_)
                        mm_group(psf[:, b_, 0:N], [(wt[:, kc, wh, :], XT[:, kc * N:(kc + 1) * N], [wkey, xtkey]) for kc in range(KC)], bkey)
                        A, akey = conv3(psf[:, b_, 0:N].rearrange("p (b t) -> p b t", b=nb), bkey, None, cwf[:, l, j, :],
                                        CT[:, l, j, 0:nb, :], ctkey, nb, T)
                        bank_put(b_)
                        As.append((A, akey))
                    gi = nxt("g", 2)
                    G = Gb[gi][:, 0:N].rearrange("p (b t) -> p b t", b=nb)
                    act(G, As[0][0], AF.Silu, [As[0][1]], [("G", gi)])
                    tt("pool", HF[:, pl, 0:N].rearrange("p (b t) -> p b t", b=nb), G, As[1][0], ALU.mult, [("G", gi), As[1][1]], ["HF"])
                down_proj(lambda k, t_: HF[:, k, t_ * 128:t_ * 128 + npart], "HF", grp * 11, 11, wb_dn[l], nt, npart, Xt, "X", ("wc", "dn", l),
                          after_tile=(after_tile if grp == 1 else None))

        def logsig(ps_lf, pkey, npart, li):
            L, Lt = LG[li], LGt[li]
            lk, ltk = ("LG", li), ("LGt", li)
            tt("dve", L[0:npart], ps_lf, bf_bc[0:npart], ALU.add, [pkey, "bf_bc"], [lk])
            ts("dve", Lt[0:npart], L[0:npart], -1.0, None, ALU.mult, None, [lk], [ltk])
            tt("dve", Lt[0:npart], Lt[0:npart], L[0:npart], ALU.min, [lk, ltk], [ltk])
            act(Lt[0:npart], Lt[0:npart], AF.Exp, [ltk], [ltk])
            act(Lt[0:npart], Lt[0:npart], AF.Ln, [ltk], [ltk], bias=1.0, scale=1.0)
            ts("dve", L[0:npart], L[0:npart], 0.0, None, ALU.min, None, [lk], [lk])
            tt("dve", L[0:npart], L[0:npart], Lt[0:npart], ALU.subtract, [lk, ltk], [lk])
            return L, lk

        def cumsum_tile(L_ap, lk, npart, call_ap, ckey, carry_in_ap, carry_out_ap, cakey, sel, selkey):
            b_ = bank_get()
            bkey = ("ps", b_)
            mm(psf[0:npart, b_, 0:H], tri[0:npart, 0:npart], L_ap, True, True, ["tri", lk], [bkey])
            tt("dve", call_ap, psf[0:npart, b_, 0:H], carry_in_ap, ALU.add, [bkey, cakey], [ckey])
            mm(psf[:, b_, 256:256 + H], sel[0:npart, :], call_ap, True, True, [selkey, ckey], [bkey])
            cp("act", carry_out_ap, psf[:, b_, 256:256 + H], [bkey], [cakey])
            bank_put(b_)

        def phase_a(x_src, N, nb, T, cta, ctakey, ctf, ctfkey, sample, chunk_idx):
            nt = max(1, N // 128)
            npart = min(N, 128)
            Xt = lambda t_: X[0:npart, t_, :]
            XNt = lambda t_: XN[0:npart, t_, :]
            dma(X[0:npart, 0:nt, :], x_src, reads=[], writes=xall(nt))
            rms_stats(Xt, "X", nt, npart)
            rms_apply(Xt, "X", nt, npart, 0, XNt, "XN")
            transpose_fm(XNt, "XN", nt, npart, XT, "XT")
            for j in range(KC):
                wt, wkey = wload(wb_ain[j], [128, KC, 3, 128], [("wc", "ain", j, g_) for g_ in range(3)])
                bs = [bank_get() for _ in range(3)]
                for g in range(3):
                    mm_group(psf[:, bs[g], 0:N], [(wt[:, kc, g, :], XT[:, kc * N:(kc + 1) * N], [wkey, "XT"]) for kc in range(KC)],
                             ("ps", bs[g]))
                gci = nxt("gc", 2)
                GC = GCb[gci][:, 0:N].rearrange("p (b t) -> p b t", b=nb)
                cp("act", GC, psf[:, bs[1], 0:N].rearrange("p (b t) -> p b t", b=nb), [("ps", bs[1])], [("GC", gci)])

                def pre(Uv, ukey, b2=bs[2], GC=GC, gci=gci):
                    tt("dve", Uv, psf[:, b2, 0:N].rearrange("p (b t) -> p b t", b=nb), GC, ALU.mult, [("ps", b2), ("GC", gci)], [ukey])
                A, akey = conv3(None, None, pre, cwa[:, j, :], cta[:, j, 0:nb, :], ctakey, nb, T)
                tt("dve", HA[:, j, 0:N].rearrange("p (b t) -> p b t", b=nb), psf[:, bs[0], 0:N].rearrange("p (b t) -> p b t", b=nb), A,
                   ALU.mult, [("ps", bs[0]), akey], ["HA"])
                for b_ in bs:
                    bank_put(b_)
            down_proj(lambda k, t_: HA[:, k, t_ * 128:t_ * 128 + npart], "HA", 0, KC, wb_aout, nt, npart, Xt, "X", ("wc", "aout"),
                      after_tile=norm_cb(Xt, "X", npart, 1, XNt, "XN"))
            transpose_fm(XNt, "XN", nt, npart, XT, "XT")
            ffn_full(0, "XT", N, nb, T, ctf, ctfkey, nt, npart, Xt, after_tile=norm_cb(Xt, "X", npart, 2, XNt, "XN"))
            gpre_b = gload(3)
            tok0 = chunk_idx * CH
            if not sample:
                for t_ in range(nt):
                    dma(x1s[2 + tok0 + t_ * 128:2 + tok0 + (t_ + 1) * 128, :], X[:, t_, :], reads=[("X", t_)], writes=["x1s"], q="pool")
            transpose_fm(XNt, "XN", nt, npart, XT, "XT")
            wkv = wb_kv.rearrange("(k p) n -> p k n", p=128)
            for cb in range(4):
                wt, wkey = wload(wkv[:, :, cb * 512:(cb + 1) * 512], [128, KC, 512], ("wc", "kv"))
                for t_ in range(nt):
                    b_ = bank_get()
                    mm_group(psf[0:npart, b_, :], [(XT[:, kc * N + t_ * 128:kc * N + t_ * 128 + npart], wt[:, kc, :], ["XT", wkey])
                                                   for kc in range(KC)], ("ps", b_))
                    kvo = (t_ % 2) * 2048 + cb * 512
                    kvkey = ("KVST", t_ % 2, cb)
                    cp("act", KVST[0:npart, kvo:kvo + 512], psf[0:npart, b_, :], [("ps", b_)], [kvkey])
                    if cb >= 2:
                        h0 = (cb - 2) * 8
                        vbv = VB[0:npart, t_ * 1040:(t_ + 1) * 1040].rearrange("p (h e) -> p h e", e=65)
                        cp("dve", vbv[:, h0:h0 + 8, 0:64], KVST[0:npart, kvo:kvo + 512].rearrange("p (h e) -> p h e", e=64), [kvkey], ["VB"])
                    bank_put(b_)
                    dst = ((sk, sv) if sample else (pk, pv))[cb // 2]
                    r0 = 0 if sample else tok0 + t_ * 128
                    dma(dst[r0:r0 + npart, (cb % 2) * 512:(cb % 2 + 1) * 512], KVST[0:npart, kvo:kvo + 512],
                        reads=[kvkey], writes=["kvout"], q="pool")
                if cb < 2:
                    for f4 in range(4):
                        f = cb * 4 + f4
                        b_ = bank_get()
                        mm_group(psf[:, b_, 0:N], [(wt[:, kc, f4 * 128:(f4 + 1) * 128], XT[:, kc * N:(kc + 1) * N], [wkey, "XT"])
                                                   for kc in range(KC)], ("ps", b_))
                        cp(("act", "dve")[nxt("ev", 2)], KTST[:, f * N:(f + 1) * N], psf[:, b_, 0:N], [("ps", b_)], ["KTST"])
                        bank_put(b_)
            wtl, wlkey = wload(wkv[:, :, 2048:2064], [128, KC, H], ("wc", "kv"))
            for t_ in range(nt):
                b_ = bank_get()
                mm_group(psf[0:npart, b_, 0:H], [(XT[:, kc * N + t_ * 128:kc * N + t_ * 128 + npart], wtl[:, kc, :], ["XT", wlkey])
                                                 for kc in range(KC)], ("ps", b_))
                li = nxt("lg", 2)
                L, lk = logsig(psf[0:npart, b_, 0:H], ("ps", b_), npart, li)
                bank_put(b_)
                if not sample:
                    j = chunk_idx * 4 + t_
                    dma(plf[tok0 + t_ * 128:tok0 + (t_ + 1) * 128, :], L[:, :], reads=[lk], writes=["lfout"], q="pool")
                    cumsum_tile(L[:, :], lk, 128, CALL[:, j, :], "CALL", CARRY[:, j, :], CARRY[:, j + 1, :], "CARRY", sel127, "sel127")
                else:
                    dma(slf[:, :], L[0:npart, :], reads=[lk], writes=["lfout"])
                    sample_state["L"] = (L, lk)
            if not sample:
                with slow_dma("V head split"):
                    for hp in range(8):
                        dma(vsc[hp, :, chunk_idx * 4:chunk_idx * 4 + 4, :],
                            VB[:, 0:4 * 1040].rearrange("p (t x) -> p t x", t=4)[:, :, 2 * hp * 65:2 * hp * 65 + 130],
                            reads=["VB"], writes=["vsc"], q="pool")
                dma(kts.rearrange("(f p) t -> p f t", p=128)[:, :, tok0:tok0 + CH],
                    KTST[:, 0:KC * N].rearrange("p (f t) -> p f t", f=KC), reads=["KTST"], writes=["kts"], q="pool")
            rms_apply(Xt, "X", nt, npart, 3, XNt, "XN", pre=gpre_b)
            transpose_fm(XNt, "XN", nt, npart, XT, "XT")
            wqv = wb_q.rearrange("(k p) n -> p k n", p=128)
            for cb in range(2):
                wt, wkey = wload(wqv[:, :, cb * 512:(cb + 1) * 512], [128, KC, 512], ("wc", "q"))
                for f4 in range(4):
                    f = cb * 4 + f4
                    b_ = bank_get()
                    mm_group(psf[:, b_, 0:N], [(wt[:, kc, f4 * 128:(f4 + 1) * 128], XT[:, kc * N:(kc + 1) * N], [wkey, "XT"])
                                               for kc in range(KC)], ("ps", b_))
                    cp(("act", "dve")[nxt("ev", 2)], QTST[:, f * N:(f + 1) * N], psf[:, b_, 0:N], [("ps", b_)], ["QTST"])
                    bank_put(b_)
            if not sample:
                dma(qts.rearrange("(f p) t -> p f t", p=128)[:, :, 2 + tok0:2 + tok0 + CH],
                    QTST[:, 0:KC * N].rearrange("p (f t) -> p f t", f=KC), reads=["QTST"], writes=["qts"], q="pool")

        sample_state = {}

        def att_open(nq):
            ob = bank_get()
            return dict(ob=ob, okey=("ps", ob), first=True, nq=nq)

        def att_close(stt_, out_ap, okey_out):
            nq, ob, okb = stt_["nq"], stt_["ob"], stt_["okey"]
            ts("dve", RB[64:65, 0:nq], psf[64:65, ob, 0:nq], 1e-30, None, ALU.add, None, [okb], ["RBrow"])
            P.op("dve", lambda: nc.vector.reciprocal(out=RB[64:65, 0:nq], in_=RB[64:65, 0:nq]), reads=["RBrow"], writes=["RBrow"])
            bb = bank_get()
            mm(psf[0:64, bb, 0:nq], ones_f[64:65, 0:64], RB[64:65, 0:nq], True, True, ["RBrow", "ones_f"], [("ps", bb)])
            cp("dve", OS[0:64, 0:nq], psf[0:64, ob, 0:nq], [okb], ["OS"])
            tt("dve", out_ap, OS[0:64, 0:nq], psf[0:64, bb, 0:nq], ALU.mult, ["OS", ("ps", bb)], [okey_out])
            bank_put(ob)
            bank_put(bb)

        def run_items(items, lag=1, grp=2):
            groups = [items[i:i + grp] for i in range(0, len(items), grp)]
            for gi, g in enumerate(groups):
                for it in g:
                    it["qk"]()
                for it in g:
                    it["sm"]()
                if gi >= lag:
                    for it in groups[gi - lag]:
                        it["pv"]()
            for g in groups[max(0, len(groups) - lag):]:
                for it in g:
                    it["pv"]()

        def main_item(stt_, q_ap, qkey, kt_ap, v_ap, kvkeys, bias_ap, mask_ap, kn=128):
            nq = stt_["nq"]
            d = {}

            def qk():
                sb_ = bank_get()
                d["sb"] = sb_
                mm(psf[0:kn, sb_, 0:nq], kt_ap, q_ap, True, True, kvkeys + [qkey], [("ps", sb_)])

            def sm():
                pi = nxt("pt", NPT)
                d["pt"] = pi
                ptv = PT[pi]
                act(ptv[0:kn, 0:nq], psf[0:kn, d["sb"], 0:nq], AF.Exp, [("ps", d["sb"]), "BIAS"], [f"PT{pi}"], bias=bias_ap, scale=0.125)
                bank_put(d["sb"])
                if mask_ap is not None:
                    tt(("pool", "dve")[nxt("mk", 2)], ptv[0:kn, 0:nq], ptv[0:kn, 0:nq], mask_ap, ALU.mult, [f"PT{pi}", "MK"], [f"PT{pi}"])

            def pv():
                pi = d["pt"]
                mm(psf[0:65, stt_["ob"], 0:nq], v_ap, PT[pi][0:kn, 0:nq], stt_["first"], False, kvkeys + [f"PT{pi}"], [stt_["okey"]], skip=True)
                stt_["first"] = False
            return dict(qk=qk, sm=sm, pv=pv)

        def group_item(stt_, q_ap, qkey, tiles, kvkeys, bias_grp, bkeys):
            nq = stt_["nq"]
            n = len(tiles)
            w = n * nq
            kn0 = tiles[0][2]
            assert all(t[2] == kn0 for t in tiles)
            d = {}

            def qk():
                sb_ = bank_get()
                d["sb"] = sb_
                for i, (kt_ap, v_ap, kn, mask_ap) in enumerate(tiles):
                    mm(psf[0:kn, sb_, i * nq:(i + 1) * nq], kt_ap, q_ap, True, True, kvkeys + [qkey], [("ps", sb_)])

            def sm():
                pi = nxt("pt", NPT)
                d["pt"] = pi
                ptv = PT[pi]
                stt(GS[0:kn0, 0:w].rearrange("p (g q) -> p g q", q=nq), psf[0:kn0, d["sb"], 0:w].rearrange("p (g q) -> p g q", q=nq), 0.125,
                    bias_grp[0:kn0].unsqueeze(2).to_broadcast([kn0, n, nq]), ALU.mult, ALU.add, [("ps", d["sb"])] + bkeys, ["GS"])
                bank_put(d["sb"])
                act(ptv[0:kn0, 0:w], GS[0:kn0, 0:w], AF.Exp, ["GS"], [f"PT{pi}"])
                for i, (kt_ap, v_ap, kn, mask_ap) in enumerate(tiles):
                    if mask_ap is not None:
                        tt("pool", ptv[0:kn, i * nq:(i + 1) * nq], ptv[0:kn, i * nq:(i + 1) * nq], mask_ap, ALU.mult,
                           [f"PT{pi}", "MK"], [f"PT{pi}"])

            def pv():
                pi = d["pt"]
                for i, (kt_ap, v_ap, kn, mask_ap) in enumerate(tiles):
                    mm(psf[0:65, stt_["ob"], 0:nq], v_ap, PT[pi][0:kn, i * nq:(i + 1) * nq], stt_["first"], False,
                       kvkeys + [f"PT{pi}"], [stt_["okey"]], skip=True)
                    stt_["first"] = False
            return dict(qk=qk, sm=sm, pv=pv)

        def layer1_rest(N, nb, T, ctf, ctfkey, y_dst, halo):
            nt = max(1, N // 128)
            npart = min(N, 128)
            Xt = lambda t_: X[0:npart, t_, :]
            XNt = lambda t_: XN[0:npart, t_, :]
            if halo:
                down_proj(lambda k, t_: HAH[:, k, 0:2], "HAH", 0, KC, wb_o, 1, 2, lambda t_: XH[0:2, :], "XH", ("wc", "o"))
                rms_stats(lambda t_: XH[0:2, :], "XH", 1, 2)
                rms_apply(lambda t_: XH[0:2, :], "XH", 1, 2, 4, lambda t_: XNH[0:2, :], "XNH")
                transpose_fm(lambda t_: XNH[0:2, :], "XNH", 1, 2, XTH[:].rearrange("p k t -> p (k t)"), "XTH")
                ffn_halo(1, ctf, ctfkey)
            down_proj(lambda k, t_: HA[:, k, t_ * 128:t_ * 128 + npart], "HA", 0, KC, wb_o, nt, npart, Xt, "X", ("wc", "o"),
                      after_tile=norm_cb(Xt, "X", npart, 4, XNt, "XN"))
            transpose_fm(XNt, "XN", nt, npart, XT, "XT")
            ffn_full(1, "XT", N, nb, T, ctf, ctfkey, nt, npart, Xt, after_tile=norm_cb(Xt, "X", npart, 5, Xt, "X"))
            dma(y_dst, X[0:npart, 0:nt, :], reads=xall(nt), writes=["yout"])

        with slow_dma("conv state layout (tiny)"):
            for b in range(SB_):
                for t2 in range(2):
                    dma(CTAs[:, :, b, t2], sta[b, t2].rearrange("(j p) -> p j", p=128), reads=[], writes=["CTAs"])
                    for l in range(2):
                        dma(CTFs[:, l, :, b, t2], stf[l, b, t2].rearrange("(j p) -> p j", p=128), reads=[], writes=["CTFs"])
        memset("pool", VB[:, 0:4 * 1040], 1.0, ["VB"])
        phase_a(xs.rearrange("(t p) d -> p t d", p=128), NS, SB_, ST_, CTAs, "CTAs", CTFs, "CTFs", True, 0)
        with slow_dma("conv state layout (tiny)"):
            for b in range(SB_):
                for t2 in range(2):
                    dma(sca[b, t2].rearrange("(j p) -> p j", p=128), CTAs[:, :, b, t2], reads=["CTAs"], writes=["sca"])
                    dma(sfc[0, b, t2].rearrange("(j p) -> p j", p=128), CTFs[:, 0, :, b, t2], reads=["CTFs"], writes=["sfc"])
        kts_sv = kts_s.rearrange("(f p) b t -> p f b t", p=128)
        for b in range(SB_):
            dma(kts_sv[:, :, b, PAST:PAST + ST_], KTST[:, 0:KC * NS].rearrange("p (f b t) -> p f b t", f=KC, b=SB_)[:, :, b, :],
                reads=["KTST"], writes=["kts_s"])
        for b in range(SB_):
            for t8 in range(8):
                dma(CKB[:, :], ck[b, t8 * 128:(t8 + 1) * 128, :], reads=[], writes=["CKB"], q="pool")
                transpose_fm(lambda t_: CKB, "CKB", 1, 128, XT, "XT")
                dma(kts_sv[:, :, b, t8 * 128:(t8 + 1) * 128], XT[:, 0:KC * 128].rearrange("p (f t) -> p f t", f=KC),
                    reads=["XT"], writes=["kts_s"])
        Ls, lsk = sample_state["L"]
        for b in range(SB_):
            for t8 in range(8):
                li = nxt("lg", 2)
                dma(LGt[li][:, :], clf[b, t8 * 128:(t8 + 1) * 128, :], reads=[], writes=[("LGt", li)])
                cumsum_tile(LGt[li][:, :], ("LGt", li), 128, CALLs[:, b, t8, :], "CALLs", CARRYs[:, b, t8, :], CARRYs[:, b, t8 + 1, :],
                            "CARRYs", sel127, "sel127")
            li = nxt("lg", 2)
            dma(LGt[li][0:ST_, :], Ls[b * ST_:(b + 1) * ST_, :], reads=[lsk], writes=[("LGt", li)])
            cumsum_tile(LGt[li][0:ST_, :], ("LGt", li), ST_, CALLs[0:ST_, b, 8, :], "CALLs", CARRYs[0:ST_, b, 8, :], CARRYs[:, b, 9, :],
                        "CARRYs", sel31, "sel31")
        cp("dve", QS[:, :], QTST[:, 0:KC * NS], ["QTST"], ["QS"])
        cp("dve", VBS[:, :], VB[:, 0:1040], ["VB"], ["VBS"])
        fence(KEYS_A, KEYS_B)
        for i in range(2):
            memset("dve", VR[i][:, :], 1.0, [f"VR{i}"])
        for b in range(SB_):
            cp("dve", QW[:, 0:KC * ST_].rearrange("p (f t) -> p f t", f=KC),
               QS[:, :].rearrange("p (f b t) -> p f b t", f=KC, b=SB_)[:, :, b, :], ["QS"], ["QW"])
            tt("dve", BIAS[:, 0:H * 9].rearrange("p (h t) -> p h t", h=H), CARRYs[:, b, 9, :].unsqueeze(2).to_broadcast([128, H, 9]),
               CALLs[:, b, :, :].rearrange("p t h -> p h t"), ALU.subtract, ["CALLs", "CARRYs"], ["BIAS"])
            qv = QW[:, 0:KC * ST_].rearrange("p (f t) -> p f t", f=KC)
            for hp in range(8):
                ki = nxt("kr", 2)
                vi = nxt("vr", 2)
                dma(KR[ki][:, 0:PAST + ST_], kts_sv[:, hp, b, :], reads=["kts_s"], writes=[f"KR{ki}"])
                vrv = VR[vi][:, 0:9 * 130].rearrange("p (t h e) -> p t h e", t=9, e=65)
                with slow_dma("V head split"):
                    for h2 in range(2):
                        dma(vrv[:, 0:8, h2, 0:64], cv[b].rearrange("(t p) (h e) -> p t h e", p=128, e=64)[:, :, 2 * hp + h2, :],
                            reads=[], writes=[f"VR{vi}"], q="pool")
                    dma(vrv[0:ST_, 8, :, 0:64], VBS[b * ST_:(b + 1) * ST_, :].rearrange("p (h e) -> p h e", e=65)[:, 2 * hp:2 * hp + 2, 0:64],
                        reads=["VBS"], writes=[f"VR{vi}"])
                for hl in range(2):
                    h = 2 * hp + hl
                    st_ = att_open(ST_)
                    tiles = []
                    for t9 in range(9):
                        kn = 128 if t9 < 8 else ST_
                        tiles.append((KR[ki][64 * hl:64 * hl + 64, t9 * 128:t9 * 128 + kn],
                                      VR[vi][0:kn, t9 * 130 + hl * 65:t9 * 130 + hl * 65 + 65], kn,
                                      (MKS[0:ST_, 0:ST_] if t9 == 8 else None)))
                    it1 = group_item(st_, qv[64 * hl:64 * hl + 64, hp, :], "QW", tiles[0:8], [f"KR{ki}", f"VR{vi}"],
                                     BIAS[:, h * 9:h * 9 + 8], ["BIAS"])
                    it2 = group_item(st_, qv[64 * hl:64 * hl + 64, hp, :], "QW", tiles[8:9], [f"KR{ki}", f"VR{vi}"],
                                     BIAS[:, h * 9 + 8:h * 9 + 9], ["BIAS"])
                    run_items([it1, it2])
                    att_close(st_, HA[64 * hl:64 * hl + 64, hp, b * ST_:(b + 1) * ST_], "HA")
        layer1_rest(NS, SB_, ST_, CTFs, "CTFs", ys.rearrange("(t p) d -> p t d", p=128), halo=False)
        with slow_dma("conv state layout (tiny)"):
            for b in range(SB_):
                for t2 in range(2):
                    dma(sfc[1, b, t2].rearrange("(j p) -> p j", p=128), CTFs[:, 1, :, b, t2], reads=["CTFs"], writes=["sfc"])
        fence(KEYS_B, KEYS_A)
        memset("pool", VB[:, 0:4 * 1040], 1.0, ["VB"])

        for c in range(NCH):
            phase_a(xp[c * CH:(c + 1) * CH, :].rearrange("(t p) d -> p t d", p=128), CH, 1, CH, CTA, "CTA", CTF, "CTF", False, c)
        with slow_dma("conv state layout (tiny)"):
            for t2 in range(2):
                dma(pca[t2].rearrange("(j p) -> p j", p=128), CTA[:, :, 0, t2], reads=["CTA"], writes=["pca"])
                dma(pfc0[t2].rearrange("(j p) -> p j", p=128), CTF[:, 0, :, 0, t2], reads=["CTF"], writes=["pfc0"])
        fence(KEYS_A, KEYS_B)

        ktsv = kts.rearrange("(f p) t -> p f t", p=128)
        dma(x1own.rearrange("(s o r) d -> s o r d", o=1, r=CH),
            x1s[2:SEQ + 2, :].rearrange("(s p r) d -> s p r d", p=2, r=CH)[:, bass.ds(par, 1), :, :], reads=["x1s"], writes=["x1own"])
        dma(x1hal.rearrange("s (o t) d -> s o t d", o=1),
            x1s[0:SEQ, :].rearrange("(s p r) d -> s p r d", p=2, r=CH)[:, bass.ds(par, 1), 0:2, :], reads=["x1s", "x1pad"], writes=["x1hal"])
        dma(qown[:, :, 2:CH + 2].rearrange("f s (o t) -> f s o t", o=1),
            qts[:, 2:SEQ + 2].rearrange("f (s p r) -> f s p r", p=2, r=CH)[:, :, bass.ds(par, 1), :], reads=["qts"], writes=["qown"])
        with slow_dma("tiny halo columns"):
            dma(qown[:, :, 0:2].rearrange("f s (o t) -> f s o t", o=1),
                qts[:, 0:SEQ].rearrange("f (s p r) -> f s p r", p=2, r=CH)[:, :, bass.ds(par, 1), 0:2], reads=["qts", "qtpad"], writes=["qownh"])
        qownv = qown.rearrange("(f p) s t -> p f s t", p=128)
        for s in range(NSLOT):
            nkt = 8 * (s + 1)
            nkth = 8 * s + 4
            dma(QW[:, 0:KC * 514].rearrange("p (f t) -> p f t", f=KC), qownv[:, :, s, :], reads=["qown", "qownh"], writes=["QW"])
            dma(X[:, 0:4, :], x1own[s * CH:(s + 1) * CH, :].rearrange("(t p) d -> p t d", p=128), reads=["x1own"], writes=xall(4))
            dma(XH[0:2, :], x1hal[s], reads=["x1hal"], writes=["XH"])
            qv = QW[:, 0:KC * 514].rearrange("p (f t) -> p f t", f=KC)
            jref = 8 * s + 8
            tt("dve", BIAS[:, 0:H * nkt].rearrange("p (h t) -> p h t", h=H), CARRY[:, jref, :].unsqueeze(2).to_broadcast([128, H, nkt]),
               CALL[:, 0:nkt, :].rearrange("p t h -> p h t"), ALU.subtract, ["CALL", "CARRY"], ["BIAS"])
            bv = BIAS[:, 0:H * nkt].rearrange("p (h t) -> p h t", h=H)
            for hp in range(8):
                stm = [att_open(CH) for _ in range(2)]
                sth = [att_open(2) for _ in range(2)]
                for seg in range(0, nkt, 16):
                    segn = min(16, nkt - seg)
                    ki = nxt("kr", 2)
                    vi = nxt("vr", 2)
                    dma(KR[ki][:, 0:segn * 128], ktsv[:, hp, seg * 128:(seg + segn) * 128], reads=["kts"], writes=[f"KR{ki}"])
                    dma(VR[vi][:, 0:segn * 130], vsc[hp, :, seg:seg + segn, :].rearrange("p t e -> p (t e)"), reads=["vsc"], writes=[f"VR{vi}"])
                    kvk = [f"KR{ki}", f"VR{vi}"]
                    items = []
                    for lt in range(segn):
                        t_ = seg + lt
                        j = t_ - 8 * s
                        for hl in range(2):
                            h = 2 * hp + hl
                            items.append(main_item(stm[hl], qv[64 * hl:64 * hl + 64, hp, 2:514], "QW",
                                                   KR[ki][64 * hl:64 * hl + 64, lt * 128:(lt + 1) * 128],
                                                   VR[vi][:, lt * 130 + hl * 65:lt * 130 + hl * 65 + 65], kvk,
                                                   bv[:, h, t_:t_ + 1], (MK[:, j, :] if j >= 0 else None)))
                    nh = min(segn, nkth - seg)
                    if nh > 0:
                        for hl in range(2):
                            h = 2 * hp + hl
                            tiles = []
                            for lt in range(nh):
                                j = seg + lt - 8 * s
                                tiles.append((KR[ki][64 * hl:64 * hl + 64, lt * 128:(lt + 1) * 128],
                                              VR[vi][:, lt * 130 + hl * 65:lt * 130 + hl * 65 + 65], 128,
                                              (MKH[:, j + 1, :] if j >= -1 else None)))
                            items.append(group_item(sth[hl], qv[64 * hl:64 * hl + 64, hp, 0:2], "QW", tiles, kvk,
                                                    bv[:, h, seg:seg + nh], ["BIAS"]))
                    run_items(items)
                for hl in range(2):
                    att_close(stm[hl], HA[64 * hl:64 * hl + 64, hp, :], "HA")
                    att_close(sth[hl], HAH[64 * hl:64 * hl + 64, hp, :], "HAH")
            layer1_rest(CH, 1, CH, CTF, "CTF", yp[s].rearrange("(t p) d -> p t d", p=128), halo=True)
        with slow_dma("conv state layout (tiny)"):
            for t2 in range(2):
                dma(pfc1[t2].rearrange("(j p) -> p j", p=128), CTF[:, 1, :, 0, t2], reads=["CTF"], writes=["pfc1"])
        P.emit()
    return nc


_NC_CACHE = {}


def kernel(x_prompt, x_sample, state_conv_a, state_ffn_conv, cache_k, cache_v, cache_logf,
           a_norm, w_a_in, a_conv_w, w_a_out, kv_norm, w_kv, b_f, b_norm, w_q, w_o,
           ffn_norm, w_ffn_up, ffn_conv_w, w_ffn_down, final_norm):
    f = lambda a: np.ascontiguousarray(np.asarray(a, dtype=np.float32))
    x_prompt, x_sample = f(x_prompt), f(x_sample)
    B = x_prompt.shape[0]
    assert x_prompt.shape[1] == SEQ
    n = 2 * B
    if "nc" not in _NC_CACHE:
        _NC_CACHE["nc"] = build_program()
    nc = _NC_CACHE["nc"]
    shared = dict(a_norm=f(a_norm).reshape(D), w_a_in=f(w_a_in)[0], a_conv_w=f(a_conv_w)[0], w_a_out=f(w_a_out)[0],
                  kv_norm=f(kv_norm), w_kv=f(w_kv), b_f=f(b_f), b_norm=f(b_norm).reshape(D), w_q=f(w_q)[0], w_o=f(w_o)[0],
                  ffn_norm=f(ffn_norm), w_up=f(w_ffn_up), ffn_conv_w=f(ffn_conv_w), w_dn=f(w_ffn_down), final_norm=f(final_norm))
    sca_, sfc_ = f(state_conv_a), f(state_ffn_conv)
    ck_, cv_, clf_ = f(cache_k), f(cache_v), f(cache_logf)
    in_maps = []
    for c in range(n):
        sl = slice(SB_ * c, SB_ * (c + 1))
        m = dict(shared)
        m.update(xp=x_prompt[c // 2], xs=x_sample[sl].reshape(NS, D), sta=sca_[0, sl], stf=np.ascontiguousarray(sfc_[:, sl]),
                 ck=ck_[sl].reshape(SB_, PAST, D), cv=cv_[sl].reshape(SB_, PAST, D), clf=clf_[sl])
        in_maps.append(m)
    res = run_bass_kernel_spmd(nc, in_maps, core_ids=list(range(n)))
    R = res.results
    NSLOT = SEQ // CH // 2
    DB = SB_ * n
    y_prompt = np.zeros((B, SEQ, D), np.float32)
    y_sample = np.zeros((DB, ST_, D), np.float32)
    p_conv_a = np.zeros((1, B, 2, D), np.float32)
    p_ffn_conv = np.zeros((2, B, 2, NUP), np.float32)
    p_k = np.zeros((B, SEQ, H, 64), np.float32)
    p_v = np.zeros((B, SEQ, H, 64), np.float32)
    p_logf = np.zeros((B, SEQ, H), np.float32)
    s_conv_a = np.zeros((1, DB, 2, D), np.float32)
    s_ffn_conv = np.zeros((2, DB, 2, NUP), np.float32)
    s_k = np.zeros((DB, ST_, H, 64), np.float32)
    s_v = np.zeros((DB, ST_, H, 64), np.float32)
    s_logf = np.zeros((DB, ST_, H), np.float32)
    for c in range(n):
        b, half = c // 2, c % 2
        r = R[c]
        for s in range(NSLOT):
            qb = 2 * s + half
            y_prompt[b, qb * CH:(qb + 1) * CH] = r["yp"][s]
        sl = slice(SB_ * c, SB_ * (c + 1))
        y_sample[sl] = r["ys"].reshape(SB_, ST_, D)
        if half == 0:
            p_conv_a[0, b] = r["pca"]
            p_ffn_conv[0, b] = r["pfc0"]
            p_k[b] = r["pk"].reshape(SEQ, H, 64)
            p_v[b] = r["pv"].reshape(SEQ, H, 64)
            p_logf[b] = r["plf"]
        else:
            p_ffn_conv[1, b] = r["pfc1"]
        s_conv_a[0, sl] = r["sca"]
        s_ffn_conv[:, sl] = r["sfc"]
        s_k[sl] = r["sk"].reshape(SB_, ST_, H, 64)
        s_v[sl] = r["sv"].reshape(SB_, ST_, H, 64)
        s_logf[sl] = r["slf"].reshape(SB_, ST_, H)
    return (y_prompt, y_sample, p_conv_a, p_ffn_conv, p_k, p_v, p_logf, s_conv_a, s_ffn_conv, s_k, s_v, s_logf)
```

```python
import os, sys, contextlib
import numpy as np
import concourse.bass as bass
import concourse.mybir as mybir
from concourse.bass_utils import run_bass_kernel_spmd

F32 = mybir.dt.float32
BF16 = mybir.dt.bfloat16
ALU = mybir.AluOpType
AF = mybir.ActivationFunctionType

D = 1024
NUP = 5632
DFF = 2816
H = 16
KC = 8
NF = 44
NPAIR = 22
PAST = 1024
SB_ = 4
ST_ = 32
NS = SB_ * ST_
EPS = 1e-6
SEQ = int(os.environ.get("YK_SEQ", "8192"))
CH = 512
STRICT = bool(int(os.environ.get("YK_STRICT", "0")))


class Prog:
    COMPUTE = ("pe", "act", "dve", "pool")

    def __init__(self, nc, es, n_dma_sems=24):
        self.nc = nc
        self.ops = []
        self.res = {}
        self.eng = {"pe": nc.tensor, "act": nc.scalar, "dve": nc.vector, "pool": nc.gpsimd, "sp": nc.sync}
        self.sem = {e: es.enter_context(nc.semaphore("s_" + e)) for e in self.COMPUTE}
        self.dma_sems = {q: [es.enter_context(nc.semaphore(f"d_{q}{i}")) for i in range(n_dma_sems)]
                         for q in ("sp", "pool")}
        self.eng_idx = {e: 0 for e in self.eng}

    def op(self, eng, fn, reads=(), writes=(), dma=False):
        o = dict(eng=eng, fn=fn, dma=dma, deps=[], milestone=False, idx=self.eng_idx[eng], dbg=(list(reads), list(writes)))
        self.eng_idx[eng] += 1
        deps = {}
        for r in reads:
            st = self.res.setdefault(r, dict(w=None, rs=[]))
            if st["w"] is not None:
                deps[id(st["w"])] = (st["w"], "raw")
        for w in writes:
            st = self.res.setdefault(w, dict(w=None, rs=[]))
            if st["w"] is not None and id(st["w"]) not in deps:
                deps[id(st["w"])] = (st["w"], "waw")
            for r in st["rs"]:
                if id(r) not in deps:
                    deps[id(r)] = (r, "war")
        for p, kind in deps.values():
            if p is o:
                continue
            if (not p["dma"]) and p["eng"] == eng and not dma and not STRICT:
                if eng == "pe":
                    continue
                if kind != "raw" or (o["idx"] - p["idx"]) > 3:
                    continue
            o["deps"].append(p)
            p["milestone"] = True
        for r in reads:
            rs = self.res[r]["rs"]
            if not dma and not STRICT:
                rs[:] = [x for x in rs if x["dma"] or x["eng"] != eng]
            rs.append(o)
        for w in writes:
            st = self.res[w]
            st["w"] = o
            st["rs"] = []
        self.ops.append(o)
        return o

    def emit(self, final_wait_eng="sp"):
        cnt = {e: 0 for e in self.COMPUTE}
        dcnt = {q: 0 for q in self.dma_sems}
        semuse = {}
        for o in self.ops:
            if o["dma"]:
                q = o["eng"]
                pool = self.dma_sems[q]
                s = pool[dcnt[q] % len(pool)]
                dcnt[q] += 1
                semuse[id(s)] = semuse.get(id(s), 0) + 16
                o["sig"] = (s, semuse[id(s)])
            elif o["milestone"]:
                cnt[o["eng"]] += 1
                o["sig"] = (self.sem[o["eng"]], cnt[o["eng"]])
        waited = {e: {} for e in self.eng}
        nwaits = 0
        for o in self.ops:
            e = o["eng"]
            h = self.eng[e]
            need = {}
            if o["dma"]:
                s, v = o["sig"]
                if v > 16:
                    need[id(s)] = (s, v - 16)
            for p in o["deps"]:
                s, v = p["sig"]
                if id(s) not in need or need[id(s)][1] < v:
                    need[id(s)] = (s, v)
            for k, (s, v) in need.items():
                if waited[e].get(k, 0) >= v:
                    continue
                h.wait_ge(s, v)
                nwaits += 1
                waited[e][k] = v
            try:
                ins = o["fn"]()
            except Exception:
                print("[prog] failing op:", o["eng"], o.get("dbg"), file=sys.stderr)
                raise
            if o["dma"]:
                ins.then_inc(o["sig"][0], 16)
            elif o["milestone"]:
                ins.then_inc(o["sig"][0], 1)
        h = self.eng[final_wait_eng]
        for q, pool in self.dma_sems.items():
            for s in pool:
                v = semuse.get(id(s), 0)
                if v > 0 and waited[final_wait_eng].get(id(s), 0) < v:
                    h.wait_ge(s, v)
        print(f"[prog] ops={len(self.ops)} waits={nwaits} milestones={cnt} dmas={dcnt}", file=sys.stderr)


def build_program():
    nc = bass.Bass("TRN2", target_bir_lowering=False)
    NCH = SEQ // CH
    NSLOT = NCH // 2
    NKT = SEQ // 128
    din = lambda n, s: nc.dram_tensor(n, list(s), F32, kind="ExternalInput").ap()
    dout = lambda n, s: nc.dram_tensor(n, list(s), F32, kind="ExternalOutput").ap()
    dscr = lambda n, s, dt: nc.dram_tensor(n, list(s), dt).ap()
    xp = din("xp", [SEQ, D]); xs = din("xs", [NS, D])
    sta = din("sta", [SB_, 2, D]); stf = din("stf", [2, SB_, 2, NUP])
    ck = din("ck", [SB_, PAST, D]); cv = din("cv", [SB_, PAST, D]); clf = din("clf", [SB_, PAST, H])
    a_norm = din("a_norm", [D]); w_a_in = din("w_a_in", [D, 3 * D]); a_conv_w = din("a_conv_w", [3, D])
    w_a_out = din("w_a_out", [D, D]); kv_norm = din("kv_norm", [D]); w_kv = din("w_kv", [D, 2 * D + H])
    b_f = din("b_f", [H]); b_norm = din("b_norm", [D]); w_q = din("w_q", [D, D]); w_o = din("w_o", [D, D])
    ffn_norm = din("ffn_norm", [2, D]); w_up = din("w_up", [2, D, NUP]); ffn_conv_w = din("ffn_conv_w", [2, 3, NUP])
    w_dn = din("w_dn", [2, DFF, D]); final_norm = din("final_norm", [D])
    gain_src = [a_norm, ffn_norm[0], kv_norm, b_norm, ffn_norm[1], final_norm]
    yp = dout("yp", [NSLOT, CH, D]); ys = dout("ys", [NS, D])
    pca = dout("pca", [2, D]); pfc0 = dout("pfc0", [2, NUP]); pfc1 = dout("pfc1", [2, NUP])
    pk = dout("pk", [SEQ, D]); pv = dout("pv", [SEQ, D]); plf = dout("plf", [SEQ, H])
    sca = dout("sca", [SB_, 2, D]); sfc = dout("sfc", [2, SB_, 2, NUP])
    sk = dout("sk", [NS, D]); sv = dout("sv", [NS, D]); slf = dout("slf", [NS, H])
    wb_ain = dscr("wb_ain", [KC, 128, KC, 3, 128], BF16)
    wb_up = dscr("wb_up", [2, NPAIR, 128, KC, 2, 128], BF16)
    wb_aout = dscr("wb_aout", [D, D], BF16)
    wb_kv = dscr("wb_kv", [D, 2 * D + H], BF16)
    wb_q = dscr("wb_q", [D, D], BF16)
    wb_o = dscr("wb_o", [D, D], BF16)
    wb_dn = dscr("wb_dn", [2, DFF, D], BF16)
    x1s = dscr("x1s", [SEQ + 2, D], F32)
    qts = dscr("qts", [D, SEQ + 2], BF16)
    kts = dscr("kts", [D, SEQ], BF16)
    vsc = dscr("vsc", [8, 128, NKT, 130], BF16)
    kts_s = dscr("kts_s", [D, SB_, PAST + ST_], BF16)
    halfc = dscr("halfc", [2, 128], F32)
    x1own = dscr("x1own", [NSLOT * CH, D], F32)
    x1hal = dscr("x1hal", [NSLOT, 2, D], F32)
    qown = dscr("qown", [D, NSLOT, CH + 2], BF16)

    with contextlib.ExitStack() as es:
        P = Prog(nc, es)
        sb_bytes = [0]

        def sbt(n, s, dt=F32):
            sb_bytes[0] += int(np.prod(s[1:])) * (2 if dt == BF16 else 4)
            return nc.alloc_sbuf_tensor(n, list(s), dt)
        ident = sbt("ident", [128, 128], BF16)
        identf = sbt("identf", [128, 128])
        tri = sbt("tri", [128, 128])
        sel127 = sbt("sel127", [128, 128])
        sel31 = sbt("sel31", [128, 128])
        Dm = sbt("Dm", [128, 512])
        ones_f = sbt("ones_f", [128, 64])
        HALF = sbt("HALF", [128, 1])
        DELTA = sbt("DELTA", [128, 8])
        DELTAH = sbt("DELTAH", [128, 5])
        JT = sbt("JT", [128, 8])
        epsT = sbt("epsT", [128, 1])
        MK = sbt("MK", [128, 8, 512], BF16)
        MKH = sbt("MKH", [128, 5, 2], BF16)
        MKS = sbt("MKS", [128, ST_], BF16)
        gainb = [sbt(f"gain{i}", [128, D]) for i in range(2)]
        bf_bc = sbt("bf_bc", [128, H])
        cwa = sbt("cwa", [128, KC, 3])
        cwf = sbt("cwf", [128, 2, NF, 3])
        X = sbt("X", [128, 4, D])
        XN = sbt("XN", [128, 4, D], BF16)
        XNH = sbt("XNH", [128, D], BF16)
        XT = sbt("XT", [128, KC * CH], BF16)
        XTH = sbt("XTH", [128, KC, 2], BF16)
        HA = sbt("HA", [128, KC, CH], BF16)
        HAH = sbt("HAH", [128, KC, 2], BF16)
        HF = sbt("HF", [128, 11, CH], BF16)
        Ub = [sbt(f"U{i}", [128, CH + 2 * SB_]) for i in range(2)]
        Ab = [sbt(f"A{i}", [128, CH]) for i in range(3)]
        GCb = [sbt(f"GC{i}", [128, CH]) for i in range(2)]
        Gb = [sbt(f"G{i}", [128, CH]) for i in range(2)]
        CTA = sbt("CTA", [128, KC, SB_, 2])
        CTF = sbt("CTF", [128, 2, NF, SB_, 2])
        CTAs = sbt("CTAs", [128, KC, SB_, 2])
        CTFs = sbt("CTFs", [128, 2, NF, SB_, 2])
        ss = sbt("ss", [128, 8])
        rstd = sbt("rstd", [128, 8])
        junk = sbt("junk", [128, D])
        CALL = sbt("CALL", [128, NKT + 1, H])
        CARRY = sbt("CARRY", [128, NKT + 1, H])
        CALLs = sbt("CALLs", [128, SB_, 9, H])
        CARRYs = sbt("CARRYs", [128, SB_, 10, H])
        LG = [sbt(f"LG{i}", [128, H]) for i in range(2)]
        LGt = [sbt(f"LGt{i}", [128, H]) for i in range(2)]
        NWS, NWB = 3, 2
        WS = sbt("WS", [128, NWS, 3072], BF16)
        WB = sbt("WB", [128, NWB, 5632], BF16)
        fence_t = sbt("fence_t", [128, 2])
        QS = sbt("QS", [128, KC * NS], BF16)
        VBS = sbt("VBS", [128, 1040], BF16)
        SCR_BYTES = 46080
        SCR = sbt("SCR", [128, SCR_BYTES // 4])
        scr_b = SCR[:].bitcast(BF16)

        def scr_alloc(cur, nelem, dt):
            sz = 2 if dt == BF16 else 4
            off = (cur[0] + 63) // 64 * 64
            cur[0] = off + nelem * sz
            assert cur[0] <= SCR_BYTES, (cur[0], SCR_BYTES)
            return scr_b[:, off // 2: off // 2 + nelem] if dt == BF16 else SCR[:, off // 4: off // 4 + nelem]
        ca = [0]
        KVST = scr_alloc(ca, 2 * 2048, F32)
        VB = scr_alloc(ca, 4 * 1040, BF16)
        KTST = scr_alloc(ca, KC * CH, BF16)
        QTST = scr_alloc(ca, KC * CH, BF16)
        CKB = scr_alloc(ca, 1024, BF16)
        KEYS_A = ["VB", "KTST", "QTST", "CKB"] + [("KVST", p_, c_) for p_ in range(2) for c_ in range(4)]
        cb_ = [0]
        QW = scr_alloc(cb_, KC * 514, BF16)
        KR = [scr_alloc(cb_, 2048, BF16) for _ in range(2)]
        VR = [scr_alloc(cb_, 16 * 130, BF16) for _ in range(2)]
        NPT = 6
        PT = [scr_alloc(cb_, 512, BF16) for _ in range(NPT)]
        OS = scr_alloc(cb_, 512, F32)
        RB = scr_alloc(cb_, 512, F32)
        GS = scr_alloc(cb_, 512, F32)
        BIAS = scr_alloc(cb_, H * 64, F32)
        XH = scr_alloc(cb_, D, F32)
        KEYS_B = ["QW", "KR0", "KR1", "VR0", "VR1"] + [f"PT{i}" for i in range(NPT)] + ["OS", "RBrow", "GS", "BIAS", "XH"]
        print(f"[sbuf] {sb_bytes[0] / 1024:.1f} KiB/partition, scrA={ca[0]} scrB={cb_[0]}", file=sys.stderr)
        psf = es.enter_context(nc.psum_tensor("psf", [128, 8, 512], F32))
        free_banks = list(range(8))
        rr = {}

        def bank_get():
            return free_banks.pop(0)

        def bank_put(b):
            free_banks.append(b)

        def nxt(k, n):
            v = rr.get(k, 0)
            rr[k] = (v + 1) % n
            return v

        def E(eng):
            return P.eng[eng]

        def cp(eng, out, in_, reads, writes):
            if eng == "act":
                P.op("act", lambda: nc.scalar.copy(out=out, in_=in_), reads, writes)
            else:
                P.op(eng, lambda: E(eng).tensor_copy(out=out, in_=in_), reads, writes)

        def act(out, in_, func, reads, writes, bias=None, scale=None, accum=None):
            kw = {}
            if bias is not None:
                kw["bias"] = bias
            if scale is not None:
                kw["scale"] = scale
            if accum is not None:
                kw["accum_out"] = accum
            P.op("act", lambda: nc.scalar.activation(out=out, in_=in_, func=func, **kw), reads, writes)

        def tt(eng, out, in0, in1, op, reads, writes):
            P.op(eng, lambda: E(eng).tensor_tensor(out=out, in0=in0, in1=in1, op=op), reads, writes)

        def ts(eng, out, in0, s1, s2, op0, op1, reads, writes):
            if op1 is None:
                P.op(eng, lambda: E(eng).tensor_scalar(out=out, in0=in0, scalar1=s1, scalar2=None, op0=op0), reads, writes)
            else:
                P.op(eng, lambda: E(eng).tensor_scalar(out=out, in0=in0, scalar1=s1, scalar2=s2, op0=op0, op1=op1), reads, writes)

        def stt(out, in0, scalar, in1, op0, op1, reads, writes):
            P.op("dve", lambda: nc.vector.scalar_tensor_tensor(out=out, in0=in0, scalar=scalar, in1=in1, op0=op0, op1=op1),
                 reads, writes)

        def mm(out, lhsT, rhs, start, stop, reads, writes, skip=False):
            if skip:
                P.op("pe", lambda: nc.tensor.matmul(out, lhsT=lhsT, rhs=rhs, start=start, stop=stop, skip_group_check=True), reads, writes)
            else:
                P.op("pe", lambda: nc.tensor.matmul(out, lhsT=lhsT, rhs=rhs, start=start, stop=stop), reads, writes)

        def tr(out, in_, idn, reads, writes):
            P.op("pe", lambda: nc.tensor.transpose(out=out, in_=in_, identity=idn), reads, writes)

        slow_flag = [False]

        @contextlib.contextmanager
        def slow_dma(reason=""):
            old = slow_flag[0]
            slow_flag[0] = True
            try:
                yield
            finally:
                slow_flag[0] = old

        def dma(out, in_, reads, writes, q="sp"):
            slow = slow_flag[0]
            if slow:
                P.op(q, lambda: E(q).dma_start(out=out, in_=in_, allow_slow_non_contiguous=True), reads, writes, dma=True)
            else:
                P.op(q, lambda: E(q).dma_start(out=out, in_=in_), reads, writes, dma=True)

        def memset(eng, ap, val, writes):
            P.op(eng, lambda: E(eng).memset(ap, val), [], writes)

        def fence(from_keys, to_keys):
            P.op("pool", lambda: nc.gpsimd.memset(fence_t[:], 0.0), reads=list(from_keys),
                 writes=list(from_keys) + list(to_keys) + ["fence_t"])

        def wload(src_ap, shape, rkey="wcast"):
            n = int(np.prod(shape[1:]))
            if n <= 3072:
                s = nxt("ws", NWS)
                dst = WS[:, s, 0:n]
                key = ("WS", s)
            else:
                s = nxt("wb", NWB)
                dst = WB[:, s, 0:n]
                key = ("WB", s)
            if len(shape) == 3:
                dstv = dst.rearrange("p (a b) -> p a b", a=shape[1])
            elif len(shape) == 4:
                dstv = dst.rearrange("p (a b c) -> p a b c", a=shape[1], b=shape[2])
            else:
                dstv = dst
            dma(dstv, src_ap, reads=(list(rkey) if isinstance(rkey, list) else [rkey]), writes=[key])
            return dstv, key

        def gload(gi):
            s = nxt("gain", 2)
            dma(gainb[s][:], gain_src[gi].partition_broadcast(128), reads=[], writes=[("gain", s)])
            return gainb[s], ("gain", s)

        memset("pool", identf[:], 1.0, ["identf"])
        P.op("pool", lambda: nc.gpsimd.affine_select(out=identf[:], in_=identf[:], pattern=[[-1, 128]],
                                                     compare_op=ALU.is_equal, fill=0.0, base=0, channel_multiplier=1),
             reads=["identf"], writes=["identf"])
        cp("dve", ident[:], identf[:], ["identf"], ["ident"])
        memset("pool", tri[:], 1.0, ["tri"])
        P.op("pool", lambda: nc.gpsimd.affine_select(out=tri[:], in_=tri[:], pattern=[[1, 128]],
                                                     compare_op=ALU.is_ge, fill=0.0, base=0, channel_multiplier=-1),
             reads=["tri"], writes=["tri"])
        memset("pool", sel127[:], 1.0, ["sel127"])
        P.op("pool", lambda: nc.gpsimd.affine_select(out=sel127[:], in_=sel127[:], pattern=[[0, 128]], compare_op=ALU.is_equal,
                                                     fill=0.0, base=-127, channel_multiplier=1), reads=["sel127"], writes=["sel127"])
        memset("pool", sel31[:], 1.0, ["sel31"])
        P.op("pool", lambda: nc.gpsimd.affine_select(out=sel31[:], in_=sel31[:], pattern=[[0, 128]], compare_op=ALU.is_equal,
                                                     fill=0.0, base=-31, channel_multiplier=1), reads=["sel31"], writes=["sel31"])
        P.op("pool", lambda: nc.gpsimd.iota(Dm[:], pattern=[[-1, 512]], base=0, channel_multiplier=1,
                                            allow_small_or_imprecise_dtypes=True), writes=["Dm"])
        P.op("pool", lambda: nc.gpsimd.iota(JT[:], pattern=[[-128, 8]], base=0, channel_multiplier=0,
                                            allow_small_or_imprecise_dtypes=True), writes=["JT"])
        memset("dve", ones_f[:], 1.0, ["ones_f"])
        memset("dve", junk[:], 0.0, ["junk"])
        memset("dve", epsT[:], EPS, ["epsT"])
        memset("dve", CTA[:], 0.0, ["CTA"])
        memset("pool", CTF[:], 0.0, ["CTF"])
        memset("dve", CARRY[:, 0, :], 0.0, ["CARRY"])
        memset("dve", CARRYs[:], 0.0, ["CARRYs"])
        memset("dve", CALLs[:], 0.0, ["CALLs"])
        memset("dve", XNH[:], 0.0, ["XNH"])
        dma(halfc[0:1, :], junk[0:1, 0:128], reads=["junk"], writes=["halfc0"])
        dma(halfc[1:2, 0:64], ones_f[0:1, 0:64], reads=["ones_f"], writes=["halfc1"])
        dma(halfc[1:2, 64:128], ones_f[0:1, 0:64], reads=["ones_f"], writes=["halfc2"])
        pid = nc.sync.partition_id()
        par = pid % 2
        with slow_dma("tiny"):
            dma(HALF[:], halfc[bass.ds(par, 1), :].rearrange("a p -> p a"), reads=["halfc0", "halfc1", "halfc2"], writes=["HALF"])
        ts("dve", HALF[:], HALF[:], 512.0, None, ALU.mult, None, ["HALF"], ["H512"])
        ts("dve", DELTA[:], JT[:], HALF[:, 0:1], None, ALU.add, None, ["H512", "JT"], ["DELTA"])
        ts("dve", DELTAH[:], JT[:, 0:5], HALF[:, 0:1], 126.0, ALU.add, ALU.add, ["H512", "JT"], ["DELTAH"])
        for j in range(8):
            ts("pool" if j % 2 else "dve", MK[:, j, :], Dm[:], DELTA[:, j:j + 1], None, ALU.is_le, None, ["Dm", "DELTA"], ["MK"])
        for i in range(5):
            ts("dve", MKH[:, i, :], Dm[:, 0:2], DELTAH[:, i:i + 1], None, ALU.is_le, None, ["Dm", "DELTAH"], ["MKH"])
        ts("dve", MKS[:], Dm[:, 0:ST_], 0.0, None, ALU.is_le, None, ["Dm"], ["MKS"])
        dma(x1s[0:2, :], junk[0:2, :], reads=["junk"], writes=["x1pad"])
        zb = junk[:].bitcast(BF16)
        with slow_dma("tiny pad"):
            dma(qts[:, 0:2].rearrange("(f p) t -> p f t", p=128), zb[:, 0:16].rearrange("p (f t) -> p f t", t=2),
                reads=["junk"], writes=["qtpad"])
        dma(bf_bc[:], b_f.partition_broadcast(128), reads=[], writes=["bf_bc"])
        with slow_dma("tiny conv weight layout"):
            for w_ in range(3):
                dma(cwa[:, :, w_], a_conv_w[w_].rearrange("(j p) -> p j", p=128), reads=[], writes=["cw"])
                for l in range(2):
                    dma(cwf[:, l, :, w_], ffn_conv_w[l, w_].rearrange("(j p) -> p j", p=128), reads=[], writes=["cw"])
        for j in range(KC):
            for g in range(3):
                dma(wb_ain[j][:, :, g, :], w_a_in.rearrange("(kc p) (g f) -> p kc g f", p=128, g=3)[:, :, g, j * 128:(j + 1) * 128],
                    reads=[], writes=[("wc", "ain", j, g)], q="pool")
        dma(wb_aout, w_a_out, reads=[], writes=[("wc", "aout")], q="pool")
        for pr in range(NPAIR):
            for w_ in range(2):
                dma(wb_up[0, pr][:, :, w_, :], w_up[0].rearrange("(kc p) (w f) -> p kc w f", p=128, w=2)[:, :, w_, pr * 128:(pr + 1) * 128],
                    reads=[], writes=[("wc", "up", 0, pr, w_)], q="pool")
        dma(wb_dn[0], w_dn[0], reads=[], writes=[("wc", "dn", 0)], q="pool")
        dma(wb_kv, w_kv, reads=[], writes=[("wc", "kv")], q="pool")
        dma(wb_q, w_q, reads=[], writes=[("wc", "q")], q="pool")
        dma(wb_o, w_o, reads=[], writes=[("wc", "o")], q="pool")
        for pr in range(NPAIR):
            for w_ in range(2):
                dma(wb_up[1, pr][:, :, w_, :], w_up[1].rearrange("(kc p) (w f) -> p kc w f", p=128, w=2)[:, :, w_, pr * 128:(pr + 1) * 128],
                    reads=[], writes=[("wc", "up", 1, pr, w_)], q="pool")
        dma(wb_dn[1], w_dn[1], reads=[], writes=[("wc", "dn", 1)], q="pool")

        def mm_group(out_ap, pairs, bank_key):
            n = len(pairs)
            for i, (l_ap, r_ap, rds) in enumerate(pairs):
                mm(out_ap, l_ap, r_ap, i == 0, i == n - 1, rds, [bank_key])

        def xk(xkey, t_):
            return (xkey, t_) if xkey == "X" else xkey

        def rms_stats_tile(Xt, xkey, t_, npart):
            act(junk[0:npart, :], Xt(t_), AF.Square, [xk(xkey, t_)], ["junk", ("ss", t_)], accum=ss[0:npart, t_:t_ + 1])
            act(rstd[0:npart, t_:t_ + 1], ss[0:npart, t_:t_ + 1], AF.Sqrt, [("ss", t_), "epsT"], [("rstd", t_)],
                bias=epsT[0:npart, 0:1], scale=1.0 / D)
            P.op("dve", lambda: nc.vector.reciprocal(out=rstd[0:npart, t_:t_ + 1], in_=rstd[0:npart, t_:t_ + 1]),
                 reads=[("rstd", t_)], writes=[("rstd", t_)])

        def rms_apply_tile(Xt, xkey, t_, npart, g, gkey, ofn, okey):
            wk = xk(okey, t_)
            stt(ofn(t_), Xt(t_), rstd[0:npart, t_:t_ + 1], g[0:npart, :], ALU.mult, ALU.mult,
                [xk(xkey, t_), ("rstd", t_), gkey], [wk])

        def rms_stats(Xt, xkey, nt, npart):
            for t_ in range(nt):
                rms_stats_tile(Xt, xkey, t_, npart)

        def rms_apply(Xt, xkey, nt, npart, gi, ofn, okey, extra_reads=(), pre=None):
            g, gkey = pre if pre is not None else gload(gi)
            for t_ in range(nt):
                rms_apply_tile(Xt, xkey, t_, npart, g, gkey, ofn, okey)

        def norm_cb(Xt, xkey, npart, gi, ofn, okey):
            g, gkey = gload(gi)

            def cb(t_):
                rms_stats_tile(Xt, xkey, t_, npart)
                rms_apply_tile(Xt, xkey, t_, npart, g, gkey, ofn, okey)
            return cb

        def xall(nt):
            return [("X", t_) for t_ in range(nt)]

        def transpose_fm(src_fn, skey, nt, npart, dst_flat, dkey):
            blocks = [(kc, t_) for kc in range(KC) for t_ in range(nt)]
            for g0 in range(0, len(blocks), 4):
                grp = blocks[g0:g0 + 4]
                tb = bank_get()
                pbv = psf[:, tb, :].bitcast(BF16)
                for i, (kc, t_) in enumerate(grp):
                    tr(pbv[:, i * npart:(i + 1) * npart], src_fn(t_)[:, kc * 128:(kc + 1) * 128], ident[0:npart, 0:npart],
                       [skey, "ident"], [("ps", tb)])
                w = len(grp) * npart
                cp(("act", "dve")[nxt("ev", 2)], dst_flat[:, g0 * npart:g0 * npart + w], pbv[:, 0:w], [("ps", tb)], [dkey])
                bank_put(tb)

        def conv3(ps_ap3, pskey, pre_fn, cw_ap, ct_ap, ctkey, nb, T):
            ui = nxt("u", 2)
            U = Ub[ui][:, 0:nb * (T + 2)].rearrange("p (b t) -> p b t", b=nb)
            ukey = ("U", ui)
            ai = nxt("a", 3)
            A = Ab[ai][:, 0:nb * T].rearrange("p (b t) -> p b t", b=nb)
            akey = ("A", ai)
            if pre_fn is None:
                cp("act", U[:, :, 2:2 + T], ps_ap3, [pskey], [ukey])
                act(A, ps_ap3, AF.Copy, [pskey, "cw"], [akey], scale=cw_ap[:, 2:3])
            else:
                pre_fn(U[:, :, 2:2 + T], ukey)
                act(A, U[:, :, 2:2 + T], AF.Copy, [ukey, "cw"], [akey], scale=cw_ap[:, 2:3])
            cp("pool", U[:, :, 0:2], ct_ap, [ctkey], [ukey])
            cp("pool", ct_ap, U[:, :, T:T + 2], [ukey], [ctkey])
            for k in (1, 0):
                stt(A, U[:, :, k:k + T], cw_ap[:, k:k + 1], A, ALU.mult, ALU.add, [ukey, akey, "cw"], [akey])
            return A, akey

        def ffn_halo(l, CT, ctkey):
            hb = bank_get()
            hkey = ("ps", hb)
            for pr in range(NPAIR):
                wt, wkey = wload(wb_up[l, pr], [128, KC, 2, 128], [("wc", "up", l, pr, 0), ("wc", "up", l, pr, 1)])
                for wh in range(2):
                    j = wh * NPAIR + pr
                    mm_group(psf[:, hb, 2 * j:2 * j + 2], [(wt[:, kc, wh, :], XTH[:, kc, :], [wkey, "XTH"]) for kc in range(KC)], hkey)
            cp("act", CT[:, l, :, 0, :], psf[:, hb, 0:2 * NF].rearrange("p (j t) -> p j t", t=2), [hkey], [ctkey])
            bank_put(hb)

        def w_rows(wb, k0, kn, hf):
            return wb.rearrange("(k p) n -> p k n", p=128)[:, k0:k0 + kn, hf * 512:(hf + 1) * 512]

        def down_proj(lhs_fn, lkey, k_base, nk, wb, nt, npart, Xt, xkey, rkey, after_tile=None):
            ws = [wload(w_rows(wb, k_base, nk, hf), [128, nk, 512], rkey) for hf in range(2)]
            for t_ in range(nt):
                for hf in range(2):
                    wt, wkey = ws[hf]
                    b_ = bank_get()
                    bkey = ("ps", b_)
                    mm_group(psf[0:npart, b_, :], [(lhs_fn(k, t_), wt[:, k, :], [lkey, wkey]) for k in range(nk)], bkey)
                    tt("dve", Xt(t_)[:, hf * 512:(hf + 1) * 512], psf[0:npart, b_, :], Xt(t_)[:, hf * 512:(hf + 1) * 512], ALU.add,
                       [bkey, xk(xkey, t_)], [xk(xkey, t_)])
                    bank_put(b_)
                if after_tile is not None:
                    after_tile(t_)

        def ffn_full(l, xtkey, N, nb, T, CT, ctkey, nt, npart, Xt, after_tile=None, halo=False):
            hbks = [bank_get(), bank_get()] if halo else None
            for grp in range(2):
                for pl in range(11):
                    pr = grp * 11 + pl
                    wt, wkey = wload(wb_up[l, pr], [128, KC, 2, 128], [("wc", "up", l, pr, 0), ("wc", "up", l, pr, 1)])
                    As = []
                    for wh in range(2):
                        j = wh * NPAIR + pr
                        if halo:
                            hb = hbks[nxt("hb", 2)]
                            mm_group(psf[:, hb, 0:2], [(wt[:, kc, wh, :], XTH[:, kc, :], [wkey, "XTH"]) for kc in range(KC)], ("ps", hb))
                            cp("act", CT[:, l, j, 0, :], psf[:, hb, 0:2], [("ps", hb)], [ctkey])
                        b_ = bank_get()
                        bkey = ("ps", b_)
                        mm_group(psf[:, b_, 0:N], [(wt[:, kc, wh, :], XT[:, kc * N:(kc + 1) * N], [wkey, xtkey]) for kc in range(KC)], bkey)
                        A, akey = conv3(psf[:, b_, 0:N].rearrange("p (b t) -> p b t", b=nb), bkey, None, cwf[:, l, j, :],
                                        CT[:, l, j, 0:nb, :], ctkey, nb, T)
                        bank_put(b_)
                        As.append((A, akey))
                    gi = nxt("g", 2)
                    G = Gb[gi][:, 0:N].rearrange("p (b t) -> p b t", b=nb)
                    act(G, As[0][0], AF.Silu, [As[0][1]], [("G", gi)])
                    tt("pool", HF[:, pl, 0:N].rearrange("p (b t) -> p b t", b=nb), G, As[1][0], ALU.mult, [("G", gi), As[1][1]], ["HF"])
                down_proj(lambda k, t_: HF[:, k, t_ * 128:t_ * 128 + npart], "HF", grp * 11, 11, wb_dn[l], nt, npart, Xt, "X", ("wc", "dn", l),
                          after_tile=(after_tile if grp == 1 else None))
            if halo:
                for hb in hbks:
                    bank_put(hb)

        def logsig(ps_lf, pkey, npart, li):
            L, Lt = LG[li], LGt[li]
            lk, ltk = ("LG", li), ("LGt", li)
            tt("dve", L[0:npart], ps_lf, bf_bc[0:npart], ALU.add, [pkey, "bf_bc"], [lk])
            ts("dve", Lt[0:npart], L[0:npart], -1.0, None, ALU.mult, None, [lk], [ltk])
            tt("dve", Lt[0:npart], Lt[0:npart], L[0:npart], ALU.min, [lk, ltk], [ltk])
            act(Lt[0:npart], Lt[0:npart], AF.Exp, [ltk], [ltk])
            act(Lt[0:npart], Lt[0:npart], AF.Ln, [ltk], [ltk], bias=1.0, scale=1.0)
            ts("dve", L[0:npart], L[0:npart], 0.0, None, ALU.min, None, [lk], [lk])
            tt("dve", L[0:npart], L[0:npart], Lt[0:npart], ALU.subtract, [lk, ltk], [lk])
            return L, lk

        def cumsum_tile(L_ap, lk, npart, call_ap, ckey, carry_in_ap, carry_out_ap, cakey, sel, selkey):
            b_ = bank_get()
            bkey = ("ps", b_)
            mm(psf[0:npart, b_, 0:H], tri[0:npart, 0:npart], L_ap, True, True, ["tri", lk], [bkey])
            tt("dve", call_ap, psf[0:npart, b_, 0:H], carry_in_ap, ALU.add, [bkey, cakey], [ckey])
            mm(psf[:, b_, 256:256 + H], sel[0:npart, :], call_ap, True, True, [selkey, ckey], [bkey])
            cp("act", carry_out_ap, psf[:, b_, 256:256 + H], [bkey], [cakey])
            bank_put(b_)

        def phase_a(x_src, N, nb, T, cta, ctakey, ctf, ctfkey, sample, chunk_idx):
            nt = max(1, N // 128)
            npart = min(N, 128)
            Xt = lambda t_: X[0:npart, t_, :]
            XNt = lambda t_: XN[0:npart, t_, :]
            dma(X[0:npart, 0:nt, :], x_src, reads=[], writes=xall(nt))
            rms_stats(Xt, "X", nt, npart)
            rms_apply(Xt, "X", nt, npart, 0, XNt, "XN")
            transpose_fm(XNt, "XN", nt, npart, XT, "XT")
            for j in range(KC):
                wt, wkey = wload(wb_ain[j], [128, KC, 3, 128], [("wc", "ain", j, g_) for g_ in range(3)])
                bs = [bank_get() for _ in range(3)]
                for g in range(3):
                    mm_group(psf[:, bs[g], 0:N], [(wt[:, kc, g, :], XT[:, kc * N:(kc + 1) * N], [wkey, "XT"]) for kc in range(KC)],
                             ("ps", bs[g]))
                gci = nxt("gc", 2)
                GC = GCb[gci][:, 0:N].rearrange("p (b t) -> p b t", b=nb)
                cp("act", GC, psf[:, bs[1], 0:N].rearrange("p (b t) -> p b t", b=nb), [("ps", bs[1])], [("GC", gci)])

                def pre(Uv, ukey, b2=bs[2], GC=GC, gci=gci):
                    tt("dve", Uv, psf[:, b2, 0:N].rearrange("p (b t) -> p b t", b=nb), GC, ALU.mult, [("ps", b2), ("GC", gci)], [ukey])
                A, akey = conv3(None, None, pre, cwa[:, j, :], cta[:, j, 0:nb, :], ctakey, nb, T)
                tt("dve", HA[:, j, 0:N].rearrange("p (b t) -> p b t", b=nb), psf[:, bs[0], 0:N].rearrange("p (b t) -> p b t", b=nb), A,
                   ALU.mult, [("ps", bs[0]), akey], ["HA"])
                for b_ in bs:
                    bank_put(b_)
            down_proj(lambda k, t_: HA[:, k, t_ * 128:t_ * 128 + npart], "HA", 0, KC, wb_aout, nt, npart, Xt, "X", ("wc", "aout"),
                      after_tile=norm_cb(Xt, "X", npart, 1, XNt, "XN"))
            transpose_fm(XNt, "XN", nt, npart, XT, "XT")
            ffn_full(0, "XT", N, nb, T, ctf, ctfkey, nt, npart, Xt, after_tile=norm_cb(Xt, "X", npart, 2, XNt, "XN"))
            gpre_b = gload(3)
            tok0 = chunk_idx * CH
            if not sample:
                for t_ in range(nt):
                    dma(x1s[2 + tok0 + t_ * 128:2 + tok0 + (t_ + 1) * 128, :], X[:, t_, :], reads=[("X", t_)], writes=["x1s"], q="pool")
            transpose_fm(XNt, "XN", nt, npart, XT, "XT")
            wkv = wb_kv.rearrange("(k p) n -> p k n", p=128)
            for cb in range(4):
                wt, wkey = wload(wkv[:, :, cb * 512:(cb + 1) * 512], [128, KC, 512], ("wc", "kv"))
                for t_ in range(nt):
                    b_ = bank_get()
                    mm_group(psf[0:npart, b_, :], [(XT[:, kc * N + t_ * 128:kc * N + t_ * 128 + npart], wt[:, kc, :], ["XT", wkey])
                                                   for kc in range(KC)], ("ps", b_))
                    kvo = (t_ % 2) * 2048 + cb * 512
                    kvkey = ("KVST", t_ % 2, cb)
                    cp("act", KVST[0:npart, kvo:kvo + 512], psf[0:npart, b_, :], [("ps", b_)], [kvkey])
                    if cb >= 2:
                        h0 = (cb - 2) * 8
                        vbv = VB[0:npart, t_ * 1040:(t_ + 1) * 1040].rearrange("p (h e) -> p h e", e=65)
                        cp("dve", vbv[:, h0:h0 + 8, 0:64], KVST[0:npart, kvo:kvo + 512].rearrange("p (h e) -> p h e", e=64), [kvkey], ["VB"])
                    bank_put(b_)
                    dst = ((sk, sv) if sample else (pk, pv))[cb // 2]
                    r0 = 0 if sample else tok0 + t_ * 128
                    dma(dst[r0:r0 + npart, (cb % 2) * 512:(cb % 2 + 1) * 512], KVST[0:npart, kvo:kvo + 512],
                        reads=[kvkey], writes=["kvout"], q="pool")
                if cb < 2:
                    for f4 in range(4):
                        f = cb * 4 + f4
                        b_ = bank_get()
                        mm_group(psf[:, b_, 0:N], [(wt[:, kc, f4 * 128:(f4 + 1) * 128], XT[:, kc * N:(kc + 1) * N], [wkey, "XT"])
                                                   for kc in range(KC)], ("ps", b_))
                        cp(("act", "dve")[nxt("ev", 2)], KTST[:, f * N:(f + 1) * N], psf[:, b_, 0:N], [("ps", b_)], ["KTST"])
                        bank_put(b_)
            wtl, wlkey = wload(wkv[:, :, 2048:2064], [128, KC, H], ("wc", "kv"))
            for t_ in range(nt):
                b_ = bank_get()
                mm_group(psf[0:npart, b_, 0:H], [(XT[:, kc * N + t_ * 128:kc * N + t_ * 128 + npart], wtl[:, kc, :], ["XT", wlkey])
                                                 for kc in range(KC)], ("ps", b_))
                li = nxt("lg", 2)
                L, lk = logsig(psf[0:npart, b_, 0:H], ("ps", b_), npart, li)
                bank_put(b_)
                if not sample:
                    j = chunk_idx * 4 + t_
                    dma(plf[tok0 + t_ * 128:tok0 + (t_ + 1) * 128, :], L[:, :], reads=[lk], writes=["lfout"], q="pool")
                    cumsum_tile(L[:, :], lk, 128, CALL[:, j, :], "CALL", CARRY[:, j, :], CARRY[:, j + 1, :], "CARRY", sel127, "sel127")
                else:
                    dma(slf[:, :], L[0:npart, :], reads=[lk], writes=["lfout"])
                    sample_state["L"] = (L, lk)
            if not sample:
                with slow_dma("V head split"):
                    for hp in range(8):
                        dma(vsc[hp, :, chunk_idx * 4:chunk_idx * 4 + 4, :],
                            VB[:, 0:4 * 1040].rearrange("p (t x) -> p t x", t=4)[:, :, 2 * hp * 65:2 * hp * 65 + 130],
                            reads=["VB"], writes=["vsc"], q="pool")
                dma(kts.rearrange("(f p) t -> p f t", p=128)[:, :, tok0:tok0 + CH],
                    KTST[:, 0:KC * N].rearrange("p (f t) -> p f t", f=KC), reads=["KTST"], writes=["kts"], q="pool")
            rms_apply(Xt, "X", nt, npart, 3, XNt, "XN", pre=gpre_b)
            transpose_fm(XNt, "XN", nt, npart, XT, "XT")
            wqv = wb_q.rearrange("(k p) n -> p k n", p=128)
            for cb in range(2):
                wt, wkey = wload(wqv[:, :, cb * 512:(cb + 1) * 512], [128, KC, 512], ("wc", "q"))
                for f4 in range(4):
                    f = cb * 4 + f4
                    b_ = bank_get()
                    mm_group(psf[:, b_, 0:N], [(wt[:, kc, f4 * 128:(f4 + 1) * 128], XT[:, kc * N:(kc + 1) * N], [wkey, "XT"])
                                               for kc in range(KC)], ("ps", b_))
                    cp(("act", "dve")[nxt("ev", 2)], QTST[:, f * N:(f + 1) * N], psf[:, b_, 0:N], [("ps", b_)], ["QTST"])
                    bank_put(b_)
            if not sample:
                dma(qts.rearrange("(f p) t -> p f t", p=128)[:, :, 2 + tok0:2 + tok0 + CH],
                    QTST[:, 0:KC * N].rearrange("p (f t) -> p f t", f=KC), reads=["QTST"], writes=["qts"], q="pool")

        sample_state = {}

        def att_open(nq):
            ob = bank_get()
            return dict(ob=ob, okey=("ps", ob), first=True, nq=nq)

        def att_close(stt_, out_ap, okey_out):
            nq, ob, okb = stt_["nq"], stt_["ob"], stt_["okey"]
            ts("dve", RB[64:65, 0:nq], psf[64:65, ob, 0:nq], 1e-30, None, ALU.add, None, [okb], ["RBrow"])
            P.op("dve", lambda: nc.vector.reciprocal(out=RB[64:65, 0:nq], in_=RB[64:65, 0:nq]), reads=["RBrow"], writes=["RBrow"])
            bb = bank_get()
            mm(psf[0:64, bb, 0:nq], ones_f[64:65, 0:64], RB[64:65, 0:nq], True, True, ["RBrow", "ones_f"], [("ps", bb)])
            cp("dve", OS[0:64, 0:nq], psf[0:64, ob, 0:nq], [okb], ["OS"])
            tt("dve", out_ap, OS[0:64, 0:nq], psf[0:64, bb, 0:nq], ALU.mult, ["OS", ("ps", bb)], [okey_out])
            bank_put(ob)
            bank_put(bb)

        def run_items(items, lag=1, grp=2):
            groups = [items[i:i + grp] for i in range(0, len(items), grp)]
            for gi, g in enumerate(groups):
                for it in g:
                    it["qk"]()
                for it in g:
                    it["sm"]()
                if gi >= lag:
                    for it in groups[gi - lag]:
                        it["pv"]()
            for g in groups[max(0, len(groups) - lag):]:
                for it in g:
                    it["pv"]()

        def main_item(stt_, q_ap, qkey, kt_ap, v_ap, kvkeys, bias_ap, mask_ap, kn=128):
            nq = stt_["nq"]
            d = {}

            def qk():
                sb_ = bank_get()
                d["sb"] = sb_
                mm(psf[0:kn, sb_, 0:nq], kt_ap, q_ap, True, True, kvkeys + [qkey], [("ps", sb_)])

            def sm():
                pi = nxt("pt", NPT)
                d["pt"] = pi
                ptv = PT[pi]
                act(ptv[0:kn, 0:nq], psf[0:kn, d["sb"], 0:nq], AF.Exp, [("ps", d["sb"]), "BIAS"], [f"PT{pi}"], bias=bias_ap, scale=0.125)
                bank_put(d["sb"])
                if mask_ap is not None:
                    tt(("pool", "dve")[nxt("mk", 2)], ptv[0:kn, 0:nq], ptv[0:kn, 0:nq], mask_ap, ALU.mult, [f"PT{pi}", "MK"], [f"PT{pi}"])

            def pv():
                pi = d["pt"]
                mm(psf[0:65, stt_["ob"], 0:nq], v_ap, PT[pi][0:kn, 0:nq], stt_["first"], False, kvkeys + [f"PT{pi}"], [stt_["okey"]], skip=True)
                stt_["first"] = False
            return dict(qk=qk, sm=sm, pv=pv)

        def group_item(stt_, q_ap, qkey, tiles, kvkeys, bias_grp, bkeys):
            nq = stt_["nq"]
            n = len(tiles)
            w = n * nq
            kn0 = tiles[0][2]
            assert all(t[2] == kn0 for t in tiles)
            d = {}

            def qk():
                sb_ = bank_get()
                d["sb"] = sb_
                for i, (kt_ap, v_ap, kn, mask_ap) in enumerate(tiles):
                    mm(psf[0:kn, sb_, i * nq:(i + 1) * nq], kt_ap, q_ap, True, True, kvkeys + [qkey], [("ps", sb_)])

            def sm():
                pi = nxt("pt", NPT)
                d["pt"] = pi
                ptv = PT[pi]
                stt(GS[0:kn0, 0:w].rearrange("p (g q) -> p g q", q=nq), psf[0:kn0, d["sb"], 0:w].rearrange("p (g q) -> p g q", q=nq), 0.125,
                    bias_grp[0:kn0].unsqueeze(2).to_broadcast([kn0, n, nq]), ALU.mult, ALU.add, [("ps", d["sb"])] + bkeys, ["GS"])
                bank_put(d["sb"])
                act(ptv[0:kn0, 0:w], GS[0:kn0, 0:w], AF.Exp, ["GS"], [f"PT{pi}"])
                for i, (kt_ap, v_ap, kn, mask_ap) in enumerate(tiles):
                    if mask_ap is not None:
                        tt("pool", ptv[0:kn, i * nq:(i + 1) * nq], ptv[0:kn, i * nq:(i + 1) * nq], mask_ap, ALU.mult,
                           [f"PT{pi}", "MK"], [f"PT{pi}"])

            def pv():
                pi = d["pt"]
                for i, (kt_ap, v_ap, kn, mask_ap) in enumerate(tiles):
                    mm(psf[0:65, stt_["ob"], 0:nq], v_ap, PT[pi][0:kn, i * nq:(i + 1) * nq], stt_["first"], False,
                       kvkeys + [f"PT{pi}"], [stt_["okey"]], skip=True)
                    stt_["first"] = False
            return dict(qk=qk, sm=sm, pv=pv)

        def layer1_rest(N, nb, T, ctf, ctfkey, y_dst, halo):
            nt = max(1, N // 128)
            npart = min(N, 128)
            Xt = lambda t_: X[0:npart, t_, :]
            XNt = lambda t_: XN[0:npart, t_, :]
            if halo:
                down_proj(lambda k, t_: HAH[:, k, 0:2], "HAH", 0, KC, wb_o, 1, 2, lambda t_: XH[0:2, :], "XH", ("wc", "o"))
                rms_stats(lambda t_: XH[0:2, :], "XH", 1, 2)
                rms_apply(lambda t_: XH[0:2, :], "XH", 1, 2, 4, lambda t_: XNH[0:2, :], "XNH")
                transpose_fm(lambda t_: XNH[0:2, :], "XNH", 1, 2, XTH[:].rearrange("p k t -> p (k t)"), "XTH")
            down_proj(lambda k, t_: HA[:, k, t_ * 128:t_ * 128 + npart], "HA", 0, KC, wb_o, nt, npart, Xt, "X", ("wc", "o"),
                      after_tile=norm_cb(Xt, "X", npart, 4, XNt, "XN"))
            transpose_fm(XNt, "XN", nt, npart, XT, "XT")
            ffn_full(1, "XT", N, nb, T, ctf, ctfkey, nt, npart, Xt, after_tile=norm_cb(Xt, "X", npart, 5, Xt, "X"), halo=halo)
            dma(y_dst, X[0:npart, 0:nt, :], reads=xall(nt), writes=["yout"])

        with slow_dma("conv state layout (tiny)"):
            for b in range(SB_):
                for t2 in range(2):
                    dma(CTAs[:, :, b, t2], sta[b, t2].rearrange("(j p) -> p j", p=128), reads=[], writes=["CTAs"])
                    for l in range(2):
                        dma(CTFs[:, l, :, b, t2], stf[l, b, t2].rearrange("(j p) -> p j", p=128), reads=[], writes=["CTFs"])
        memset("pool", VB[:, 0:4 * 1040], 1.0, ["VB"])
        phase_a(xs.rearrange("(t p) d -> p t d", p=128), NS, SB_, ST_, CTAs, "CTAs", CTFs, "CTFs", True, 0)
        with slow_dma("conv state layout (tiny)"):
            for b in range(SB_):
                for t2 in range(2):
                    dma(sca[b, t2].rearrange("(j p) -> p j", p=128), CTAs[:, :, b, t2], reads=["CTAs"], writes=["sca"])
                    dma(sfc[0, b, t2].rearrange("(j p) -> p j", p=128), CTFs[:, 0, :, b, t2], reads=["CTFs"], writes=["sfc"])
        kts_sv = kts_s.rearrange("(f p) b t -> p f b t", p=128)
        for b in range(SB_):
            dma(kts_sv[:, :, b, PAST:PAST + ST_], KTST[:, 0:KC * NS].rearrange("p (f b t) -> p f b t", f=KC, b=SB_)[:, :, b, :],
                reads=["KTST"], writes=["kts_s"])
        for b in range(SB_):
            for t8 in range(8):
                dma(CKB[:, :], ck[b, t8 * 128:(t8 + 1) * 128, :], reads=[], writes=["CKB"], q="pool")
                transpose_fm(lambda t_: CKB, "CKB", 1, 128, XT, "XT")
                dma(kts_sv[:, :, b, t8 * 128:(t8 + 1) * 128], XT[:, 0:KC * 128].rearrange("p (f t) -> p f t", f=KC),
                    reads=["XT"], writes=["kts_s"])
        Ls, lsk = sample_state["L"]
        for b in range(SB_):
            for t8 in range(8):
                li = nxt("lg", 2)
                dma(LGt[li][:, :], clf[b, t8 * 128:(t8 + 1) * 128, :], reads=[], writes=[("LGt", li)])
                cumsum_tile(LGt[li][:, :], ("LGt", li), 128, CALLs[:, b, t8, :], "CALLs", CARRYs[:, b, t8, :], CARRYs[:, b, t8 + 1, :],
                            "CARRYs", sel127, "sel127")
            li = nxt("lg", 2)
            dma(LGt[li][0:ST_, :], Ls[b * ST_:(b + 1) * ST_, :], reads=[lsk], writes=[("LGt", li)])
            cumsum_tile(LGt[li][0:ST_, :], ("LGt", li), ST_, CALLs[0:ST_, b, 8, :], "CALLs", CARRYs[0:ST_, b, 8, :], CARRYs[:, b, 9, :],
                        "CARRYs", sel31, "sel31")
        cp("dve", QS[:, :], QTST[:, 0:KC * NS], ["QTST"], ["QS"])
        cp("dve", VBS[:, :], VB[:, 0:1040], ["VB"], ["VBS"])
        fence(KEYS_A, KEYS_B)
        for i in range(2):
            memset("dve", VR[i][:, :], 1.0, [f"VR{i}"])
        for b in range(SB_):
            cp("dve", QW[:, 0:KC * ST_].rearrange("p (f t) -> p f t", f=KC),
               QS[:, :].rearrange("p (f b t) -> p f b t", f=KC, b=SB_)[:, :, b, :], ["QS"], ["QW"])
            tt("dve", BIAS[:, 0:H * 9].rearrange("p (h t) -> p h t", h=H), CARRYs[:, b, 9, :].unsqueeze(2).to_broadcast([128, H, 9]),
               CALLs[:, b, :, :].rearrange("p t h -> p h t"), ALU.subtract, ["CALLs", "CARRYs"], ["BIAS"])
            qv = QW[:, 0:KC * ST_].rearrange("p (f t) -> p f t", f=KC)
            for hp in range(8):
                ki = nxt("kr", 2)
                vi = nxt("vr", 2)
                dma(KR[ki][:, 0:PAST + ST_], kts_sv[:, hp, b, :], reads=["kts_s"], writes=[f"KR{ki}"])
                vrv = VR[vi][:, 0:9 * 130].rearrange("p (t h e) -> p t h e", t=9, e=65)
                with slow_dma("V head split"):
                    for h2 in range(2):
                        dma(vrv[:, 0:8, h2, 0:64], cv[b].rearrange("(t p) (h e) -> p t h e", p=128, e=64)[:, :, 2 * hp + h2, :],
                            reads=[], writes=[f"VR{vi}"], q="pool")
                    dma(vrv[0:ST_, 8, :, 0:64], VBS[b * ST_:(b + 1) * ST_, :].rearrange("p (h e) -> p h e", e=65)[:, 2 * hp:2 * hp + 2, 0:64],
                        reads=["VBS"], writes=[f"VR{vi}"])
                for hl in range(2):
                    h = 2 * hp + hl
                    st_ = att_open(ST_)
                    tiles = []
                    for t9 in range(9):
                        kn = 128 if t9 < 8 else ST_
                        tiles.append((KR[ki][64 * hl:64 * hl + 64, t9 * 128:t9 * 128 + kn],
                                      VR[vi][0:kn, t9 * 130 + hl * 65:t9 * 130 + hl * 65 + 65], kn,
                                      (MKS[0:ST_, 0:ST_] if t9 == 8 else None)))
                    it1 = group_item(st_, qv[64 * hl:64 * hl + 64, hp, :], "QW", tiles[0:8], [f"KR{ki}", f"VR{vi}"],
                                     BIAS[:, h * 9:h * 9 + 8], ["BIAS"])
                    it2 = group_item(st_, qv[64 * hl:64 * hl + 64, hp, :], "QW", tiles[8:9], [f"KR{ki}", f"VR{vi}"],
                                     BIAS[:, h * 9 + 8:h * 9 + 9], ["BIAS"])
                    run_items([it1, it2])
                    att_close(st_, HA[64 * hl:64 * hl + 64, hp, b * ST_:(b + 1) * ST_], "HA")
        layer1_rest(NS, SB_, ST_, CTFs, "CTFs", ys.rearrange("(t p) d -> p t d", p=128), halo=False)
        with slow_dma("conv state layout (tiny)"):
            for b in range(SB_):
                for t2 in range(2):
                    dma(sfc[1, b, t2].rearrange("(j p) -> p j", p=128), CTFs[:, 1, :, b, t2], reads=["CTFs"], writes=["sfc"])
        fence(KEYS_B, KEYS_A)
        memset("pool", VB[:, 0:4 * 1040], 1.0, ["VB"])

        for c in range(NCH):
            phase_a(xp[c * CH:(c + 1) * CH, :].rearrange("(t p) d -> p t d", p=128), CH, 1, CH, CTA, "CTA", CTF, "CTF", False, c)
        with slow_dma("conv state layout (tiny)"):
            for t2 in range(2):
                dma(pca[t2].rearrange("(j p) -> p j", p=128), CTA[:, :, 0, t2], reads=["CTA"], writes=["pca"])
                dma(pfc0[t2].rearrange("(j p) -> p j", p=128), CTF[:, 0, :, 0, t2], reads=["CTF"], writes=["pfc0"])
        fence(KEYS_A, KEYS_B)

        ktsv = kts.rearrange("(f p) t -> p f t", p=128)
        dma(x1own.rearrange("(s o r) d -> s o r d", o=1, r=CH),
            x1s[2:SEQ + 2, :].rearrange("(s p r) d -> s p r d", p=2, r=CH)[:, bass.ds(par, 1), :, :], reads=["x1s"], writes=["x1own"])
        dma(x1hal.rearrange("s (o t) d -> s o t d", o=1),
            x1s[0:SEQ, :].rearrange("(s p r) d -> s p r d", p=2, r=CH)[:, bass.ds(par, 1), 0:2, :], reads=["x1s", "x1pad"], writes=["x1hal"])
        dma(qown[:, :, 2:CH + 2].rearrange("f s (o t) -> f s o t", o=1),
            qts[:, 2:SEQ + 2].rearrange("f (s p r) -> f s p r", p=2, r=CH)[:, :, bass.ds(par, 1), :], reads=["qts"], writes=["qown"])
        with slow_dma("tiny halo columns"):
            dma(qown[:, :, 0:2].rearrange("f s (o t) -> f s o t", o=1),
                qts[:, 0:SEQ].rearrange("f (s p r) -> f s p r", p=2, r=CH)[:, :, bass.ds(par, 1), 0:2], reads=["qts", "qtpad"], writes=["qownh"])
        qownv = qown.rearrange("(f p) s t -> p f s t", p=128)
        for s in range(NSLOT):
            nkt = 8 * (s + 1)
            nkth = 8 * s + 4
            dma(QW[:, 0:KC * 514].rearrange("p (f t) -> p f t", f=KC), qownv[:, :, s, :], reads=["qown", "qownh"], writes=["QW"])
            dma(X[:, 0:4, :], x1own[s * CH:(s + 1) * CH, :].rearrange("(t p) d -> p t d", p=128), reads=["x1own"], writes=xall(4))
            dma(XH[0:2, :], x1hal[s], reads=["x1hal"], writes=["XH"])
            qv = QW[:, 0:KC * 514].rearrange("p (f t) -> p f t", f=KC)
            jref = 8 * s + 8
            tt("dve", BIAS[:, 0:H * nkt].rearrange("p (h t) -> p h t", h=H), CARRY[:, jref, :].unsqueeze(2).to_broadcast([128, H, nkt]),
               CALL[:, 0:nkt, :].rearrange("p t h -> p h t"), ALU.subtract, ["CALL", "CARRY"], ["BIAS"])
            bv = BIAS[:, 0:H * nkt].rearrange("p (h t) -> p h t", h=H)
            for hp in range(8):
                stm = [att_open(CH) for _ in range(2)]
                sth = [att_open(2) for _ in range(2)]
                for seg in range(0, nkt, 16):
                    segn = min(16, nkt - seg)
                    ki = nxt("kr", 2)
                    vi = nxt("vr", 2)
                    dma(KR[ki][:, 0:segn * 128], ktsv[:, hp, seg * 128:(seg + segn) * 128], reads=["kts"], writes=[f"KR{ki}"])
                    dma(VR[vi][:, 0:segn * 130], vsc[hp, :, seg:seg + segn, :].rearrange("p t e -> p (t e)"), reads=["vsc"], writes=[f"VR{vi}"])
                    kvk = [f"KR{ki}", f"VR{vi}"]
                    items = []
                    for lt in range(segn):
                        t_ = seg + lt
                        j = t_ - 8 * s
                        for hl in range(2):
                            h = 2 * hp + hl
                            items.append(main_item(stm[hl], qv[64 * hl:64 * hl + 64, hp, 2:514], "QW",
                                                   KR[ki][64 * hl:64 * hl + 64, lt * 128:(lt + 1) * 128],
                                                   VR[vi][:, lt * 130 + hl * 65:lt * 130 + hl * 65 + 65], kvk,
                                                   bv[:, h, t_:t_ + 1], (MK[:, j, :] if j >= 0 else None)))
                    nh = min(segn, nkth - seg)
                    if nh > 0:
                        for hl in range(2):
                            h = 2 * hp + hl
                            tiles = []
                            for lt in range(nh):
                                j = seg + lt - 8 * s
                                tiles.append((KR[ki][64 * hl:64 * hl + 64, lt * 128:(lt + 1) * 128],
                                              VR[vi][:, lt * 130 + hl * 65:lt * 130 + hl * 65 + 65], 128,
                                              (MKH[:, j + 1, :] if j >= -1 else None)))
                            items.append(group_item(sth[hl], qv[64 * hl:64 * hl + 64, hp, 0:2], "QW", tiles, kvk,
                                                    bv[:, h, seg:seg + nh], ["BIAS"]))
                    run_items(items)
                for hl in range(2):
                    att_close(stm[hl], HA[64 * hl:64 * hl + 64, hp, :], "HA")
                    att_close(sth[hl], HAH[64 * hl:64 * hl + 64, hp, :], "HAH")
            layer1_rest(CH, 1, CH, CTF, "CTF", yp[s].rearrange("(t p) d -> p t d", p=128), halo=True)
        with slow_dma("conv state layout (tiny)"):
            for t2 in range(2):
                dma(pfc1[t2].rearrange("(j p) -> p j", p=128), CTF[:, 1, :, 0, t2], reads=["CTF"], writes=["pfc1"])
        P.emit()
    return nc


_NC_CACHE = {}


def kernel(x_prompt, x_sample, state_conv_a, state_ffn_conv, cache_k, cache_v, cache_logf,
           a_norm, w_a_in, a_conv_w, w_a_out, kv_norm, w_kv, b_f, b_norm, w_q, w_o,
           ffn_norm, w_ffn_up, ffn_conv_w, w_ffn_down, final_norm):
    f = lambda a: np.ascontiguousarray(np.asarray(a, dtype=np.float32))
    x_prompt, x_sample = f(x_prompt), f(x_sample)
    B = x_prompt.shape[0]
    assert x_prompt.shape[1] == SEQ
    n = 2 * B
    if "nc" not in _NC_CACHE:
        _NC_CACHE["nc"] = build_program()
    nc = _NC_CACHE["nc"]
    shared = dict(a_norm=f(a_norm).reshape(D), w_a_in=f(w_a_in)[0], a_conv_w=f(a_conv_w)[0], w_a_out=f(w_a_out)[0],
                  kv_norm=f(kv_norm), w_kv=f(w_kv), b_f=f(b_f), b_norm=f(b_norm).reshape(D), w_q=f(w_q)[0], w_o=f(w_o)[0],
                  ffn_norm=f(ffn_norm), w_up=f(w_ffn_up), ffn_conv_w=f(ffn_conv_w), w_dn=f(w_ffn_down), final_norm=f(final_norm))
    sca_, sfc_ = f(state_conv_a), f(state_ffn_conv)
    ck_, cv_, clf_ = f(cache_k), f(cache_v), f(cache_logf)
    in_maps = []
    for c in range(n):
        sl = slice(SB_ * c, SB_ * (c + 1))
        m = dict(shared)
        m.update(xp=x_prompt[c // 2], xs=x_sample[sl].reshape(NS, D), sta=sca_[0, sl], stf=np.ascontiguousarray(sfc_[:, sl]),
                 ck=ck_[sl].reshape(SB_, PAST, D), cv=cv_[sl].reshape(SB_, PAST, D), clf=clf_[sl])
        in_maps.append(m)
    res = run_bass_kernel_spmd(nc, in_maps, core_ids=list(range(n)))
    R = res.results
    NSLOT = SEQ // CH // 2
    DB = SB_ * n
    y_prompt = np.zeros((B, SEQ, D), np.float32)
    y_sample = np.zeros((DB, ST_, D), np.float32)
    p_conv_a = np.zeros((1, B, 2, D), np.float32)
    p_ffn_conv = np.zeros((2, B, 2, NUP), np.float32)
    p_k = np.zeros((B, SEQ, H, 64), np.float32)
    p_v = np.zeros((B, SEQ, H, 64), np.float32)
    p_logf = np.zeros((B, SEQ, H), np.float32)
    s_conv_a = np.zeros((1, DB, 2, D), np.float32)
    s_ffn_conv = np.zeros((2, DB, 2, NUP), np.float32)
    s_k = np.zeros((DB, ST_, H, 64), np.float32)
    s_v = np.zeros((DB, ST_, H, 64), np.float32)
    s_logf = np.zeros((DB, ST_, H), np.float32)
    for c in range(n):
        b, half = c // 2, c % 2
        r = R[c]
        for s in range(NSLOT):
            qb = 2 * s + half
            y_prompt[b, qb * CH:(qb + 1) * CH] = r["yp"][s]
        sl = slice(SB_ * c, SB_ * (c + 1))
        y_sample[sl] = r["ys"].reshape(SB_, ST_, D)
        if half == 0:
            p_conv_a[0, b] = r["pca"]
            p_ffn_conv[0, b] = r["pfc0"]
            p_k[b] = r["pk"].reshape(SEQ, H, 64)
            p_v[b] = r["pv"].reshape(SEQ, H, 64)
            p_logf[b] = r["plf"]
        else:
            p_ffn_conv[1, b] = r["pfc1"]
        s_conv_a[0, sl] = r["sca"]
        s_ffn_conv[:, sl] = r["sfc"]
        s_k[sl] = r["sk"].reshape(SB_, ST_, H, 64)
        s_v[sl] = r["sv"].reshape(SB_, ST_, H, 64)
        s_logf[sl] = r["slf"].reshape(SB_, ST_, H)
    return (y_prompt, y_sample, p_conv_a, p_ffn_conv, p_k, p_v, p_logf, s_conv_a, s_ffn_conv, s_k, s_v, s_logf)
```
